# Optimizing a Trainium2 kernel written in Bass

```python
import math
import jax, jax.numpy as jnp
from jax import lax
import numpy as np

D_MODEL = 2048
BATCH = 8
SEQ = 4096
DEPTH = 4

NORM_EPS = 1e-6
NEG_INF = -1e30
MEM_TOKENS = 256

NSA_HEADS = 8
NSA_KV_GROUPS = 2
NSA_GROUP_HEADS = NSA_HEADS // NSA_KV_GROUPS
NSA_HEAD_DIM = 128
NSA_CMP_LEN = 32
NSA_CMP_STRIDE = 16
NSA_CMP_HIDDEN = 256
NSA_SEL_BLOCK = 64
NSA_N_SELECT = 16
NSA_WINDOW = 512
NSA_Q_BLOCK = 64
NSA_FORCE_SCORE = 1e6
NSA_WIDTH = NSA_HEADS * NSA_HEAD_DIM
NSA_KV_WIDTH = NSA_KV_GROUPS * NSA_HEAD_DIM

RET_HEADS = 4
RET_DK = 256
RET_DV = 256
RET_CHUNK = 128
RET_ROPE_BASE = 10000.0
RET_GN_EPS = 1e-5

GDN_QK_HEADS = 16
GDN_V_HEADS = 32
GDN_HEAD_DIM = 128
GDN_CONV = 4
GDN_CHUNK = 64
GDN_QK_WIDTH = GDN_QK_HEADS * GDN_HEAD_DIM
GDN_V_WIDTH = GDN_V_HEADS * GDN_HEAD_DIM
GDN_CONV_CH = 2 * GDN_QK_WIDTH + GDN_V_WIDTH
GDN_IN = GDN_CONV_CH + GDN_V_WIDTH + 2 * GDN_V_HEADS

REL_BUCKETS = 32
REL_MAX_DIST = 1024

XA_HEADS = 4
XA_HEAD_DIM = 128
XA_WIDTH = XA_HEADS * XA_HEAD_DIM

FFN_HIDDEN = -(-8 * D_MODEL // (3 * 256)) * 256

HYB_IN = NSA_WIDTH + 6 * NSA_KV_WIDTH + 3 * NSA_HEADS + 2 * RET_HEADS * RET_DK + 2 * RET_HEADS * RET_DV
HYB_OUT = NSA_WIDTH + RET_HEADS * RET_DV
N_EVEN = (DEPTH + 1) // 2
N_ODD = DEPTH // 2

kernel_name = 'hybrid_nsa_retention_gdn_trunk'


def rms_norm(x, gain, eps=NORM_EPS):
    x32 = x.astype(jnp.float32)
    y = x32 * lax.rsqrt(jnp.mean(x32 * x32, axis=-1, keepdims=True) + eps)
    return (y * gain.astype(jnp.float32)).astype(x.dtype)


def l2_norm(x, eps=NORM_EPS):
    x32 = x.astype(jnp.float32)
    return x32 * lax.rsqrt(jnp.sum(x32 * x32, axis=-1, keepdims=True) + eps)


def masked_softmax(logits, mask):
    p = jax.nn.softmax(jnp.where(mask, logits.astype(jnp.float32), NEG_INF), axis=-1)
    return jnp.where(mask, p, 0.0)


def rel_bucket(dist):
    dist = jnp.maximum(dist, 0)
    max_exact = REL_BUCKETS // 2
    scaled = (jnp.log(jnp.maximum(dist, max_exact).astype(jnp.float32) / max_exact)
              / math.log(REL_MAX_DIST / max_exact) * (REL_BUCKETS - max_exact))
    large = jnp.minimum(max_exact + scaled.astype(jnp.int32), REL_BUCKETS - 1)
    return jnp.where(dist < max_exact, dist, large)


def rotary(x, pos):
    half = x.shape[-1] // 2
    inv = RET_ROPE_BASE ** (-jnp.arange(half, dtype=jnp.float32) / half)
    ang = pos.astype(jnp.float32)[:, None] * inv[None, :]
    cos, sin = jnp.cos(ang)[None, :, None, :], jnp.sin(ang)[None, :, None, :]
    x1, x2 = x[..., :half], x[..., half:]
    return jnp.concatenate([x1 * cos - x2 * sin, x1 * sin + x2 * cos], axis=-1)


def causal_depthwise_conv(x, w):
    k, ch = w.shape
    return lax.conv_general_dilated(x, w[:, None, :].astype(x.dtype), window_strides=(1,),
                                    padding=[(k - 1, 0)], dimension_numbers=('NWC', 'WIO', 'NWC'),
                                    feature_group_count=ch)


def compress_blocks(t, pos, w1, w2):
    b, s, g, dh = t.shape
    r = NSA_CMP_LEN // NSA_CMP_STRIDE
    n_chunks = s // NSA_CMP_STRIDE
    n_cmp = n_chunks - r + 1
    c = t.reshape(b, n_chunks, NSA_CMP_STRIDE, g, dh)
    blocks = jnp.concatenate([c[:, j:j + n_cmp] for j in range(r)], axis=2)
    blocks = blocks + pos[None, None, :, None, :].astype(t.dtype)
    flat = blocks.transpose(0, 3, 1, 2, 4).reshape(b, g, n_cmp, NSA_CMP_LEN * dh)
    return jax.nn.silu(flat @ w1) @ w2


def nsa_attention(q, k_cmp, v_cmp, k_sel, v_sel, k_win, v_win, gate_logits,
                  q_gain, k_gain, cmp_pos, cmp_w1, cmp_w2, rel_bias):
    f32 = jnp.float32
    B, S = q.shape[0], q.shape[1]
    G, HG, dh = NSA_KV_GROUPS, NSA_GROUP_HEADS, NSA_HEAD_DIM
    QB, W, SB = NSA_Q_BLOCK, NSA_WINDOW, NSA_SEL_BLOCK
    q = rms_norm(q, q_gain) * (dh ** -0.5)
    q = q.reshape(B, S, G, HG, dh).transpose(0, 2, 3, 1, 4)

    kc = rms_norm(compress_blocks(k_cmp, cmp_pos[0], cmp_w1[0], cmp_w2[0]), k_gain[0])
    vc = compress_blocks(v_cmp, cmp_pos[1], cmp_w1[1], cmp_w2[1])
    n_cmp = kc.shape[2]
    n_sb = S // SB
    ks_blocks = rms_norm(k_sel, k_gain[1]).transpose(0, 2, 1, 3).reshape(B, G, n_sb, SB, dh)
    vs_blocks = v_sel.transpose(0, 2, 1, 3).reshape(B, G, n_sb, SB, dh)
    pad = ((0, 0), (0, 0), (W, 0), (0, 0))
    kw_pad = jnp.pad(rms_norm(k_win, k_gain[2]).transpose(0, 2, 1, 3), pad)
    vw_pad = jnp.pad(v_win.transpose(0, 2, 1, 3), pad)
    gates = jax.nn.sigmoid(gate_logits.astype(f32)).reshape(B, S, G, HG, 3).transpose(0, 2, 3, 1, 4)

    cmp_start = jnp.arange(n_cmp) * NSA_CMP_STRIDE
    cmp_end = cmp_start + NSA_CMP_LEN - 1
    sel_start = jnp.arange(n_sb) * SB
    cover = ((cmp_start[:, None] < sel_start[None, :] + SB) & (cmp_end[:, None] >= sel_start[None, :])).astype(f32)
    n_top = min(NSA_N_SELECT, n_sb)
    b_idx = jnp.arange(B)[:, None, None, None]
    g_idx = jnp.arange(G)[None, :, None, None]
    bias_by_group = rel_bias.reshape(REL_BUCKETS, G, HG)
    blk = jnp.arange(n_sb)
    n_qb = S // QB

    def head_bias(dist):
        return jnp.moveaxis(rel_bias[rel_bucket(dist)], -1, 0).reshape((G, HG) + dist.shape).astype(f32)

    def query_block(args):
        qb, gb, iq = args
        t = iq * QB + jnp.arange(QB)
        dist_c = t[:, None] - cmp_end[None, :]
        logit_c = jnp.einsum('bghqd,bgnd->bghqn', qb, kc).astype(f32) + head_bias(dist_c)
        p_c = masked_softmax(logit_c, dist_c >= 0)
        o_c = jnp.einsum('bghqn,bgnd->bghqd', p_c.astype(vc.dtype), vc)
        imp = jnp.einsum('bghqn,nj->bgqj', p_c, cover)
        cur = (t // SB)[:, None]
        forced = (blk[None, :] == 0) | (blk[None, :] == cur) | (blk[None, :] == cur - 1)
        imp = jnp.where(forced, NSA_FORCE_SCORE, imp)
        imp = jnp.where(blk[None, :] > cur, -NSA_FORCE_SCORE, imp)
        _, sel = lax.top_k(imp, n_top)
        k_g = ks_blocks[b_idx, g_idx, sel].reshape(B, G, QB, n_top * SB, dh)
        v_g = vs_blocks[b_idx, g_idx, sel].reshape(B, G, QB, n_top * SB, dh)
        kpos = (sel[..., None] * SB + jnp.arange(SB)).reshape(B, G, QB, n_top * SB)
        dist_s = t[None, None, :, None] - kpos
        bias_s = bias_by_group[rel_bucket(dist_s), g_idx].transpose(0, 1, 4, 2, 3).astype(f32)
        logit_s = jnp.einsum('bghqd,bgqkd->bghqk', qb, k_g).astype(f32) + bias_s
        p_s = masked_softmax(logit_s, (dist_s >= 0)[:, :, None])
        o_s = jnp.einsum('bghqk,bgqkd->bghqd', p_s.astype(v_g.dtype), v_g)
        start = iq * QB
        k_w = lax.dynamic_slice_in_dim(kw_pad, start, QB + W, axis=2)
        v_w = lax.dynamic_slice_in_dim(vw_pad, start, QB + W, axis=2)
        wpos = start - W + jnp.arange(QB + W)
        dist_w = t[:, None] - wpos[None, :]
        mask_w = (dist_w >= 0) & (dist_w < W) & (wpos[None, :] >= 0)
        logit_w = jnp.einsum('bghqd,bgkd->bghqk', qb, k_w).astype(f32) + head_bias(dist_w)
        p_w = masked_softmax(logit_w, mask_w)
        o_w = jnp.einsum('bghqk,bgkd->bghqd', p_w.astype(v_w.dtype), v_w)
        out = gb[..., 0:1] * o_c + gb[..., 1:2] * o_s + gb[..., 2:3] * o_w
        return out.astype(qb.dtype)

    q_blocks = q.reshape(B, G, HG, n_qb, QB, dh).transpose(3, 0, 1, 2, 4, 5)
    g_blocks = gates.reshape(B, G, HG, n_qb, QB, 3).transpose(3, 0, 1, 2, 4, 5)
    outs = lax.map(query_block, (q_blocks, g_blocks, jnp.arange(n_qb)))
    return outs.transpose(1, 0, 4, 2, 3, 5).reshape(B, S, NSA_WIDTH)


def retention(q, k, v, gate, gn_gain):
    f32 = jnp.float32
    B, S, H, dk = q.shape
    dv = v.shape[-1]
    C = RET_CHUNK
    N = S // C
    pos = jnp.arange(S)
    q = rotary(q.astype(f32), pos) * (dk ** -0.5)
    k = rotary(k.astype(f32), pos)
    v = v.astype(f32)
    log_gamma = jnp.log(1.0 - 2.0 ** (-5.0 - jnp.arange(H, dtype=f32)))
    idx = jnp.arange(C, dtype=f32)
    rel = idx[:, None] - idx[None, :]
    decay_in = jnp.where(rel >= 0, jnp.exp(log_gamma[:, None, None] * jnp.maximum(rel, 0.0)), 0.0)
    q_decay = jnp.exp(log_gamma[:, None] * (idx + 1.0))
    k_decay = jnp.exp(log_gamma[:, None] * (C - 1.0 - idx))
    chunk_decay = jnp.exp(log_gamma * C)

    def to_chunks(t):
        return t.reshape(B, N, C, H, t.shape[-1]).transpose(1, 0, 3, 2, 4)

    def step(state, inp):
        qn, kn, vn = inp
        inner = jnp.einsum('bhcd,bhmd->bhcm', qn, kn) * decay_in
        o = (jnp.einsum('bhcm,bhme->bhce', inner, vn)
             + jnp.einsum('bhcd,bhde->bhce', qn * q_decay[..., None], state))
        state = state * chunk_decay[:, None, None] + jnp.einsum('bhmd,bhme->bhde', kn * k_decay[..., None], vn)
        return state, o

    _, o = lax.scan(step, jnp.zeros((B, H, dk, dv), f32), (to_chunks(q), to_chunks(k), to_chunks(v)))
    o = o.transpose(1, 0, 3, 2, 4).reshape(B, S, H, dv)
    mu = jnp.mean(o, axis=-1, keepdims=True)
    var = jnp.mean(jnp.square(o - mu), axis=-1, keepdims=True)
    o = ((o - mu) * lax.rsqrt(var + RET_GN_EPS)).reshape(B, S, H * dv) * gn_gain.astype(f32)
    return (jax.nn.silu(gate.astype(f32)) * o).astype(gate.dtype)


def nsa_retention_mixer(h, w_in, w_out, q_gain, k_gain, cmp_pos, cmp_w1, cmp_w2, gn_gain, rel_bias):
    B, S, _ = h.shape
    widths = ([NSA_WIDTH] + [NSA_KV_WIDTH] * 6 + [3 * NSA_HEADS]
              + [RET_HEADS * RET_DK] * 2 + [RET_HEADS * RET_DV] * 2)
    cuts = [int(c) for c in np.cumsum(widths)[:-1]]
    (nq, kc, vc, ks, vs, kw, vw, gl, rq, rk, rv, rg) = jnp.split(h @ w_in, cuts, axis=-1)
    kv = lambda t: t.reshape(B, S, NSA_KV_GROUPS, NSA_HEAD_DIM)
    a_out = nsa_attention(nq.reshape(B, S, NSA_HEADS, NSA_HEAD_DIM), kv(kc), kv(vc), kv(ks), kv(vs),
                          kv(kw), kv(vw), gl.reshape(B, S, NSA_HEADS, 3),
                          q_gain, k_gain, cmp_pos, cmp_w1, cmp_w2, rel_bias)
    b_out = retention(rq.reshape(B, S, RET_HEADS, RET_DK), rk.reshape(B, S, RET_HEADS, RET_DK),
                      rv.reshape(B, S, RET_HEADS, RET_DV), rg, gn_gain)
    return jnp.concatenate([a_out, b_out], axis=-1) @ w_out


def chunk_gated_delta_rule(q, k, v, beta, g):
    f32 = jnp.float32
    B, S, H, dk = q.shape
    dv = v.shape[-1]
    C = GDN_CHUNK
    N = S // C

    def chunks(t):
        return jnp.moveaxis(t.astype(f32).reshape((B, N, C, H) + t.shape[3:]), 3, 1)

    q = chunks(q) * (dk ** -0.5)
    k, v, beta, g = chunks(k), chunks(v), chunks(beta), chunks(g)
    gc = jnp.cumsum(g, axis=-1)
    i = jnp.arange(C)
    causal = i[:, None] >= i[None, :]
    strict = i[:, None] > i[None, :]
    decay = jnp.exp(jnp.where(causal, gc[..., :, None] - gc[..., None, :], -jnp.inf))
    kb = k * beta[..., None]
    lower = jnp.where(strict, jnp.einsum('bhncd,bhnmd->bhncm', kb, k) * decay, 0.0)
    rhs = jnp.concatenate([v * beta[..., None], kb * jnp.exp(gc)[..., None]], axis=-1)
    sol = lax.linalg.triangular_solve(lower + jnp.eye(C, dtype=f32), rhs, left_side=True,
                                      lower=True, unit_diagonal=True)
    u, w = sol[..., :dv], sol[..., dv:]
    attn = jnp.where(causal, jnp.einsum('bhncd,bhnmd->bhncm', q, k) * decay, 0.0)
    q_dec = q * jnp.exp(gc)[..., None]
    k_dec = k * jnp.exp(gc[..., -1:] - gc)[..., None]
    last = jnp.exp(gc[..., -1])
    xs = tuple(jnp.moveaxis(t, 2, 0) for t in (u, w, attn, q_dec, k_dec, last))

    def step(state, inp):
        u_n, w_n, attn_n, qd_n, kd_n, last_n = inp
        v_new = u_n - w_n @ state
        o = qd_n @ state + attn_n @ v_new
        state = state * last_n[..., None, None] + jnp.einsum('bhcd,bhce->bhde', kd_n, v_new)
        return state, o

    _, o = lax.scan(step, jnp.zeros((B, H, dk, dv), f32), xs)
    return o.transpose(1, 0, 3, 2, 4).reshape(B, S, H, dv)


def gated_deltanet_mixer(h, w_in, conv_w, a_log, dt_bias, norm_gain, w_out):
    B, S, _ = h.shape
    f32 = jnp.float32
    cuts = [GDN_CONV_CH, GDN_CONV_CH + GDN_V_WIDTH, GDN_CONV_CH + GDN_V_WIDTH + GDN_V_HEADS]
    qkv, z, b, a = jnp.split(h @ w_in, cuts, axis=-1)
    qkv = jax.nn.silu(causal_depthwise_conv(qkv, conv_w))
    q, k, v = jnp.split(qkv, [GDN_QK_WIDTH, 2 * GDN_QK_WIDTH], axis=-1)
    rep = GDN_V_HEADS // GDN_QK_HEADS
    q = jnp.repeat(l2_norm(q.reshape(B, S, GDN_QK_HEADS, GDN_HEAD_DIM)), rep, axis=2)
    k = jnp.repeat(l2_norm(k.reshape(B, S, GDN_QK_HEADS, GDN_HEAD_DIM)), rep, axis=2)
    v = v.reshape(B, S, GDN_V_HEADS, GDN_HEAD_DIM)
    beta = jax.nn.sigmoid(b.astype(f32))
    g = -jnp.exp(a_log.astype(f32)) * jax.nn.softplus(a.astype(f32) + dt_bias.astype(f32))
    o = chunk_gated_delta_rule(q, k, v, beta, g)
    o = rms_norm(o, norm_gain) * jax.nn.silu(z.astype(f32).reshape(B, S, GDN_V_HEADS, GDN_HEAD_DIM))
    return o.reshape(B, S, GDN_V_WIDTH).astype(h.dtype) @ w_out


def memory_cross_attention(h, mem, mem_gain, wq, wkv, q_gain, k_gain, wo):
    B, S, _ = h.shape
    M = mem.shape[1]
    q = rms_norm((h @ wq).reshape(B, S, XA_HEADS, XA_HEAD_DIM), q_gain)
    k, v = jnp.split(rms_norm(mem, mem_gain) @ wkv, 2, axis=-1)
    k = rms_norm(k.reshape(B, M, XA_HEADS, XA_HEAD_DIM), k_gain)
    v = v.reshape(B, M, XA_HEADS, XA_HEAD_DIM)
    logits = jnp.einsum('bshd,bmhd->bhsm', q, k).astype(jnp.float32) * (XA_HEAD_DIM ** -0.5)
    p = jax.nn.softmax(logits, axis=-1).astype(v.dtype)
    return jnp.einsum('bhsm,bmhd->bshd', p, v).reshape(B, S, XA_WIDTH) @ wo


def swiglu_ffn(h, w_in, w_out):
    gate, up = jnp.split(h @ w_in, 2, axis=-1)
    return (jax.nn.silu(gate) * up) @ w_out


def setup_inputs(seed: int = 0) -> dict:
    key = jax.random.key(seed)
    keys = iter(jax.random.split(key, 32))
    f32 = jnp.float32
    E, O, L, dh = N_EVEN, N_ODD, DEPTH, NSA_HEAD_DIM

    def dense(shape, fan_in):
        return jax.random.normal(next(keys), shape, f32) * (fan_in ** -0.5)

    def gain(shape):
        return 1.0 + 0.02 * jax.random.normal(next(keys), shape, f32)

    x = jax.random.normal(next(keys), (BATCH, SEQ, D_MODEL), f32)
    mem = jax.random.normal(next(keys), (BATCH, MEM_TOKENS, D_MODEL), f32)
    rel_bias = 0.1 * jax.random.normal(next(keys), (REL_BUCKETS, NSA_HEADS), f32)
    ln_mix = gain((L, D_MODEL))
    ln_mem = gain((L, D_MODEL))
    ln_ffn = gain((L, D_MODEL))
    hyb_w_in = dense((E, D_MODEL, HYB_IN), D_MODEL)
    hyb_w_out = dense((E, HYB_OUT, D_MODEL), HYB_OUT)
    nsa_q_gain = gain((E, dh))
    nsa_k_gain = gain((E, 3, dh))
    nsa_cmp_pos = 0.1 * jax.random.normal(next(keys), (E, 2, NSA_CMP_LEN, dh), f32)
    nsa_cmp_w1 = dense((E, 2, NSA_CMP_LEN * dh, NSA_CMP_HIDDEN), NSA_CMP_LEN * dh)
    nsa_cmp_w2 = dense((E, 2, NSA_CMP_HIDDEN, dh), NSA_CMP_HIDDEN)
    ret_gn_gain = gain((E, RET_HEADS * RET_DV))
    gdn_w_in = dense((O, D_MODEL, GDN_IN), D_MODEL)
    gdn_conv_w = dense((O, GDN_CONV, GDN_CONV_CH), GDN_CONV)
    gdn_a_log = jnp.log(jax.random.uniform(next(keys), (O, GDN_V_HEADS), f32, 1.0, 16.0))
    dt = jnp.exp(jax.random.uniform(next(keys), (O, GDN_V_HEADS), f32, math.log(1e-3), math.log(1e-1)))
    gdn_dt_bias = dt + jnp.log(-jnp.expm1(-dt))
    gdn_norm_gain = gain((O, GDN_HEAD_DIM))
    gdn_w_out = dense((O, GDN_V_WIDTH, D_MODEL), GDN_V_WIDTH)
    xa_wq = dense((L, D_MODEL, XA_WIDTH), D_MODEL)
    xa_wkv = dense((L, D_MODEL, 2 * XA_WIDTH), D_MODEL)
    xa_q_gain = gain((L, XA_HEAD_DIM))
    xa_k_gain = gain((L, XA_HEAD_DIM))
    xa_mem_gain = gain((L, D_MODEL))
    xa_wo = dense((L, XA_WIDTH, D_MODEL), XA_WIDTH)
    ffn_w_in = dense((L, D_MODEL, 2 * FFN_HIDDEN), D_MODEL)
    ffn_w_out = dense((L, FFN_HIDDEN, D_MODEL), FFN_HIDDEN)
    return {'x': x, 'mem': mem, 'rel_bias': rel_bias, 'ln_mix': ln_mix, 'ln_mem': ln_mem, 'ln_ffn': ln_ffn,
            'hyb_w_in': hyb_w_in, 'hyb_w_out': hyb_w_out, 'nsa_q_gain': nsa_q_gain, 'nsa_k_gain': nsa_k_gain,
            'nsa_cmp_pos': nsa_cmp_pos, 'nsa_cmp_w1': nsa_cmp_w1, 'nsa_cmp_w2': nsa_cmp_w2,
            'ret_gn_gain': ret_gn_gain, 'gdn_w_in': gdn_w_in, 'gdn_conv_w': gdn_conv_w, 'gdn_a_log': gdn_a_log,
            'gdn_dt_bias': gdn_dt_bias, 'gdn_norm_gain': gdn_norm_gain, 'gdn_w_out': gdn_w_out,
            'xa_wq': xa_wq, 'xa_wkv': xa_wkv, 'xa_q_gain': xa_q_gain, 'xa_k_gain': xa_k_gain,
            'xa_mem_gain': xa_mem_gain, 'xa_wo': xa_wo, 'ffn_w_in': ffn_w_in, 'ffn_w_out': ffn_w_out}


def reference(x, mem, rel_bias, ln_mix, ln_mem, ln_ffn, hyb_w_in, hyb_w_out, nsa_q_gain, nsa_k_gain,
              nsa_cmp_pos, nsa_cmp_w1, nsa_cmp_w2, ret_gn_gain, gdn_w_in, gdn_conv_w, gdn_a_log,
              gdn_dt_bias, gdn_norm_gain, gdn_w_out, xa_wq, xa_wkv, xa_q_gain, xa_k_gain, xa_mem_gain,
              xa_wo, ffn_w_in, ffn_w_out):
    h = x
    for layer in range(DEPTH):
        hn = rms_norm(h, ln_mix[layer])
        if layer % 2 == 0:
            e = layer // 2
            mix = nsa_retention_mixer(hn, hyb_w_in[e], hyb_w_out[e], nsa_q_gain[e], nsa_k_gain[e],
                                      nsa_cmp_pos[e], nsa_cmp_w1[e], nsa_cmp_w2[e], ret_gn_gain[e], rel_bias)
        else:
            o = layer // 2
            mix = gated_deltanet_mixer(hn, gdn_w_in[o], gdn_conv_w[o], gdn_a_log[o], gdn_dt_bias[o],
                                       gdn_norm_gain[o], gdn_w_out[o])
        h = h + mix
        h = h + memory_cross_attention(rms_norm(h, ln_mem[layer]), mem, xa_mem_gain[layer], xa_wq[layer],
                                       xa_wkv[layer], xa_q_gain[layer], xa_k_gain[layer], xa_wo[layer])
        h = h + swiglu_ffn(rms_norm(h, ln_ffn[layer]), ffn_w_in[layer], ffn_w_out[layer])
    return h
```

```python
import math
from contextlib import ExitStack

import numpy as np
import ml_dtypes
import concourse.bass as bass
import concourse.mybir as mybir
from concourse.bass_utils import run_bass_kernel_spmd

F32 = mybir.dt.float32
BF16 = mybir.dt.bfloat16
AF = mybir.ActivationFunctionType
ALU = mybir.AluOpType
AX = mybir.AxisListType


class Ev:
    __slots__ = ("eng", "sem", "val")

    def __init__(self, eng, sem=None, val=None):
        self.eng, self.sem, self.val = eng, sem, val


class Buf:
    __slots__ = ("w", "rd", "name")

    def __init__(self, name=""):
        self.w = None
        self.rd = {}
        self.name = name


class Eng:
    def __init__(self, name, h, sem, is_pe=False):
        self.name, self.h, self.sem, self.is_pe = name, h, sem, is_pe
        self.n = 0
        self.pending = []
        self.last = None
        self.waited = {}
        self.ring = []
        self.ring_cnt = []
        self.dma_i = 0


class MK:
    RING = 12

    def __init__(self, nc, es):
        self.nc = nc
        self.es = es
        mk = lambda n: es.enter_context(nc.semaphore(n))
        self.pe = Eng("pe", nc.tensor, mk("s_pe"), is_pe=True)
        self.act = Eng("act", nc.scalar, mk("s_act"))
        self.dve = Eng("dve", nc.vector, mk("s_dve"))
        self.pool = Eng("pool", nc.gpsimd, mk("s_pool"))
        self.sp = Eng("sp", nc.sync, mk("s_sp"))
        self.engs = [self.pe, self.act, self.dve, self.pool, self.sp]
        for e in (self.sp, self.pool, self.act):
            r = self.RING if e is self.sp else 6
            e.ring = [mk(f"r_{e.name}{i}") for i in range(r)]
            e.ring_cnt = [0] * r
        self.ninst = 0
        self.castq = Eng("castq", nc.gpsimd, None)
        self.castq.ring = [mk(f"r_cast{i}") for i in range(8)]
        self.castq.ring_cnt = [0] * 8
        self.on_barrier = None

    def _resolve(self, ev):
        if ev.val is None:
            p = ev.eng
            p.n += 1
            p.last.then_inc(p.sem, 1)
            for e in p.pending:
                e.val = p.n
                e.sem = p.sem
            p.pending = []

    def _wait(self, eng, ev):
        if ev.eng is eng and eng.is_pe:
            return
        self._resolve(ev)
        k = id(ev.sem)
        if eng.waited.get(k, 0) < ev.val:
            eng.h.wait_ge(ev.sem, ev.val)
            eng.waited[k] = ev.val
            self.ninst += 1

    def _deps(self, r, w):
        deps = []
        for b in r:
            if b.w is not None:
                deps.append(b.w)
        for b in w:
            if b.w is not None:
                deps.append(b.w)
            deps.extend(b.rd.values())
        return deps

    def _record(self, ev, key, r, w):
        for b in r:
            b.rd[key] = ev
        for b in w:
            b.w = ev
            b.rd = {}

    def op(self, eng, fn, r=(), w=()):
        for ev in self._deps(r, w):
            self._wait(eng, ev)
        inst = fn()
        self.ninst += 1
        ev = Ev(eng)
        eng.pending.append(ev)
        eng.last = inst
        self._record(ev, id(eng), r, w)
        return inst

    def dma(self, q, out, in_, r=(), w=(), **kw):
        for ev in self._deps(r, w):
            self._wait(q, ev)
        i = q.dma_i
        q.dma_i += 1
        slot = i % len(q.ring)
        s = q.ring[slot]
        base = q.ring_cnt[slot]
        if base > 0 and q.waited.get(id(s), 0) < base:
            q.h.wait_ge(s, base)
            q.waited[id(s)] = base
        q.h.dma_start(out=out, in_=in_, **kw).then_inc(s, 16)
        self.ninst += 1
        q.ring_cnt[slot] = base + 16
        ev = Ev(None, s, base + 16)
        self._record(ev, (id(s), base + 16), r, w)
        return ev

    def barrier(self):
        evs = []
        for e in self.engs:
            if e.pending:
                self._resolve(e.pending[-1])
            if e.n > 0:
                evs.append(Ev(e, e.sem, e.n))
            for s, c in zip(e.ring, e.ring_cnt):
                if c > 0:
                    evs.append(Ev(None, s, c))
        for e in self.engs:
            for ev in evs:
                if ev.eng is e:
                    continue
                k = id(ev.sem)
                if e.waited.get(k, 0) < ev.val:
                    e.h.wait_ge(ev.sem, ev.val)
                    e.waited[k] = ev.val
        if self.on_barrier is not None:
            self.on_barrier()


D = 2048
EPS = 1e-6
HYB_IN = 6680
GDN_IN = 12352
FFN_H = 5632
BIG = 30000.0


class Prog:
    def __init__(self, S=4096, M=256, dbg_out=()):
        self.S, self.M = S, M
        self.nc = nc = bass.Bass("TRN2", target_bir_lowering=False)
        self.es = ExitStack()
        self.mk = MK(nc, self.es)
        self.dram = {}
        self.dbg_out = set(dbg_out)
        self.in_names = []
        self.pb = [self.es.enter_context(nc.psum_tensor(f"pb{i}", [128, 512], F32)) for i in range(8)]
        self.pbb = [Buf(f"pb{i}") for i in range(8)]
        self.rot = list(range(8))
        self.rot_i = 0
        self.uid = 0
        self.ident_bf = self.const_sb("ident_bf", np.eye(128, dtype=np.float32).astype(ml_dtypes.bfloat16), BF16)
        self.ident_f = self.const_sb("ident_f", np.eye(128, dtype=np.float32), F32)
        self.ones_mean = self.const_sb("ones_mean", np.full((128, 128), 1.0 / 128, np.float32).astype(ml_dtypes.bfloat16), BF16)
        self.ones_bf = self.const_sb("ones_bf", np.ones((128, 128), np.float32).astype(ml_dtypes.bfloat16), BF16)
        self.epsc = {}
        for i, e in enumerate((1e-6, 1e-5)):
            t = self.es.enter_context(nc.sbuf_tensor(f"c_eps{i}", [128, 1], F32))
            self.mk.op(self.mk.pool, lambda: nc.gpsimd.memset(t[:], float(e)), w=[Buf()])
            self.epsc[float(e)] = t
        self.mk.barrier()

    def inp(self, name, shape, dt=F32):
        t = self.nc.dram_tensor(name, list(shape), dt, kind="ExternalInput").ap()
        self.dram[name] = t
        self.in_names.append(name)
        return t

    def scr(self, name, shape, dt=BF16):
        if name in self.dram:
            return self.dram[name]
        kind = "ExternalOutput" if name in self.dbg_out else "Internal"
        t = self.nc.dram_tensor(name, list(shape), dt, kind=kind).ap()
        self.dram[name] = t
        return t

    consts = {}

    def const_sb(self, name, arr, dt):
        self.consts = dict(self.consts)
        self.consts[name] = np.ascontiguousarray(arr)
        d = self.inp(name, arr.shape, dt)
        t = self.es.enter_context(self.nc.sbuf_tensor("c_" + name, list(arr.shape), dt))
        b = Buf(name)
        self.mk.dma(self.mk.sp, t[:], d, w=[b])
        return (t, b)

    def const_dram(self, name, arr, dt):
        self.consts = dict(self.consts)
        self.consts[name] = np.ascontiguousarray(arr)
        return self.inp(name, arr.shape, dt)

    def ps(self):
        i = self.rot[self.rot_i % len(self.rot)]
        self.rot_i += 1
        return self.pb[i], self.pbb[i]

    def sbt(self, st, name, shape, dt):
        self.uid += 1
        return st.enter_context(self.nc.sbuf_tensor(f"{name}_{self.uid}", list(shape), dt))

    def evac_eng(self):
        self.uid += 1
        return self.mk.act if self.uid % 2 else self.mk.dve

    def copy(self, eng, out, in_, r, w):
        nc = self.nc
        if eng is self.mk.act:
            return self.mk.op(eng, lambda: nc.scalar.copy(out=out, in_=in_), r=r, w=w)
        return self.mk.op(eng, lambda: eng.h.tensor_copy(out=out, in_=in_), r=r, w=w)

    def rsqrt(self, out, in_, eps, r, w):
        nc, mk = self.nc, self.mk
        mk.op(mk.act, lambda: nc.scalar.activation(out=out, in_=in_, func=AF.Ln, bias=self.eps_col(eps)), r=r, w=w)
        mk.op(mk.act, lambda: nc.scalar.activation(out=out, in_=out, func=AF.Exp, scale=-0.5), r=w, w=w)

    def eps_col(self, eps):
        return self.epsc[float(eps)][:, 0:1]

    def wdep(self, ap):
        reg = getattr(self, "wreg", {})
        ent = reg.get(ap.tensor.name)
        if ent is None:
            return []
        per, bufs = ent
        b = bufs[int(ap.offset) // per]
        q = getattr(self, "cast_pending", [])
        if any(it[3] is b for it in q):
            last = max(i for i, it in enumerate(q) if it[3] is b)
            for _ in range(last + 1):
                _, dst, src, buf = q.pop(0)
                self.mk.dma(self.mk.castq, dst, src, w=[buf])
        return [b]

    def emit_casts(self, n=6):
        q = getattr(self, "cast_pending", [])
        for _ in range(min(n, len(q))):
            _, dst, src, buf = q.pop(0)
            self.mk.dma(self.mk.castq, dst, src, w=[buf])

    def emit_casts_upto(self, layer):
        q = getattr(self, "cast_pending", [])
        while q and q[0][0] < layer + 1:
            _, dst, src, buf = q.pop(0)
            self.mk.dma(self.mk.castq, dst, src, w=[buf])

    def defer(self, fn):
        self._deferred = getattr(self, "_deferred", [])
        self._deferred.append(fn)

    def take(self):
        q = getattr(self, "_deferred", [])
        self._deferred = []
        return q

    def flush(self):
        while getattr(self, "_deferred", []):
            for fn in self.take():
                fn()

    def finish(self):
        self.flush()
        self.mk.barrier()
        self.es.close()

    def precast(self, name, src, rows_per=None):
        shp = list(src.shape)
        dst = self.scr(name + "_bf", shp, BF16)
        s2 = src if len(shp) == 2 else src.rearrange("a b c -> (a b) c")
        d2 = dst if len(shp) == 2 else dst.rearrange("a b c -> (a b) c")
        R, C = s2.shape
        rp = max(1, min(R, (2 << 20) // C))
        b = Buf()
        for r0 in range(0, R, rp):
            r1 = min(R, r0 + rp)
            self.mk.dma(self.mk.pool, d2[r0:r1, :], s2[r0:r1, :])
        return dst

    class Pro:
        pass

    def make_pro(self, st, name, src, K, gain, src_bf16, TB):
        nc, mk = self.nc, self.mk
        p = Prog.Pro()
        p.K, p.KC, p.TB, p.src, p.gain, p.src_bf16 = K, K // 128, TB, src, gain, src_bf16
        p.xT = [self.sbt(st, name + "xT", [128, p.KC, TB], BF16) for _ in range(2)]
        p.xTb = [Buf(), Buf()]
        p.xin = [self.sbt(st, name + "xin", [128, K], BF16 if src_bf16 else F32) for _ in range(2)]
        p.xinb = [Buf(), Buf()]
        p.cnt = 0
        if gain is not None:
            p.xn = [self.sbt(st, name + "xn", [128, K], BF16) for _ in range(2)]
            p.xnb = [Buf(), Buf()]
            p.gt = self.sbt(st, name + "g", [128, K], F32)
            p.gb = Buf()
            mk.dma(mk.sp, p.gt[:], gain.partition_broadcast(128), w=[p.gb])
            p.junk = self.sbt(st, name + "junk", [128, K], BF16)
            p.junkb = Buf()
            p.stat = [self.sbt(st, name + "stat", [128, 2], F32) for _ in range(2)]
            p.statb = [Buf(), Buf()]
        return p

    def prologue(self, p, tb, slot):
        nc, mk = self.nc, self.mk
        K, KC, TB = p.K, p.KC, p.TB
        for t in range(TB // 128):
            j = p.cnt % 2
            p.cnt += 1
            row0 = tb * TB + t * 128
            xin, xinb = p.xin[j], p.xinb[j]
            mk.dma(mk.sp, xin[:], p.src[row0:row0 + 128, :], w=[xinb])
            if p.gain is not None:
                stt, sttb = p.stat[j], p.statb[j]
                mk.op(mk.act, lambda: nc.scalar.activation(out=p.junk[:], in_=xin[:], func=AF.Square,
                                                           scale=float(K) ** -0.5, accum_out=stt[:, 0:1]),
                      r=[xinb], w=[p.junkb, sttb])
                self.rsqrt(stt[:, 1:2], stt[:, 0:1], EPS, [sttb], [sttb])
                xn, xnb = p.xn[j], p.xnb[j]
                mk.op(mk.dve, lambda: nc.vector.scalar_tensor_tensor(out=xn[:], in0=xin[:], scalar=stt[:, 1:2],
                                                                     in1=p.gt[:], op0=ALU.mult, op1=ALU.mult),
                      r=[xinb, sttb, p.gb], w=[xnb])
                srcT, srcb = xn, xnb
            else:
                srcT, srcb = xin, xinb
            for k0 in range(0, KC, 8):
                nk = min(8, KC - k0)
                bank, bankb = self.ps()
                bv = bank[:].bitcast(BF16)
                for kk in range(nk):
                    mk.op(mk.pe, lambda kk=kk: nc.tensor.transpose(out=bv[:, kk * 128:(kk + 1) * 128],
                                                                   in_=srcT[:, (k0 + kk) * 128:(k0 + kk + 1) * 128],
                                                                   identity=self.ident_bf[0][:]),
                          r=[srcb, self.ident_bf[1]], w=[bankb])
                self.copy(self.evac_eng(), p.xT[slot][:, k0:k0 + nk, t * 128:(t + 1) * 128],
                          bv[:, :nk * 128].rearrange("p (k c) -> p k c", c=128), r=[bankb], w=[p.xTb[slot]])

    class Pan:
        pass

    def make_pan(self, st, name, KC, PW, nslots=2):
        pn = Prog.Pan()
        pn.KC, pn.PW = KC, PW
        pn.t = [self.sbt(st, name + "wp", [128, KC, PW], BF16) for _ in range(nslots)]
        pn.b = [Buf() for _ in range(nslots)]
        pn.cnt = 0
        return pn

    def run_mm(self, pn, jobs, xT, xTb, TBW, mid_hook=None):
        nc, mk = self.nc, self.mk
        KC, PW = pn.KC, pn.PW
        panels = []
        for jb in jobs:
            g = jb.get("group", 1)
            step = PW
            for p0 in range(jb["c0"], jb["c1"], step):
                panels.append((jb, p0, min(step, jb["c1"] - p0)))

        def load(i):
            jb, p0, pw = panels[i]
            slot = pn.cnt % len(pn.t)
            pn.cnt += 1
            Wv = jb["W"].rearrange("(k p) n -> p k n", p=128)
            mk.dma(mk.sp, pn.t[slot][:, :, :pw], Wv[:, :, p0:p0 + pw], r=self.wdep(jb["W"]), w=[pn.b[slot]])
            return slot

        slots = {0: load(0)}
        for i, (jb, p0, pw) in enumerate(panels):
            if i + 1 < len(panels):
                slots[i + 1] = load(i + 1)
            slot = slots.pop(i)
            wt, wb = pn.t[slot], pn.b[slot]
            if jb["layout"] == "FM":
                grp = []
                for c0 in range(0, pw, 128):
                    cw = min(128, pw - c0)
                    bank, bankb = self.ps()
                    for kc in range(KC):
                        mk.op(mk.pe, lambda kc=kc: nc.tensor.matmul(bank[:cw, :TBW], lhsT=wt[:, kc, c0:c0 + cw],
                                                                     rhs=xT[:, kc, :], start=(kc == 0),
                                                                     stop=(kc == KC - 1)),
                              r=[wb, xTb], w=[bankb])
                    grp.append((bank[:cw, :TBW], bankb, p0 + c0, cw))
                    if len(grp) == jb.get("group", 1):
                        pend = self.take()
                        jb["epi"](grp)
                        for fn in pend:
                            fn()
                        grp = []
                assert not grp
            else:
                for t in range(TBW // 128):
                    bank, bankb = self.ps()
                    for kc in range(KC):
                        mk.op(mk.pe, lambda kc=kc: nc.tensor.matmul(bank[:, :pw], lhsT=xT[:, kc, t * 128:(t + 1) * 128],
                                                                     rhs=wt[:, kc, :pw], start=(kc == 0),
                                                                     stop=(kc == KC - 1)),
                              r=[wb, xTb], w=[bankb])
                    pend = self.take()
                    jb["epi"]([(bank[:, :pw], bankb, p0, pw, t)])
                    for fn in pend:
                        fn()
            if i == min(1, len(panels) - 1) and mid_hook is not None:
                self.flush()
                mid_hook()

    def epi_tm_store(self, st, dst, col_shift, TB, dt=BF16, func=None, width=512, row_of=None):
        nc, mk = self.nc, self.mk
        stg = [self.sbt(st, "stg", [128, width], dt) for _ in range(2)]
        stgb = [Buf(), Buf()]
        cnt = [0]

        def epi(grp, tb):
            (ps, psb, col, w, t), = grp
            j = cnt[0] % 2
            cnt[0] += 1
            if func is None:
                self.copy(self.evac_eng(), stg[j][:, :w], ps, r=[psb], w=[stgb[j]])
            else:
                mk.op(mk.act, lambda: nc.scalar.activation(out=stg[j][:, :w], in_=ps, func=func), r=[psb], w=[stgb[j]])
            row0 = tb * TB + t * 128
            mk.dma(mk.pool, dst[row0:row0 + 128, col + col_shift:col + col_shift + w], stg[j][:, :w], r=[stgb[j]])
        return epi

    def epi_fm_store(self, st, dst, row_shift, TB, dt=BF16, func=None):
        nc, mk = self.nc, self.mk
        stg = [self.sbt(st, "stgf", [128, TB], dt) for _ in range(2)]
        stgb = [Buf(), Buf()]
        cnt = [0]

        def epi(grp, tb):
            (ps, psb, col, cw), = grp
            j = cnt[0] % 2
            cnt[0] += 1
            if func is None:
                self.copy(self.evac_eng(), stg[j][:cw, :], ps, r=[psb], w=[stgb[j]])
            else:
                mk.op(mk.act, lambda: nc.scalar.activation(out=stg[j][:cw, :], in_=ps, func=func), r=[psb], w=[stgb[j]])
            mk.dma(mk.pool, dst[col + row_shift:col + row_shift + cw, tb * TB:(tb + 1) * TB], stg[j][:cw, :], r=[stgb[j]])
        return epi

    def epi_residual(self, st, h, TB, width=512, src=None):
        src = h if src is None else src
        nc, mk = self.nc, self.mk
        hr = [self.sbt(st, "hres", [128, width], F32) for _ in range(3)]
        hrb = [Buf() for _ in range(3)]
        cnt = [0]

        def epi(grp, tb):
            (ps, psb, col, w, t), = grp
            j = cnt[0] % 3
            cnt[0] += 1
            row0 = tb * TB + t * 128
            mk.dma(mk.sp, hr[j][:, :w], src[row0:row0 + 128, col:col + w], w=[hrb[j]])
            mk.op(mk.dve, lambda: nc.vector.tensor_tensor(out=hr[j][:, :w], in0=ps, in1=hr[j][:, :w], op=ALU.add),
                  r=[psb, hrb[j]], w=[hrb[j]])
            mk.dma(mk.pool, h[row0:row0 + 128, col:col + w], hr[j][:, :w], r=[hrb[j]])
        return epi

    def epi_fm_headnorm(self, st, dst, row_shift, TB, gcol, gcolb, mean=True, eps=EPS):
        nc, mk = self.nc, self.mk
        sq = [self.sbt(st, "hn_sq", [128, TB], BF16) for _ in range(2)]
        sqb = [Buf(), Buf()]
        rs = [self.sbt(st, "hn_rs", [128, TB], F32) for _ in range(2)]
        rsb = [Buf(), Buf()]
        stg = [self.sbt(st, "hn_o", [128, TB], BF16) for _ in range(2)]
        stgb = [Buf(), Buf()]
        cnt = [0]
        ones = self.ones_mean if mean else self.ones_bf

        def epi(grp, tb, gi=0, store=None, deep=False):
            (ps, psb, col, cw), = grp
            assert cw == 128
            j = cnt[0] % 2
            cnt[0] += 1
            mk.op(mk.act, lambda: nc.scalar.activation(out=sq[j][:], in_=ps, func=AF.Square), r=[psb], w=[sqb[j]])

            def part_b1():
                b2, b2b = self.ps()
                mk.op(mk.pe, lambda: nc.tensor.matmul(b2[:, :TB], lhsT=ones[0][:], rhs=sq[j][:], start=True, stop=True),
                      r=[sqb[j], ones[1]], w=[b2b])
                self.rsqrt(rs[j][:], b2[:, :TB], eps, [b2b], [rsb[j]])

            def part_b2():
                mk.op(mk.dve, lambda: nc.vector.scalar_tensor_tensor(out=stg[j][:], in0=ps, scalar=gcol[:, gi:gi + 1],
                                                                     in1=rs[j][:], op0=ALU.mult, op1=ALU.mult),
                      r=[psb, rsb[j], gcolb], w=[stgb[j]])
                if store is None:
                    mk.dma(mk.pool, dst[col + row_shift:col + row_shift + 128, tb * TB:(tb + 1) * TB], stg[j][:], r=[stgb[j]])
                else:
                    store(stg[j], stgb[j])

            if deep:
                def b1_then_defer():
                    part_b1()
                    self.defer(part_b2)
                self.defer(b1_then_defer)
            else:
                def part_b():
                    part_b1()
                    part_b2()
                self.defer(part_b)
        return epi

    def linear(self, name, src, ntok, K, gain, jobs_fn, src_bf16=False, TB=512, PW=None):
        TB = min(TB, ntok)
        KC = K // 128
        if PW is None:
            PW = 512 if KC <= 16 else 256
        with ExitStack() as st:
            p = self.make_pro(st, name, src, K, gain, src_bf16, TB)
            pn = self.make_pan(st, name, KC, PW)
            jobs = jobs_fn(st, TB)
            NB = ntok // TB
            self.prologue(p, 0, 0)
            for tb in range(NB):
                slot = tb % 2
                jj = [dict(jb, epi=(lambda grp, e=jb["epi"], tb=tb: e(grp, tb))) for jb in jobs]
                hook = (lambda tb=tb: self.prologue(p, tb + 1, (tb + 1) % 2)) if tb + 1 < NB else None
                self.run_mm(pn, jj, p.xT[slot], p.xTb[slot], TB, mid_hook=hook)
            self.flush()
            self.mk.barrier()

    def ffn(self, h, gain, w_in, w_out):
        nc, mk = self.nc, self.mk
        S, TB = self.S, min(512, self.S)
        HC = FFN_H // 128
        with ExitStack() as st:
            p = self.make_pro(st, "ffn", h, D, gain, False, TB)
            pn1 = self.make_pan(st, "ffn1", 16, 256, nslots=4)
            pn2 = self.make_pan(st, "ffn2", HC, 256)
            hid = self.sbt(st, "hid", [128, HC, TB], BF16)
            hidb = Buf()
            sg = [self.sbt(st, "sg", [128, TB], F32) for _ in range(2)]
            sgb = [Buf(), Buf()]
            cnt = [0]
            res = self.epi_residual(st, h, TB, width=256)
            NB = S // TB
            self.prologue(p, 0, 0)
            for tb in range(NB):
                slot = tb % 2
                gate_ps = {}

                def epi_gate(grp, tb=tb):
                    (ps, psb, col, cw), = grp
                    gate_ps[col] = (ps, psb)

                def epi_up(grp, tb=tb):
                    (ps, psb, col, cw), = grp
                    gps, gpsb = gate_ps.pop(col - FFN_H)
                    j = cnt[0] % 2
                    cnt[0] += 1
                    mk.op(mk.act, lambda: nc.scalar.activation(out=sg[j][:], in_=gps, func=AF.Silu), r=[gpsb], w=[sgb[j]])
                    hc = (col - FFN_H) // 128
                    mk.op(mk.dve, lambda: nc.vector.tensor_tensor(out=hid[:, hc, :], in0=ps, in1=sg[j][:], op=ALU.mult),
                          r=[psb, sgb[j]], w=[hidb])
                jobs = []
                for c0 in range(0, FFN_H, 256):
                    jobs.append(dict(W=w_in, c0=c0, c1=c0 + 256, layout="FM", epi=epi_gate))
                    jobs.append(dict(W=w_in, c0=FFN_H + c0, c1=FFN_H + c0 + 256, layout="FM", epi=epi_up))
                hook = (lambda tb=tb: self.prologue(p, tb + 1, (tb + 1) % 2)) if tb + 1 < NB else None
                self.run_mm(pn1, jobs, p.xT[slot], p.xTb[slot], TB, mid_hook=hook)
                jobs2 = [dict(W=w_out, c0=0, c1=D, layout="TM", epi=lambda grp, tb=tb: res(grp, tb))]
                self.run_mm(pn2, jobs2, hid, hidb, TB)
            mk.barrier()

    def load_col(self, st, src128, scale=None):
        nc, mk = self.nc, self.mk
        t = self.sbt(st, "col", [128, 1], F32)
        b = Buf()
        mk.dma(mk.sp, t[:], src128.rearrange("(p o) -> p o", o=1), w=[b])
        if scale is not None:
            mk.op(mk.dve, lambda: nc.vector.tensor_scalar(out=t[:], in0=t[:], scalar1=float(scale), scalar2=None,
                                                          op0=ALU.mult), r=[b], w=[b])
        return t, b

    def xa(self, h, mem, ln_mem, mem_gain, wq, wkv, q_gain, k_gain, wo):
        nc, mk = self.nc, self.mk
        S, M = self.S, self.M
        kT = self.scr("xa_kT", [512, M])
        v = self.scr("xa_v", [M, 512])
        qT = self.scr("xa_qT", [512, S])
        o = self.scr("xa_o", [S, 512])

        def jobs_kv(st, TB):
            gc, gcb = self.load_col(st, k_gain)
            hn = self.epi_fm_headnorm(st, kT, 0, TB, gc, gcb)
            tm = self.epi_tm_store(st, v, -512, TB)
            return [dict(W=wkv, c0=0, c1=512, layout="FM", epi=hn),
                    dict(W=wkv, c0=512, c1=1024, layout="TM", epi=tm)]
        self.linear("xakv", mem, M, D, mem_gain, jobs_kv, TB=256)

        def jobs_q(st, TB):
            gc, gcb = self.load_col(st, q_gain, scale=128 ** -0.5)
            hn = self.epi_fm_headnorm(st, qT, 0, TB, gc, gcb)
            return [dict(W=wq, c0=0, c1=512, layout="FM", epi=hn)]
        self.linear("xaq", h, S, D, ln_mem, jobs_q)

        MT = M // 128
        with ExitStack() as st:
            KT = self.sbt(st, "xaKT", [128, 4, M], BF16)
            KTb = Buf()
            mk.dma(mk.sp, KT[:], kT.rearrange("(h p) m -> p h m", p=128), w=[KTb])
            V = self.sbt(st, "xaV", [128, MT, 4, 129], BF16)
            Vb = Buf()
            mk.op(mk.pool, lambda: nc.gpsimd.memset(V[:], 1.0), w=[Vb])
            for kt in range(MT):
                mk.dma(mk.sp, V[:, kt, :, 0:128], v[kt * 128:(kt + 1) * 128, :].rearrange("p (h d) -> p h d", d=128), w=[Vb])
            TB = min(512, S)
            NQ = TB // 128
            qt = [self.sbt(st, "xaq", [128, 4, TB], BF16) for _ in range(2)]
            qtb = [Buf(), Buf()]
            pT = [self.sbt(st, "xapT", [128, TB], BF16) for _ in range(2 * MT)]
            pTb = [Buf() for _ in range(2 * MT)]
            ot = [self.sbt(st, "xaot", [128, NQ, 512], BF16) for _ in range(2)]
            otb = [Buf(), Buf()]
            rc = [self.sbt(st, "xarc", [128, 1], F32) for _ in range(4)]
            rcb = [Buf() for _ in range(4)]
            pc = 0
            rcc = 0

            def score(tb, hh):
                nonlocal pc
                j = tb % 2
                if hh == 0:
                    mk.dma(mk.sp, qt[j][:], qT[:, tb * TB:(tb + 1) * TB].rearrange("(h p) s -> p h s", p=128), w=[qtb[j]])
                cur = []
                for kt in range(MT):
                    bank, bankb = self.ps()
                    mk.op(mk.pe, lambda: nc.tensor.matmul(bank[:, :TB], lhsT=KT[:, hh, kt * 128:(kt + 1) * 128],
                                                          rhs=qt[j][:, hh, :], start=True, stop=True),
                          r=[KTb, qtb[j]], w=[bankb])
                    pi = pc % (2 * MT)
                    pc += 1
                    mk.op(mk.act, lambda: nc.scalar.activation(out=pT[pi][:], in_=bank[:, :TB], func=AF.Exp),
                          r=[bankb], w=[pTb[pi]])
                    cur.append(pi)
                return cur

            def pv(tb, hh, cur):
                nonlocal rcc
                j = tb % 2
                for qi in range(NQ):
                    bank, bankb = self.ps()
                    for kt in range(MT):
                        pi = cur[kt]
                        mk.op(mk.pe, lambda: nc.tensor.matmul(bank[:, :129], lhsT=pT[pi][:, qi * 128:(qi + 1) * 128],
                                                              rhs=V[:, kt, hh, :], start=(kt == 0), stop=(kt == MT - 1)),
                              r=[pTb[pi], Vb], w=[bankb])
                    ri = rcc % 4
                    rcc += 1
                    mk.op(mk.dve, lambda: nc.vector.reciprocal(out=rc[ri][:], in_=bank[:, 128:129]), r=[bankb], w=[rcb[ri]])
                    mk.op(mk.dve, lambda: nc.vector.tensor_scalar(out=ot[j][:, qi, hh * 128:(hh + 1) * 128], in0=bank[:, 0:128],
                                                                  scalar1=rc[ri][:, 0:1], scalar2=None, op0=ALU.mult),
                          r=[bankb, rcb[ri]], w=[otb[j]])
                if hh == 3:
                    for qi in range(NQ):
                        r0 = tb * TB + qi * 128
                        mk.dma(mk.pool, o[r0:r0 + 128, :], ot[j][:, qi, :], r=[otb[j]])

            prev = None
            for tb in range(S // TB):
                for hh in range(4):
                    cur = score(tb, hh)
                    if prev is not None:
                        pv(*prev)
                    prev = (tb, hh, cur)
            pv(*prev)
            mk.barrier()

        def jobs_o(st, TB):
            res = self.epi_residual(st, h, TB)
            return [dict(W=wo, c0=0, c1=D, layout="TM", epi=res)]
        self.linear("xao", o, S, 512, None, jobs_o, src_bf16=True)

    def rot_tables(self):
        if "rot_cos" in self.dram:
            return self.dram["rot_cos"], self.dram["rot_sin"]
        S = self.S
        inv = (np.float32(10000.0) ** (-np.arange(128, dtype=np.float32) / np.float32(128))).astype(np.float32)
        ang = (np.arange(S, dtype=np.float32)[None, :] * inv[:, None]).astype(np.float32)
        c = self.const_dram("rot_cos", np.cos(ang).astype(np.float32), F32)
        s = self.const_dram("rot_sin", np.sin(ang).astype(np.float32), F32)
        return c, s

    def epi_rotary(self, st, dst, row_shift, TB):
        nc, mk = self.nc, self.mk
        cosd, sind = self.rot_tables()
        cs = [self.sbt(st, "rotcs", [128, 2, TB], F32) for _ in range(2)]
        csb = [Buf(), Buf()]
        cs_tb = [None, None]
        tmp = [self.sbt(st, "rott", [128, 4, TB], F32) for _ in range(2)]
        tmpb = [Buf(), Buf()]
        og = [self.sbt(st, "roto", [128, 2, TB], BF16) for _ in range(2)]
        ogb = [Buf(), Buf()]
        cnt = [0]

        def epi(grp, tb):
            (p1, p1b, col, _), (p2, p2b, _, _) = grp
            cj = tb % 2
            if cs_tb[cj] != tb:
                mk.dma(mk.sp, cs[cj][:, 0, :], cosd[:, tb * TB:(tb + 1) * TB], w=[csb[cj]])
                mk.dma(mk.sp, cs[cj][:, 1, :], sind[:, tb * TB:(tb + 1) * TB], w=[csb[cj]])
                cs_tb[cj] = tb
            j = cnt[0] % 2
            cnt[0] += 1
            T, Tb = tmp[j], tmpb[j]
            for k, (pp, ppb, ci) in enumerate(((p1, p1b, 0), (p2, p2b, 1), (p1, p1b, 1), (p2, p2b, 0))):
                mk.op(mk.dve, lambda: nc.vector.tensor_tensor(out=T[:, k, :], in0=pp, in1=cs[cj][:, ci, :], op=ALU.mult),
                      r=[ppb, csb[cj]], w=[Tb])
            mk.op(mk.pool, lambda: nc.gpsimd.tensor_tensor(out=og[j][:, 0, :], in0=T[:, 0, :], in1=T[:, 1, :], op=ALU.subtract),
                  r=[Tb], w=[ogb[j]])
            mk.op(mk.pool, lambda: nc.gpsimd.tensor_tensor(out=og[j][:, 1, :], in0=T[:, 2, :], in1=T[:, 3, :], op=ALU.add),
                  r=[Tb], w=[ogb[j]])
            r0 = col + row_shift
            mk.dma(mk.pool, dst[r0:r0 + 256, tb * TB:(tb + 1) * TB].rearrange("(c p) s -> p c s", p=128), og[j][:], r=[ogb[j]])
        return epi

    def hyb_in(self, h, ln, w_in, q_gain, k_gain):
        S = self.S
        qT = self.scr("nsa_qT", [1024, S])
        kcT = self.scr("nsa_kcT", [256, S])
        vcT = self.scr("nsa_vcT", [256, S])
        ksT = self.scr("nsa_ksT", [256, S])
        vs = self.scr("nsa_vs", [S, 256])
        kwT = self.scr("nsa_kwT", [256, S])
        vw = self.scr("nsa_vw", [S, 256])
        gate = self.scr("nsa_gate", [S, 24], F32)
        rqT = self.scr("ret_qT", [1024, S])
        rkT = self.scr("ret_kT", [1024, S])
        rv = self.scr("ret_v", [S, 1024])
        rg = self.scr("ret_g", [S, 1024])

        def jobs(st, TB):
            gq, gqb = self.load_col(st, q_gain, scale=128 ** -0.5)
            gks, gksb = self.load_col(st, k_gain[1])
            gkw, gkwb = self.load_col(st, k_gain[2])
            J = []
            J.append(dict(W=w_in, c0=0, c1=1024, layout="FM", epi=self.epi_fm_headnorm(st, qT, 0, TB, gq, gqb)))
            J.append(dict(W=w_in, c0=1024, c1=1280, layout="FM", epi=self.epi_fm_store(st, kcT, -1024, TB)))
            J.append(dict(W=w_in, c0=1280, c1=1536, layout="FM", epi=self.epi_fm_store(st, vcT, -1280, TB)))
            J.append(dict(W=w_in, c0=1536, c1=1792, layout="FM", epi=self.epi_fm_headnorm(st, ksT, -1536, TB, gks, gksb)))
            J.append(dict(W=w_in, c0=1792, c1=2048, layout="TM", epi=self.epi_tm_store(st, vs, -1792, TB)))
            J.append(dict(W=w_in, c0=2048, c1=2304, layout="FM", epi=self.epi_fm_headnorm(st, kwT, -2048, TB, gkw, gkwb)))
            J.append(dict(W=w_in, c0=2304, c1=2560, layout="TM", epi=self.epi_tm_store(st, vw, -2304, TB)))
            J.append(dict(W=w_in, c0=2560, c1=2584, layout="TM",
                          epi=self.epi_tm_store(st, gate, -2560, TB, dt=F32, func=AF.Sigmoid, width=24)))
            J.append(dict(W=w_in, c0=2584, c1=3608, layout="FM", group=2, epi=self.epi_rotary(st, rqT, -2584, TB)))
            J.append(dict(W=w_in, c0=3608, c1=4632, layout="FM", group=2, epi=self.epi_rotary(st, rkT, -3608, TB)))
            J.append(dict(W=w_in, c0=4632, c1=5656, layout="TM", epi=self.epi_tm_store(st, rv, -4632, TB)))
            J.append(dict(W=w_in, c0=5656, c1=6680, layout="TM", epi=self.epi_tm_store(st, rg, -5656, TB, func=AF.Silu)))
            return J
        self.linear("hyb", h, S, D, ln, jobs)

    def retention(self, gn_gain, mix_in):
        nc, mk = self.nc, self.mk
        S = self.S
        NCH = S // 128
        rqT, rkT, rv, rg = (self.dram[k] for k in ("ret_qT", "ret_kT", "ret_v", "ret_g"))
        gam = [1.0 - 2.0 ** (-5.0 - hh) for hh in range(4)]
        idx = np.arange(128, dtype=np.float64)
        dm = np.zeros((128, 4, 128), np.float32)
        qd = np.zeros((128, 4), np.float32)
        kd = np.zeros((128, 4), np.float32)
        for hh in range(4):
            lg = np.float32(np.log(np.float32(gam[hh])))
            rel = idx[None, :] - idx[:, None]
            dm[:, hh, :] = np.where(rel >= 0, np.exp(lg * np.maximum(rel, 0.0)), 0.0) / 16.0
            qd[:, hh] = np.exp(lg * (idx + 1.0)) / 16.0
            kd[:, hh] = np.exp(lg * (127.0 - idx))
        cdec = [float(np.exp(np.float32(np.log(np.float32(g))) * 128.0)) for g in gam]
        with ExitStack() as st:
            if "ret_dm" not in self.dram:
                self.const_dram("ret_dm", dm, F32)
                self.const_dram("ret_qd", qd, F32)
                self.const_dram("ret_kd", kd, F32)
            DM = self.sbt(st, "retDM", [128, 4, 128], F32)
            QD = self.sbt(st, "retQD", [128, 4], F32)
            KD = self.sbt(st, "retKD", [128, 4], F32)
            GN = self.sbt(st, "retGN", [128, 1024], F32)
            cb = Buf()
            mk.dma(mk.sp, DM[:], self.dram["ret_dm"], w=[cb])
            mk.dma(mk.sp, QD[:], self.dram["ret_qd"], w=[cb])
            mk.dma(mk.sp, KD[:], self.dram["ret_kd"], w=[cb])
            mk.dma(mk.sp, GN[:], gn_gain.partition_broadcast(128), w=[cb])
            stf = self.sbt(st, "retSf", [128, 4, 2, 256], F32)
            stb = self.sbt(st, "retSb", [128, 4, 2, 256], BF16)
            stfb, stbb = Buf(), Buf()
            mk.op(mk.pool, lambda: nc.gpsimd.memset(stf[:], 0.0), w=[stfb])
            mk.op(mk.pool, lambda: nc.gpsimd.memset(stb[:], 0.0), w=[stbb])
            qt = [self.sbt(st, "retq", [128, 8, 128], BF16) for _ in range(2)]
            kt = [self.sbt(st, "retk", [128, 8, 128], BF16) for _ in range(2)]
            vt = [self.sbt(st, "retv", [128, 1024], BF16) for _ in range(2)]
            gt = [self.sbt(st, "retg", [128, 1024], BF16) for _ in range(2)]
            inb = [Buf(), Buf()]
            ktm = [self.sbt(st, "retktm", [128, 256], BF16) for _ in range(2)]
            ktmb = [Buf(), Buf()]
            innT = [self.sbt(st, "retin", [128, 128], BF16) for _ in range(2)]
            innTb = [Buf(), Buf()]
            vd = [self.sbt(st, "retvd", [128, 256], BF16) for _ in range(2)]
            vdb = [Buf(), Buf()]
            osb = [self.sbt(st, "reto", [128, 4, 256], F32) for _ in range(2)]
            osbb = [Buf(), Buf()]
            junk = self.sbt(st, "retjunk", [128, 4, 256], F32)
            junkb = Buf()
            stat = [self.sbt(st, "retstat", [128, 16], F32) for _ in range(2)]
            statb = [Buf(), Buf()]
            yo = [self.sbt(st, "rety", [128, 1024], BF16) for _ in range(2)]
            yob = [Buf(), Buf()]
            c2 = 0
            for n in range(NCH):
                j = n % 2
                cs = slice(n * 128, (n + 1) * 128)
                mk.dma(mk.sp, qt[j][:], rqT[:, cs].rearrange("(c p) s -> p c s", p=128), w=[inb[j]])
                mk.dma(mk.sp, kt[j][:], rkT[:, cs].rearrange("(c p) s -> p c s", p=128), w=[inb[j]])
                mk.dma(mk.sp, vt[j][:], rv[cs, :], w=[inb[j]])
                mk.dma(mk.sp, gt[j][:], rg[cs, :], w=[inb[j]])
                O, Ob = osb[j], osbb[j]
                for hh in range(4):
                    i2 = c2 % 2
                    c2 += 1
                    bt, btb = self.ps()
                    btv = bt[:].bitcast(BF16)
                    for dc in range(2):
                        mk.op(mk.pe, lambda: nc.tensor.transpose(out=btv[:, dc * 128:(dc + 1) * 128], in_=kt[j][:, hh * 2 + dc, :],
                                                                 identity=self.ident_bf[0][:]),
                              r=[inb[j], self.ident_bf[1]], w=[btb])
                    self.copy(mk.act, ktm[i2][:], btv[:, 0:256], r=[btb], w=[ktmb[i2]])
                    bi, bib = self.ps()
                    for dc in range(2):
                        mk.op(mk.pe, lambda: nc.tensor.matmul(bi[:, :128], lhsT=kt[j][:, hh * 2 + dc, :], rhs=qt[j][:, hh * 2 + dc, :],
                                                              start=(dc == 0), stop=(dc == 1)), r=[inb[j]], w=[bib])
                    mk.op(mk.dve, lambda: nc.vector.tensor_tensor(out=innT[i2][:], in0=bi[:, :128], in1=DM[:, hh, :], op=ALU.mult),
                          r=[bib, cb], w=[innTb[i2]])
                    bA, bAb = self.ps()
                    mk.op(mk.pe, lambda: nc.tensor.matmul(bA[:, :256], lhsT=innT[i2][:], rhs=vt[j][:, hh * 256:(hh + 1) * 256],
                                                          start=True, stop=True), r=[innTb[i2], inb[j]], w=[bAb])
                    bB, bBb = self.ps()
                    for dc in range(2):
                        mk.op(mk.pe, lambda: nc.tensor.matmul(bB[:, :256], lhsT=qt[j][:, hh * 2 + dc, :], rhs=stb[:, hh, dc, :],
                                                              start=(dc == 0), stop=(dc == 1)), r=[inb[j], stbb], w=[bBb])
                    self.copy(mk.act, O[:, hh, :], bA[:, :256], r=[bAb], w=[Ob])
                    mk.op(mk.dve, lambda: nc.vector.scalar_tensor_tensor(out=O[:, hh, :], in0=bB[:, :256], scalar=QD[:, hh:hh + 1],
                                                                         in1=O[:, hh, :], op0=ALU.mult, op1=ALU.add),
                          r=[bBb, cb, Ob], w=[Ob])
                    mk.op(mk.pool, lambda: nc.gpsimd.tensor_scalar(out=vd[i2][:], in0=vt[j][:, hh * 256:(hh + 1) * 256],
                                                                   scalar1=KD[:, hh:hh + 1], scalar2=None, op0=ALU.mult),
                          r=[inb[j], cb], w=[vdb[i2]])
                    for dc in range(2):
                        bS, bSb = self.ps()
                        mk.op(mk.pe, lambda: nc.tensor.matmul(bS[:, :256], lhsT=ktm[i2][:, dc * 128:(dc + 1) * 128], rhs=vd[i2][:],
                                                              start=True, stop=True), r=[ktmb[i2], vdb[i2]], w=[bSb])
                        mk.op(mk.dve, lambda: nc.vector.scalar_tensor_tensor(out=stf[:, hh, dc, :], in0=stf[:, hh, dc, :],
                                                                             scalar=cdec[hh], in1=bS[:, :256],
                                                                             op0=ALU.mult, op1=ALU.add),
                              r=[bSb, stfb], w=[stfb])
                    self.copy(mk.act, stb[:, hh, :, :], stf[:, hh, :, :], r=[stfb], w=[stbb])
                Sx, Sxb = stat[j], statb[j]
                mk.op(mk.dve, lambda: nc.vector.tensor_reduce(out=Sx[:, 0:4], in_=O[:], axis=AX.X, op=ALU.add), r=[Ob], w=[Sxb])
                mk.op(mk.act, lambda: nc.scalar.activation(out=junk[:], in_=O[:], func=AF.Square), r=[Ob], w=[junkb])
                mk.op(mk.dve, lambda: nc.vector.tensor_reduce(out=Sx[:, 4:8], in_=junk[:], axis=AX.X, op=ALU.add), r=[junkb, Sxb], w=[Sxb])
                mk.op(mk.dve, lambda: nc.vector.tensor_scalar(out=Sx[:, 0:4], in0=Sx[:, 0:4], scalar1=1.0 / 256, scalar2=None,
                                                              op0=ALU.mult), r=[Sxb], w=[Sxb])
                mk.op(mk.dve, lambda: nc.vector.tensor_tensor(out=Sx[:, 8:12], in0=Sx[:, 0:4], in1=Sx[:, 0:4], op=ALU.mult),
                      r=[Sxb], w=[Sxb])
                mk.op(mk.dve, lambda: nc.vector.scalar_tensor_tensor(out=Sx[:, 8:12], in0=Sx[:, 4:8], scalar=1.0 / 256,
                                                                     in1=Sx[:, 8:12], op0=ALU.mult, op1=ALU.subtract),
                      r=[Sxb], w=[Sxb])
                self.rsqrt(Sx[:, 12:16], Sx[:, 8:12], 1e-5, [Sxb], [Sxb])
                for hh in range(4):
                    mk.op(mk.dve, lambda: nc.vector.tensor_scalar(out=O[:, hh, :], in0=O[:, hh, :], scalar1=Sx[:, hh:hh + 1],
                                                                  scalar2=Sx[:, 12 + hh:13 + hh], op0=ALU.subtract, op1=ALU.mult),
                          r=[Ob, Sxb], w=[Ob])
                Of = O[:].rearrange("p h d -> p (h d)")
                mk.op(mk.pool, lambda: nc.gpsimd.tensor_tensor(out=Of, in0=Of, in1=GN[:], op=ALU.mult), r=[Ob, cb], w=[Ob])
                mk.op(mk.dve, lambda: nc.vector.tensor_tensor(out=yo[j][:], in0=Of, in1=gt[j][:], op=ALU.mult),
                      r=[Ob, inb[j]], w=[yob[j]])
                mk.dma(mk.pool, mix_in[cs, 1024:2048], yo[j][:], r=[yob[j]])
            mk.barrier()


def _rel_bucket_np(dist):
    import jax
    import jax.numpy as jnp
    with jax.default_device(jax.devices("cpu")[0]):
        d = jnp.maximum(jnp.asarray(dist, jnp.int32), 0)
        max_exact = 16
        scaled = (jnp.log(jnp.maximum(d, max_exact).astype(jnp.float32) / max_exact)
                  / math.log(1024 / max_exact) * (32 - max_exact))
        large = jnp.minimum(max_exact + scaled.astype(jnp.int32), 31)
        return np.asarray(jnp.where(d < max_exact, d, large))


def nsa_index_tables(S):
    NQT = S // 128
    NCBP = S // 16
    i = np.arange(128)
    toep = np.zeros((128, 10, 128), np.int32)
    for dlt in range(9):
        dist = 128 * dlt + i[None, :] - i[:, None]
        b = _rel_bucket_np(dist)
        toep[:, dlt, :] = np.where(dist >= 0, b, 32)
    dist = 128 * 4 + i[None, :] - i[:, None]
    toep[:, 9, :] = np.where(dist < 512, _rel_bucket_np(dist), 32)
    OFF = 8 * (NQT - 1)
    RE = OFF + NCBP
    npp = np.arange(RE) - OFF
    dist = i[None, :] - 16 * npp[:, None] - 31
    cmpx = np.where(dist >= 0, _rel_bucket_np(dist), 32).astype(np.int32)
    return toep, cmpx, OFF


def nsa_const_tables(S):
    NSB = S // 64
    NCBP = S // 16
    NCB = NCBP - 1
    n = np.arange(NCBP)
    j = np.arange(NSB)
    cs, ce = n * 16, n * 16 + 31
    cover = ((cs[:, None] < j[None, :] * 64 + 64) & (ce[:, None] >= j[None, :] * 64)).astype(np.float32)
    cover[NCB:] = 0.0
    t = np.arange(S)
    cur = (t // 64)[:, None]
    forced = (j[None, :] == 0) | (j[None, :] == cur) | (j[None, :] == cur - 1)
    future = j[None, :] > cur
    keep = np.where(forced | future, 0.0, 1.0).astype(np.float32)
    add = np.where(future, -1e6 - j[None, :], np.where(forced, 1e6 + j[None, :], 0.0)).astype(np.float32)
    NQT = S // 128
    E = np.zeros((64, NQT, 128), np.float32)
    for kt in range(NQT):
        E[2 * kt, kt, :64] = 1.0
        E[2 * kt + 1, kt, 64:] = 1.0
    return cover, keep, add, E


class ProgNSA:
    def nsa_cmp(self, cmp_pos, w1, w2, k_gain0):
        nc, mk = self.nc, self.mk
        S = self.S
        NCBP = S // 16
        NCB = NCBP - 1
        NKT = max(1, NCBP // 128)
        kcmpT = self.scr("nsa_kcmpT", [2, 128, NCBP])
        vcmp = self.scr("nsa_vcmp", [2, NCBP, 128])
        srcs = (self.dram["nsa_kcT"], self.dram["nsa_vcT"])
        with ExitStack() as st:
            gk, gkb = self.load_col(st, k_gain0)
            hn = self.epi_fm_headnorm(st, None, 0, NCBP, gk, gkb)
            W1 = self.sbt(st, "cW1", [128, 32, 256], BF16)
            W2 = self.sbt(st, "cW2", [128, 2, 128], BF16)
            posf = self.sbt(st, "cposf", [32, 128], F32)
            posb = self.sbt(st, "cposb", [32, 128], BF16)
            posT = self.sbt(st, "cposT", [128, 32], BF16)
            xT = [self.sbt(st, "cx", [128, S], BF16) for _ in range(2)]
            hid = self.sbt(st, "chid", [128, 2, NCBP], BF16)
            bias = self.sbt(st, "cbias", [128, 2], F32)
            vo = self.sbt(st, "cvo", [128, NKT, 128], BF16)
            wb, pb_, xb, hb, bb, vob = Buf(), Buf(), [Buf(), Buf()], Buf(), Buf(), Buf()
            mk.op(mk.pool, lambda: nc.gpsimd.memset(hid[:], 0.0), w=[hb])
            xi = 0
            for kv in range(2):
                mk.dma(mk.sp, W1[:], w1[kv].rearrange("(l d) n -> d l n", d=128), r=self.wdep(w1[kv]), w=[wb])
                mk.dma(mk.sp, W2[:], w2[kv].rearrange("(c p) n -> p c n", p=128), r=self.wdep(w2[kv]), w=[wb])
                mk.dma(mk.sp, posf[:], cmp_pos[kv], w=[pb_])
                self.copy(mk.dve, posb[:], posf[:], r=[pb_], w=[pb_])
                bt, btb = self.ps()
                btv = bt[:].bitcast(BF16)
                mk.op(mk.pe, lambda: nc.tensor.transpose(out=btv[:, 0:32], in_=posb[:, :], identity=self.ident_bf[0][0:32, 0:32]),
                      r=[pb_, self.ident_bf[1]], w=[btb])
                self.copy(mk.dve, posT[:], btv[:, 0:32], r=[btb], w=[pb_])
                for hc in range(2):
                    bk, bkb = self.ps()
                    for l in range(32):
                        mk.op(mk.pe, lambda: nc.tensor.matmul(bk[:, 0:1], lhsT=W1[:, l, hc * 128:(hc + 1) * 128], rhs=posT[:, l:l + 1],
                                                              start=(l == 0), stop=(l == 31)), r=[wb, pb_], w=[bkb])
                    self.copy(mk.dve, bias[:, hc:hc + 1], bk[:, 0:1], r=[bkb], w=[bb])
                for g in range(2):
                    x, xbb = xT[xi % 2], xb[xi % 2]
                    xi += 1
                    mk.dma(mk.sp, x[:], srcs[kv][g * 128:(g + 1) * 128, :], w=[xbb])
                    for hc in range(2):
                        bk, bkb = self.ps()
                        for l in range(32):
                            mk.op(mk.pe, lambda: nc.tensor.matmul(bk[:, :NCB], lhsT=W1[:, l, hc * 128:(hc + 1) * 128],
                                                                  rhs=x[:, l:l + 16 * (NCB - 1) + 1:16],
                                                                  start=(l == 0), stop=(l == 31)), r=[wb, xbb], w=[bkb])
                        mk.op(mk.act, lambda: nc.scalar.activation(out=hid[:, hc, :NCB], in_=bk[:, :NCB], func=AF.Silu,
                                                                   bias=bias[:, hc:hc + 1]), r=[bkb, bb], w=[hb])
                    if kv == 0:
                        bk, bkb = self.ps()
                        for hc in range(2):
                            mk.op(mk.pe, lambda: nc.tensor.matmul(bk[:, :NCBP], lhsT=W2[:, hc, :], rhs=hid[:, hc, :],
                                                                  start=(hc == 0), stop=(hc == 1)), r=[wb, hb], w=[bkb])
                        hn([(bk[:, :NCBP], bkb, 0, 128)], 0,
                           store=lambda t, tb_, g=g: mk.dma(mk.pool, kcmpT[g], t[:], r=[tb_]))
                        self.flush()
                    else:
                        for nt in range(NKT):
                            bk, bkb = self.ps()
                            nn = min(128, NCBP)
                            for hc in range(2):
                                mk.op(mk.pe, lambda: nc.tensor.matmul(bk[:nn, :128], lhsT=hid[:, hc, nt * 128:nt * 128 + nn], rhs=W2[:, hc, :],
                                                                      start=(hc == 0), stop=(hc == 1)), r=[wb, hb], w=[bkb])
                            self.copy(self.evac_eng(), vo[:nn, nt, :], bk[:nn, :128], r=[bkb], w=[vob])
                        nn = min(128, NCBP)
                        mk.dma(mk.pool, vcmp[g].rearrange("(t p) d -> p t d", p=nn), vo[:nn, :, :], r=[vob])
            self.flush()
            mk.barrier()

    def nsa_attn(self, rel_toep, rel_cmpx, cmp_off, mix_in):
        nc, mk = self.nc, self.mk
        S = self.S
        NQT, NSB, NCBP = S // 128, S // 64, S // 16
        NKT = max(1, NCBP // 128)
        KP = min(128, NCBP)
        RW = 129 + NSB
        assert NSB >= 16 and NSB <= 64
        if "nsa_cover" not in self.dram:
            cover, keep, add, E = nsa_const_tables(S)
            self.const_dram("nsa_cover", cover.astype(ml_dtypes.bfloat16), BF16)
            self.const_dram("nsa_keep", keep, F32)
            self.const_dram("nsa_add", add, F32)
            self.const_dram("nsa_E", E.astype(ml_dtypes.bfloat16), BF16)
        qT, ksT, kwT, vs, vw, gate = (self.dram[k] for k in ("nsa_qT", "nsa_ksT", "nsa_kwT", "nsa_vs", "nsa_vw", "nsa_gate"))
        kcmpT, vcmp = self.dram["nsa_kcmpT"], self.dram["nsa_vcmp"]
        save_rot = self.rot
        self.rot = [0, 1, 2, 3]
        accs = [(self.pb[4 + i], self.pbb[4 + i]) for i in range(4)]
        with ExitStack() as st:
            KcT = self.sbt(st, "nKcT", [128, NCBP], BF16)
            VC = self.sbt(st, "nVC", [128, NKT, RW], BF16)
            KsT = self.sbt(st, "nKsT", [128, S], BF16)
            KwT = self.sbt(st, "nKwT", [128, S], BF16)
            Vs = self.sbt(st, "nVs", [128, NQT, 129], BF16)
            Vw = self.sbt(st, "nVw", [128, NQT, 129], BF16)
            TP = self.sbt(st, "nTP", [128, 10, 4, 128], BF16)
            E = self.sbt(st, "nE", [64, NQT, 128], BF16)
            resb = Buf()
            mk.dma(mk.sp, E[:], self.dram["nsa_E"], w=[resb])
            Q = [self.sbt(st, "nQ", [128, 4, 128], BF16) for _ in range(2)]
            G = [self.sbt(st, "nG", [128, 12], F32) for _ in range(2)]
            KA = [self.sbt(st, "nKA", [128, 2, NSB], F32) for _ in range(2)]
            CB = [self.sbt(st, "nCB", [128, NKT, 4, 128], BF16) for _ in range(2)]
            inb = [Buf(), Buf()]
            Pt = [self.sbt(st, "nP", [128, 512], BF16) for _ in range(4)]
            Ptb = [Buf() for _ in range(4)]
            pti = 0
            acc = [self.sbt(st, "nacc", [128, 4, 128], F32) for _ in range(2)]
            accb = [Buf(), Buf()]
            accC = [self.sbt(st, "naccC", [128, 4, RW], F32) for _ in range(2)]
            accCb = [Buf(), Buf()]
            accW = [self.sbt(st, "naccW", [128, 4, 129], F32) for _ in range(2)]
            accWb = [Buf(), Buf()]
            ob = [self.sbt(st, "nob", [128, 512], BF16) for _ in range(2)]
            obb = [Buf(), Buf()]
            sm = [self.sbt(st, "nsm", [128, 16], F32) for _ in range(2)]
            smb = [Buf(), Buf()]
            imp = [self.sbt(st, "nimp", [128, 3, NSB], F32) for _ in range(2)]
            impb = [Buf(), Buf()]
            m8 = [self.sbt(st, "nm8", [128, 16], F32) for _ in range(2)]
            mng = [self.sbt(st, "nmng", [128, NSB], BF16) for _ in range(2)]
            mT4 = [self.sbt(st, "nmT4", [64, 4, 128], BF16) for _ in range(2)]
            mT4b = [Buf(), Buf()]
            for g in range(2):
                mk.dma(mk.sp, KcT[:KP, :] if False else KcT[:, :], kcmpT[g], w=[resb])
                mk.op(mk.pool, lambda: nc.gpsimd.memset(VC[:], 1.0), w=[resb])
                mk.op(mk.pool, lambda: nc.gpsimd.memset(Vs[:], 1.0), w=[resb])
                mk.op(mk.pool, lambda: nc.gpsimd.memset(Vw[:], 1.0), w=[resb])
                mk.dma(mk.sp, VC[:KP, :, 0:128], vcmp[g].rearrange("(t p) d -> p t d", p=KP), w=[resb])
                mk.dma(mk.sp, VC[:KP, :, 129:RW], self.dram["nsa_cover"].rearrange("(t p) j -> p t j", p=KP), w=[resb])
                mk.dma(mk.sp, KsT[:], ksT[g * 128:(g + 1) * 128, :], w=[resb])
                mk.dma(mk.sp, KwT[:], kwT[g * 128:(g + 1) * 128, :], w=[resb])
                mk.dma(mk.sp, Vs[:, :, 0:128], vs[:, g * 128:(g + 1) * 128].rearrange("(t p) d -> p t d", p=128), w=[resb])
                mk.dma(mk.sp, Vw[:, :, 0:128], vw[:, g * 128:(g + 1) * 128].rearrange("(t p) d -> p t d", p=128), w=[resb])
                for ti in range(10):
                    mk.dma(mk.pool, TP[:, ti], rel_toep[:, ti, g * 4:(g + 1) * 4, :], w=[resb])
                for qt in range(NQT):
                    j = qt % 2
                    rows = slice(qt * 128, (qt + 1) * 128)
                    mk.dma(mk.sp, Q[j][:], qT[g * 512:(g + 1) * 512, rows].rearrange("(h p) s -> p h s", p=128), w=[inb[j]])
                    mk.dma(mk.sp, G[j][:], gate[rows, g * 12:(g + 1) * 12], w=[inb[j]])
                    mk.dma(mk.sp, KA[j][:, 0, :], self.dram["nsa_keep"][rows, :], w=[inb[j]])
                    mk.dma(mk.sp, KA[j][:, 1, :], self.dram["nsa_add"][rows, :], w=[inb[j]])
                    r0 = cmp_off - 8 * qt
                    mk.dma(mk.pool, CB[j][:KP], rel_cmpx[r0:r0 + NCBP, g * 4:(g + 1) * 4, :].rearrange("(t p) h i -> p t h i", p=KP),
                           w=[inb[j]])
                    Qf = Q[j][:].rearrange("p h s -> p (h s)")
                    A, Ab = acc[j], accb[j]
                    SM, SMb = sm[j], smb[j]

                    def branch(kts, klhs, extra, Vt, vsl, width, first):
                        nonlocal pti
                        banks = accs
                        def pv(ki, kt, pi):
                            for hh in range(4):
                                bk, bkb = banks[hh]
                                reg = bk[:, 0:width]
                                mk.op(mk.pe, lambda: nc.tensor.matmul(reg, lhsT=Pt[pi][:, hh * 128:(hh + 1) * 128], rhs=vsl(kt),
                                                                      start=(ki == 0), stop=(ki == len(kts) - 1)),
                                      r=[Ptb[pi], resb], w=[bkb])
                        prev = None
                        for ki, kt in enumerate(kts):
                            bank, bankb = self.ps()
                            mms = [(klhs(kt), Qf, [resb, inb[j]])] + extra(kt)
                            for mi, (l_, r_, bufs) in enumerate(mms):
                                mk.op(mk.pe, lambda: nc.tensor.matmul(bank[:, :512], lhsT=l_, rhs=r_, start=(mi == 0),
                                                                      stop=(mi == len(mms) - 1)), r=bufs, w=[bankb])
                            pi = pti % 4
                            pti += 1
                            mk.op(mk.act, lambda: nc.scalar.activation(out=Pt[pi][:], in_=bank[:, :512], func=AF.Exp),
                                  r=[bankb], w=[Ptb[pi]])
                            if prev is not None:
                                pv(*prev)
                            prev = (ki, kt, pi)
                        pv(*prev)
                        return banks

                    def evac(banks, width, dst, dstb):
                        for hh in range(4):
                            bk, bkb = banks[hh]
                            self.copy(mk.act, dst[:, hh, 0:width], bk[:, 0:width], r=[bkb], w=[dstb])

                    def sums(src, srcb, gi):
                        c0 = 4 * gi
                        mk.op(mk.dve, lambda: nc.vector.tensor_scalar(out=SM[:, c0:c0 + 4], in0=src[:, :, 128], scalar1=1e-30,
                                                                      scalar2=None, op0=ALU.max), r=[srcb], w=[SMb])
                        mk.op(mk.dve, lambda: nc.vector.reciprocal(out=SM[:, c0:c0 + 4], in_=SM[:, c0:c0 + 4]), r=[SMb], w=[SMb])
                        mk.op(mk.dve, lambda: nc.vector.tensor_tensor(out=SM[:, 12:16], in0=SM[:, c0:c0 + 4],
                                                                      in1=G[j][:].rearrange("p (h k) -> p h k", k=3)[:, :, gi], op=ALU.mult),
                              r=[SMb, inb[j]], w=[SMb])

                    def accum(src, srcb, first):
                        for hh in range(4):
                            if first:
                                mk.op(mk.dve, lambda: nc.vector.tensor_scalar(out=A[:, hh, :], in0=src[:, hh, 0:128],
                                                                              scalar1=SM[:, 12 + hh:13 + hh], scalar2=None,
                                                                              op0=ALU.mult), r=[srcb, SMb], w=[Ab])
                            else:
                                mk.op(mk.dve, lambda: nc.vector.scalar_tensor_tensor(out=A[:, hh, :], in0=src[:, hh, 0:128],
                                                                                     scalar=SM[:, 12 + hh:13 + hh], in1=A[:, hh, :],
                                                                                     op0=ALU.mult, op1=ALU.add),
                                      r=[srcb, SMb, Ab], w=[Ab])

                    assert KP == 128
                    kts_c = [kt for kt in range(NKT) if 8 * qt + 6 >= kt * 128]
                    banks = branch(kts_c, lambda kt: KcT[:, kt * KP:(kt + 1) * KP],
                                   lambda kt: [(self.ident_bf[0][:KP, :KP], CB[j][:KP, kt].rearrange("p h s -> p (h s)"),
                                                [self.ident_bf[1], inb[j]])],
                                   VC, lambda kt: VC[:KP, kt, :], RW, True)
                    AC, ACb = accC[j], accCb[j]
                    evac(banks, RW, AC, ACb)
                    sums(AC, ACb, 0)
                    IM, IMb = imp[j], impb[j]
                    for hh in range(4):
                        reg = AC[:, hh, 129:RW]
                        if hh == 0:
                            mk.op(mk.dve, lambda: nc.vector.tensor_scalar(out=IM[:, 0, :], in0=reg, scalar1=SM[:, 0:1], scalar2=None,
                                                                          op0=ALU.mult), r=[ACb, SMb], w=[IMb])
                        else:
                            mk.op(mk.dve, lambda: nc.vector.scalar_tensor_tensor(out=IM[:, 0, :], in0=reg, scalar=SM[:, hh:hh + 1],
                                                                                 in1=IM[:, 0, :], op0=ALU.mult, op1=ALU.add),
                                  r=[ACb, SMb, IMb], w=[IMb])
                    mk.op(mk.dve, lambda: nc.vector.tensor_tensor(out=IM[:, 0, :], in0=IM[:, 0, :], in1=KA[j][:, 0, :], op=ALU.mult),
                          r=[IMb, inb[j]], w=[IMb])
                    mk.op(mk.dve, lambda: nc.vector.tensor_tensor(out=IM[:, 0, :], in0=IM[:, 0, :], in1=KA[j][:, 1, :], op=ALU.add),
                          r=[IMb, inb[j]], w=[IMb])
                    M8 = m8[j]
                    mk.op(mk.dve, lambda: nc.vector.max(out=M8[:, 0:8], in_=IM[:, 0, :]), r=[IMb], w=[IMb])
                    mk.op(mk.dve, lambda: nc.vector.match_replace(out=IM[:, 1, :], in_to_replace=M8[:, 0:8], in_values=IM[:, 0, :],
                                                                  imm_value=-3.0e6), r=[IMb], w=[IMb])
                    mk.op(mk.dve, lambda: nc.vector.max(out=M8[:, 8:16], in_=IM[:, 1, :]), r=[IMb], w=[IMb])
                    mk.op(mk.dve, lambda: nc.vector.tensor_scalar(out=IM[:, 2, :], in0=IM[:, 0, :], scalar1=M8[:, 15:16], scalar2=None,
                                                                  op0=ALU.is_ge), r=[IMb], w=[IMb])
                    mk.op(mk.dve, lambda: nc.vector.tensor_scalar(out=mng[j][:], in0=IM[:, 2, :], scalar1=1.0, scalar2=BIG,
                                                                  op0=ALU.subtract, op1=ALU.mult), r=[IMb], w=[IMb])
                    accum(AC, ACb, True)
                    banks = branch(list(range(max(0, qt - 4), qt + 1)), lambda kt: KwT[:, kt * 128:(kt + 1) * 128],
                                   lambda kt: [(self.ident_bf[0][:], TP[:, (qt - kt) if qt - kt < 4 else 9].rearrange("p h s -> p (h s)"),
                                                [self.ident_bf[1], resb])],
                                   Vw, lambda kt: Vw[:, kt, :], 129, False)
                    AW, AWb = accW[j], accWb[j]
                    evac(banks, 129, AW, AWb)
                    bt, btb = self.ps()
                    btv = bt[:].bitcast(BF16)
                    mk.op(mk.pe, lambda: nc.tensor.transpose(out=btv[:NSB, 0:128], in_=mng[j][:, :], identity=self.ident_bf[0][:]),
                          r=[IMb, self.ident_bf[1]], w=[btb])
                    mk.op(mk.dve, lambda: nc.vector.tensor_copy(out=mT4[j][:NSB], in_=btv[:NSB, 0:128].unsqueeze(1).to_broadcast([NSB, 4, 128])),
                          r=[btb], w=[mT4b[j]])
                    mTf = mT4[j][:NSB].rearrange("p h s -> p (h s)")
                    sums(AW, AWb, 2)
                    accum(AW, AWb, False)
                    banks = branch(list(range(qt + 1)), lambda kt: KsT[:, kt * 128:(kt + 1) * 128],
                                   lambda kt: [(self.ident_bf[0][:], TP[:, min(qt - kt, 8)].rearrange("p h s -> p (h s)"),
                                                [self.ident_bf[1], resb]),
                                               (E[:NSB, kt, :], mTf, [resb, mT4b[j]])],
                                   Vs, lambda kt: Vs[:, kt, :], 129, False)
                    AS, ASb = accW[j], accWb[j]
                    evac(banks, 129, AS, ASb)
                    sums(AS, ASb, 1)
                    accum(AS, ASb, False)
                    self.copy(mk.act, ob[j][:], A[:].rearrange("p h d -> p (h d)"), r=[Ab], w=[obb[j]])
                    mk.dma(mk.pool, mix_in[rows, g * 512:(g + 1) * 512], ob[j][:], r=[obb[j]])
                mk.barrier()
        self.rot = save_rot


for _n, _f in list(ProgNSA.__dict__.items()):
    if callable(_f):
        setattr(Prog, _n, _f)


class ProgGDN:
    def gdn_in(self, h, ln, w_in, conv_w):
        nc, mk = self.nc, self.mk
        S = self.S
        qT = self.scr("gdn_qT", [2048, S])
        kT = self.scr("gdn_kT", [2048, S])
        vT = self.scr("gdn_vT", [4096, S])
        zs = self.scr("gdn_zs", [S, 4096])
        ba = self.scr("gdn_ba", [S, 64], F32)

        def jobs(st, TB):
            cw4 = self.sbt(st, "cw4", [4, 8192], F32)
            CW = self.sbt(st, "CW", [128, 64, 4], F32)
            cwb = Buf()
            mk.dma(mk.sp, cw4[:], conv_w, w=[cwb])
            for c0 in range(0, 64, 32):
                bk, bkb = self.ps()
                for c in range(c0, c0 + 32):
                    mk.op(mk.pe, lambda: nc.tensor.transpose(out=bk[:, (c - c0) * 4:(c - c0) * 4 + 4], in_=cw4[0:4, c * 128:(c + 1) * 128],
                                                             identity=self.ident_f[0][0:4, 0:4]), r=[cwb, self.ident_f[1]], w=[bkb])
                self.copy(mk.dve, CW[:, c0:c0 + 32, :], bk[:, 0:128].rearrange("p (c k) -> p c k", k=4), r=[bkb], w=[cwb])
            HL = self.sbt(st, "HL", [128, 64, 3], F32)
            hlb = Buf()
            mk.op(mk.pool, lambda: nc.gpsimd.memset(HL[:], 0.0), w=[hlb])
            tmp = [self.sbt(st, "cvt", [128, TB + 3], F32) for _ in range(2)]
            tmpb = [Buf(), Buf()]
            accs = [self.sbt(st, "cva", [128, TB], F32) for _ in range(2)]
            accb = [Buf(), Buf()]
            sil = [self.sbt(st, "cvs", [128, TB], F32) for _ in range(4)]
            silb = [Buf() for _ in range(4)]
            silh = [self.sbt(st, "cvsh", [128, TB], BF16) for _ in range(2)]
            silhb = [Buf(), Buf()]
            one = self.sbt(st, "onec", [128, 2], F32)
            oneb = Buf()
            mk.op(mk.pool, lambda: nc.gpsimd.memset(one[:, 0:1], 128 ** -0.5), w=[oneb])
            mk.op(mk.pool, lambda: nc.gpsimd.memset(one[:, 1:2], 1.0), w=[oneb])
            hn = self.epi_fm_headnorm(st, None, 0, TB, one, oneb, mean=False, eps=EPS)
            cnt = [0]

            def conv(grp, tb):
                (ps, psb, col, cw), = grp
                c = col // 128
                j = cnt[0] % 2
                cnt[0] += 1
                T, Tb = tmp[j], tmpb[j]
                self.copy(mk.act, T[:, 3:TB + 3], ps, r=[psb], w=[Tb])

                def stage2():
                    mk.op(mk.dve, lambda: nc.vector.tensor_copy(out=T[:, 0:3], in_=HL[:, c, :]), r=[hlb], w=[Tb])
                    mk.op(mk.dve, lambda: nc.vector.tensor_copy(out=HL[:, c, :], in_=T[:, TB:TB + 3]), r=[Tb], w=[hlb])
                    A, Ab = accs[j], accb[j]
                    mk.op(mk.dve, lambda: nc.vector.tensor_scalar(out=A[:], in0=T[:, 3:TB + 3], scalar1=CW[:, c, 3:4], scalar2=None,
                                                                  op0=ALU.mult), r=[Tb, cwb], w=[Ab])
                    for k in (2, 1, 0):
                        mk.op(mk.dve, lambda: nc.vector.scalar_tensor_tensor(out=A[:], in0=T[:, k:k + TB], scalar=CW[:, c, k:k + 1], in1=A[:],
                                                                             op0=ALU.mult, op1=ALU.add), r=[Tb, cwb, Ab], w=[Ab])

                    def stage3():
                        if c < 32:
                            j4 = c % 4
                            mk.op(mk.act, lambda: nc.scalar.activation(out=sil[j4][:], in_=A[:], func=AF.Silu), r=[Ab], w=[silb[j4]])
                            dst = qT if c < 16 else kT
                            r0 = (c % 16) * 128
                            hn([(sil[j4][:], silb[j4], 0, 128)], tb, gi=(0 if c < 16 else 1), deep=True,
                               store=lambda t, tb_: mk.dma(mk.pool, dst[r0:r0 + 128, tb * TB:(tb + 1) * TB], t[:], r=[tb_]))
                        else:
                            mk.op(mk.act, lambda: nc.scalar.activation(out=silh[j][:], in_=A[:], func=AF.Silu), r=[Ab], w=[silhb[j]])
                            r0 = (c - 32) * 128
                            mk.dma(mk.pool, vT[r0:r0 + 128, tb * TB:(tb + 1) * TB], silh[j][:], r=[silhb[j]])
                    self.defer(stage3)
                self.defer(stage2)
            return [dict(W=w_in, c0=0, c1=8192, layout="FM", epi=conv),
                    dict(W=w_in, c0=8192, c1=12288, layout="TM", epi=self.epi_tm_store(st, zs, -8192, TB, func=AF.Silu)),
                    dict(W=w_in, c0=12288, c1=12352, layout="TM", epi=self.epi_tm_store(st, ba, -12288, TB, dt=F32, width=64))]
        self.linear("gdn", h, S, D, ln, jobs)

    def gdn_gates(self, a_log, dt_bias):
        nc, mk = self.nc, self.mk
        S = self.S
        ba = self.dram["gdn_ba"]
        gt = self.scr("gdn_gt", [S, 5, 32], F32)
        X = self.scr("gdn_X", [S, 3, 64], F32)
        if "gdn_tri" not in self.dram:
            m = np.arange(128)
            self.const_dram("gdn_tri", (m[:, None] <= m[None, :]).astype(np.float32), F32)
            l = np.zeros((128, 128), np.float32)
            l[127, :] = 1.0
            self.const_dram("gdn_l127", l, F32)
        with ExitStack() as st:
            TRI = self.sbt(st, "gTRI", [128, 128], F32)
            L127 = self.sbt(st, "gL127", [128, 128], F32)
            AL = self.sbt(st, "gAL", [128, 32], F32)
            DTB = self.sbt(st, "gDTB", [128, 32], F32)
            cb = Buf()
            mk.dma(mk.sp, TRI[:], self.dram["gdn_tri"], w=[cb])
            mk.dma(mk.sp, L127[:], self.dram["gdn_l127"], w=[cb])
            mk.dma(mk.sp, AL[:], a_log.partition_broadcast(128), w=[cb])
            mk.dma(mk.sp, DTB[:], dt_bias.partition_broadcast(128), w=[cb])
            mk.op(mk.act, lambda: nc.scalar.activation(out=AL[:], in_=AL[:], func=AF.Exp), r=[cb], w=[cb])
            inn = [self.sbt(st, "gin", [128, 64], F32) for _ in range(2)]
            innb = [Buf(), Buf()]
            W = [self.sbt(st, "gW", [128, 8, 32], F32) for _ in range(2)]
            Wb = [Buf(), Buf()]
            GT = [self.sbt(st, "gGT", [128, 5, 32], F32) for _ in range(2)]
            GTb = [Buf(), Buf()]
            XX = [self.sbt(st, "gXX", [128, 3, 64], F32) for _ in range(2)]
            XXb = [Buf(), Buf()]
            for n in range(S // 128):
                j = n % 2
                rows = slice(n * 128, (n + 1) * 128)
                I, Ib, w, wb, G, Gb, Xt, Xb = inn[j], innb[j], W[j], Wb[j], GT[j], GTb[j], XX[j], XXb[j]
                mk.dma(mk.sp, I[:], ba[rows, :], w=[Ib])
                mk.op(mk.act, lambda: nc.scalar.activation(out=w[:, 0, :], in_=I[:, 0:32], func=AF.Exp, scale=-1.0), r=[Ib], w=[wb])
                mk.op(mk.dve, lambda: nc.vector.tensor_scalar(out=w[:, 0, :], in0=w[:, 0, :], scalar1=1.0, scalar2=None, op0=ALU.add),
                      r=[wb], w=[wb])
                mk.op(mk.dve, lambda: nc.vector.reciprocal(out=G[:, 0, :], in_=w[:, 0, :]), r=[wb], w=[Gb])
                mk.op(mk.act, lambda: nc.scalar.activation(out=w[:, 1, :], in_=w[:, 0, :], func=AF.Ln), r=[wb], w=[wb])
                mk.op(mk.dve, lambda: nc.vector.tensor_tensor(out=w[:, 2, :], in0=I[:, 32:64], in1=DTB[:], op=ALU.add), r=[Ib, cb], w=[wb])
                mk.op(mk.act, lambda: nc.scalar.activation(out=w[:, 2, :], in_=w[:, 2, :], func=AF.Exp), r=[wb], w=[wb])
                mk.op(mk.dve, lambda: nc.vector.tensor_scalar(out=w[:, 2, :], in0=w[:, 2, :], scalar1=1.0, scalar2=None, op0=ALU.add),
                      r=[wb], w=[wb])
                mk.op(mk.act, lambda: nc.scalar.activation(out=w[:, 2, :], in_=w[:, 2, :], func=AF.Ln), r=[wb], w=[wb])
                mk.op(mk.dve, lambda: nc.vector.scalar_tensor_tensor(out=w[:, 3, :], in0=w[:, 2, :], scalar=-1.0, in1=AL[:],
                                                                     op0=ALU.mult, op1=ALU.mult), r=[wb, cb], w=[wb])
                bk, bkb = self.ps()
                mk.op(mk.pe, lambda: nc.tensor.matmul(bk[:, 0:32], lhsT=TRI[:], rhs=w[:, 3, :], start=True, stop=True), r=[cb, wb], w=[bkb])
                self.copy(mk.dve, w[:, 4, :], bk[:, 0:32], r=[bkb], w=[wb])
                bk2, bk2b = self.ps()
                mk.op(mk.pe, lambda: nc.tensor.matmul(bk2[:, 0:32], lhsT=L127[:], rhs=w[:, 4, :], start=True, stop=True), r=[cb, wb], w=[bk2b])
                self.copy(mk.dve, w[:, 5, :], bk2[:, 0:32], r=[bk2b], w=[wb])
                mk.op(mk.act, lambda: nc.scalar.activation(out=G[:, 1, :], in_=w[:, 4, :], func=AF.Exp), r=[wb], w=[Gb])
                mk.op(mk.dve, lambda: nc.vector.tensor_tensor(out=G[:, 2, :], in0=G[:, 0, :], in1=G[:, 1, :], op=ALU.mult), r=[Gb], w=[Gb])
                mk.op(mk.dve, lambda: nc.vector.tensor_tensor(out=w[:, 6, :], in0=w[:, 5, :], in1=w[:, 4, :], op=ALU.subtract), r=[wb], w=[wb])
                mk.op(mk.act, lambda: nc.scalar.activation(out=G[:, 3, :], in_=w[:, 6, :], func=AF.Exp), r=[wb], w=[Gb])
                mk.op(mk.act, lambda: nc.scalar.activation(out=G[:, 4, :], in_=w[:, 5, :], func=AF.Exp), r=[wb], w=[Gb])
                mk.dma(mk.pool, gt[rows], G[:], r=[Gb])
                mk.op(mk.pool, lambda: nc.gpsimd.memset(Xt[:], 1.0), w=[Xb])
                mk.op(mk.dve, lambda: nc.vector.tensor_scalar(out=Xt[:, 0, 0:32], in0=w[:, 4, :], scalar1=-1.0, scalar2=None, op0=ALU.mult),
                      r=[wb], w=[Xb])
                self.copy(mk.dve, Xt[:, 1, 32:64], w[:, 4, :], r=[wb], w=[Xb])
                mk.op(mk.dve, lambda: nc.vector.tensor_tensor(out=Xt[:, 2, 32:64], in0=w[:, 4, :], in1=w[:, 1, :], op=ALU.subtract),
                      r=[wb], w=[Xb])
                mk.dma(mk.pool, X[rows], Xt[:], r=[Xb])
            mk.barrier()

    def gdn_chunks(self, norm_gain, gdn_o):
        nc, mk = self.nc, self.mk
        S = self.S
        NCH = S // 128
        F32R = F32
        qT, kT, vT, zs, gt, X = (self.dram[k] for k in ("gdn_qT", "gdn_kT", "gdn_vT", "gdn_zs", "gdn_gt", "gdn_X"))
        if "gdn_mstrict" not in self.dram:
            m = np.arange(128)
            ms = np.where(m[None, :] > m[:, None], 0.0, -1e4).astype(np.float32)
            mi = np.where(m[None, :] >= m[:, None], 0.0, -1e4).astype(np.float32)
            self.const_dram("gdn_mstrict", np.tile(ms[:, None, :], (1, 4, 1)), F32)
            self.const_dram("gdn_mincl", np.tile(mi[:, None, :], (1, 4, 1)), F32)
            oh = np.zeros((64, 32), np.float32)
            for k in range(64):
                oh[k, k % 32] = 1.0
            self.const_dram("gdn_oh", oh, F32)
            self.const_dram("gdn_i4", np.tile(np.eye(128, dtype=np.float32)[:, None, :], (1, 4, 1)), F32)
        with ExitStack() as st:
            MS = self.sbt(st, "dMS", [128, 4, 128], F32)
            MI = self.sbt(st, "dMI", [128, 4, 128], F32)
            OH = self.sbt(st, "dOH", [64, 32], F32)
            I4 = self.sbt(st, "dI4", [128, 4, 128], F32)
            NG = self.sbt(st, "dNG", [128, 128], F32)
            cb = Buf()
            mk.dma(mk.sp, MS[:], self.dram["gdn_mstrict"], w=[cb])
            mk.dma(mk.sp, MI[:], self.dram["gdn_mincl"], w=[cb])
            mk.dma(mk.sp, OH[:], self.dram["gdn_oh"], w=[cb])
            mk.dma(mk.sp, I4[:], self.dram["gdn_i4"], w=[cb])
            mk.dma(mk.sp, NG[:], norm_gain.partition_broadcast(128), w=[cb])
            Sf = self.sbt(st, "dSf", [128, 32, 128], F32)
            Sb = self.sbt(st, "dSb", [128, 32, 128], BF16)
            Sfb = [Buf() for _ in range(8)]
            Sbb = [Buf() for _ in range(8)]
            mk.op(mk.pool, lambda: nc.gpsimd.memset(Sf[:], 0.0), w=Sfb)
            mk.op(mk.pool, lambda: nc.gpsimd.memset(Sb[:], 0.0), w=Sbb)
            GTt = [self.sbt(st, "dGT", [128, 5, 32], F32) for _ in range(2)]
            XXt = [self.sbt(st, "dXX", [128, 3, 64], F32) for _ in range(2)]
            XT = [self.sbt(st, "dXT", [64, 3, 128], F32) for _ in range(2)]
            LA = [self.sbt(st, "dLA", [64, 32, 128], F32) for _ in range(2)]
            chb = [Buf(), Buf()]
            NS = 4

            def two(name, shape, dt):
                return [self.sbt(st, name, shape, dt) for _ in range(NS)], [Buf() for _ in range(NS)]
            kTt, kTb = two("dkT", [128, 2, 128], BF16)
            qTt, qTb = two("dqT", [128, 2, 128], BF16)
            vTt, vTb = two("dvT", [128, 4, 128], BF16)
            zst, zsb = two("dzs", [128, 512], BF16)
            vbt, vbb = two("dvb", [128, 4, 128], BF16)
            kbg, kbgb = two("dkbg", [128, 4, 128], BF16)
            kdc, kdcb = two("dkdc", [128, 4, 128], BF16)
            EYt, EYb = two("dEY", [128, 4, 128], F32)
            EAt, EAb = two("dEA", [128, 4, 128], F32)
            YPt = [[self.sbt(st, "dYP", [128, 4, 2, 128], F32R) for _ in range(2)] for _ in range(NS)]
            YTt = [[self.sbt(st, "dYT", [128, 5, 128], F32R) for _ in range(2)] for _ in range(NS)]
            YPb = [[Buf(), Buf()] for _ in range(NS)]
            YTb = [[Buf(), Buf()] for _ in range(NS)]
            for sl in range(NS):
                for pp in range(2):
                    self.copy(mk.dve, YTt[sl][pp][:], Sf[:, 0:5, :], r=[Sfb[0]], w=[YTb[sl][pp]])
            atT, atTb = two("datT", [128, 4, 128], BF16)
            TT, TTb = two("dTT", [128, 4, 128], BF16)
            usb, usbb = two("du", [128, 4, 128], F32)
            wT, wTb = two("dwT", [128, 4, 128], BF16)
            vnw, vnwb = two("dvn", [128, 4, 128], BF16)
            osb, osbb = two("do", [128, 4, 128], F32)
            junk, junkb = two("djunk", [128, 4, 128], F32)
            stt_, sttb = two("dst", [128, 8], F32)
            yo, yob = two("dyo", [128, 512], BF16)

            def f4(t):
                return t[:].rearrange("p h c -> p (h c)")

            class Ctx:
                pass

            def ph_load(c):
                j, hg, rows = c.j, c.hg, c.rows
                mk.dma(mk.sp, kTt[j][:], kT[hg * 256:(hg + 1) * 256, rows].rearrange("(h p) s -> p h s", p=128), w=[kTb[j]])
                mk.dma(mk.sp, qTt[j][:], qT[hg * 256:(hg + 1) * 256, rows].rearrange("(h p) s -> p h s", p=128), w=[qTb[j]])
                mk.dma(mk.sp, vTt[j][:], vT[hg * 512:(hg + 1) * 512, rows].rearrange("(h p) s -> p h s", p=128), w=[vTb[j]])
                mk.dma(mk.sp, zst[j][:], zs[rows, hg * 512:(hg + 1) * 512], w=[zsb[j]])
                bk, bkb = self.ps()
                bv = bk[:].bitcast(BF16)
                for qh in range(2):
                    mk.op(mk.pe, lambda: nc.tensor.transpose(out=bv[:, qh * 128:(qh + 1) * 128], in_=kTt[j][:, qh, :],
                                                             identity=self.ident_bf[0][:]), r=[kTb[j], self.ident_bf[1]], w=[bkb])
                for hh in range(4):
                    mk.op(mk.pe, lambda: nc.tensor.transpose(out=bv[:, (2 + hh) * 128:(3 + hh) * 128], in_=vTt[j][:, hh, :],
                                                             identity=self.ident_bf[0][:]), r=[vTb[j], self.ident_bf[1]], w=[bkb])
                G_ = c.GT_
                H0 = c.H0
                kv4 = bv[:, 0:256].rearrange("p (h d) -> p h d", d=128).unsqueeze(2).to_broadcast([128, 2, 2, 128])
                vv4 = bv[:, 256:768].rearrange("p (h d) -> p h d", d=128)

                def sc(i):
                    return G_[:, i, H0:H0 + 4]
                mk.op(mk.dve, lambda: nc.vector.tensor_tensor(out=vbt[j][:], in0=vv4, in1=sc(0).unsqueeze(2).to_broadcast([128, 4, 128]),
                                                              op=ALU.mult), r=[bkb, c.cbb], w=[vbb[j]])
                mk.op(mk.dve, lambda: nc.vector.tensor_tensor(out=kbg[j][:].rearrange("p (a b) d -> p a b d", b=2), in0=kv4,
                                                              in1=sc(2).rearrange("p (a b) -> p a b", b=2).unsqueeze(3).to_broadcast([128, 2, 2, 128]),
                                                              op=ALU.mult), r=[bkb, c.cbb], w=[kbgb[j]])
                mk.op(mk.dve, lambda: nc.vector.tensor_tensor(out=kdc[j][:].rearrange("p (a b) d -> p a b d", b=2), in0=kv4,
                                                              in1=sc(3).rearrange("p (a b) -> p a b", b=2).unsqueeze(3).to_broadcast([128, 2, 2, 128]),
                                                              op=ALU.mult), r=[bkb, c.cbb], w=[kdcb[j]])

            def ph_gram1(c):
                j = c.j
                c.bDY, c.bDYb = self.ps()
                c.bDA, c.bDAb = self.ps()
                for hh in range(4):
                    mk.op(mk.pe, lambda: nc.tensor.matmul(c.bDY[:, hh * 128:(hh + 1) * 128], lhsT=c.LA_[:, c.H0 + hh, :], rhs=c.XT_[:, 2, :],
                                                          start=True, stop=True), r=[c.cbb], w=[c.bDYb])
                    mk.op(mk.pe, lambda: nc.tensor.matmul(c.bDA[:, hh * 128:(hh + 1) * 128], lhsT=c.LA_[:, c.H0 + hh, :], rhs=c.XT_[:, 1, :],
                                                          start=True, stop=True), r=[c.cbb], w=[c.bDAb])

            def ph_gram2(c):
                j = c.j
                EY, EA = EYt[j], EAt[j]
                mk.op(mk.dve, lambda: nc.vector.scalar_tensor_tensor(out=f4(EY), in0=c.bDY[:, :], scalar=0.0, in1=f4(MS),
                                                                     op0=ALU.min, op1=ALU.add), r=[c.bDYb, cb], w=[EYb[j]])
                mk.op(mk.dve, lambda: nc.vector.scalar_tensor_tensor(out=f4(EA), in0=c.bDA[:, :], scalar=0.0, in1=f4(MI),
                                                                     op0=ALU.min, op1=ALU.add), r=[c.bDAb, cb], w=[EAb[j]])

            def ph_gram3(c):
                j = c.j
                EY, EA = EYt[j], EAt[j]
                mk.op(mk.act, lambda: nc.scalar.activation(out=f4(EY), in_=f4(EY), func=AF.Exp), r=[EYb[j]], w=[EYb[j]])
                mk.op(mk.act, lambda: nc.scalar.activation(out=f4(EA), in_=f4(EA), func=AF.Exp), r=[EAb[j]], w=[EAb[j]])
                self.copy(mk.act, YPt[j][0][:, :, 1, :], I4[:], r=[cb], w=[YPb[j][0]])
                c.bG, c.bGb = self.ps()
                bG, bGb = c.bG, c.bGb
                for qh in range(2):
                    mk.op(mk.pe, lambda: nc.tensor.matmul(bG[:, qh * 128:(qh + 1) * 128], lhsT=kTt[j][:, qh, :], rhs=kTt[j][:, qh, :],
                                                          start=True, stop=True), r=[kTb[j]], w=[bGb])
                    mk.op(mk.pe, lambda: nc.tensor.matmul(bG[:, (2 + qh) * 128:(3 + qh) * 128], lhsT=kTt[j][:, qh, :], rhs=qTt[j][:, qh, :],
                                                          start=True, stop=True), r=[kTb[j], qTb[j]], w=[bGb])

            def ph_gram4(c):
                j = c.j
                EY, EA = EYt[j], EAt[j]
                bG, bGb = c.bG, c.bGb
                YP0 = YPt[j][0]
                for qh in range(2):
                    Gv = bG[:, qh * 128:(qh + 1) * 128].unsqueeze(1).to_broadcast([128, 2, 128])
                    mk.op(mk.dve, lambda: nc.vector.scalar_tensor_tensor(out=YP0[:, 2 * qh:2 * qh + 2, 0, :], in0=Gv, scalar=-1.0,
                                                                         in1=EY[:, 2 * qh:2 * qh + 2, :], op0=ALU.mult, op1=ALU.mult),
                          r=[bGb, EYb[j]], w=[YPb[j][0]])
                    Av = bG[:, (2 + qh) * 128:(3 + qh) * 128].unsqueeze(1).to_broadcast([128, 2, 128])
                    mk.op(mk.dve, lambda: nc.vector.tensor_tensor(out=atT[j][:, 2 * qh:2 * qh + 2, :], in0=Av, in1=EA[:, 2 * qh:2 * qh + 2, :],
                                                                  op=ALU.mult), r=[bGb, EAb[j]], w=[atTb[j]])

            def ph_gram5(c):
                j = c.j
                YP0 = YPt[j][0]
                bk, bkb = self.ps()
                for hh in range(4):
                    mk.op(mk.pe, lambda: nc.tensor.transpose(out=bk[:, hh * 128:(hh + 1) * 128], in_=YP0[:, hh, 0, :].bitcast(F32), identity=self.ident_f[0][:]),
                          r=[YPb[j][0], self.ident_f[1]], w=[bkb])
                self.copy(mk.act, YTt[j][0][:, 0:4, :].rearrange("p h c -> p (h c)"), bk[:, :], r=[bkb], w=[YTb[j][0]])
                c.cur = 0

            def ph_level(c, lvl):
                j = c.j
                cur, nxt = c.cur, 1 - c.cur
                YPc, YTc, YPn, YTn = YPt[j][cur], YTt[j][cur], YPt[j][nxt], YTt[j][nxt]
                first, last = (lvl == 0), (lvl == 6)
                b1 = [self.ps(), self.ps()]
                for hh in range(4):
                    bk, bkb = b1[hh // 2]
                    mk.op(mk.pe, lambda: nc.tensor.matmul(bk[:, (hh % 2) * 256:(hh % 2) * 256 + 256], lhsT=YTc[:, hh, :],
                                                          rhs=YPc[:, hh, :, :].rearrange("p a c -> p (a c)"),
                                                          start=True, stop=True), r=[YPb[j][cur], YTb[j][cur]], w=[bkb])
                if not last:
                    b2, b2b = self.ps()
                    for hh in range(4):
                        mk.op(mk.pe, lambda: nc.tensor.matmul(b2[:, hh * 128:(hh + 1) * 128], lhsT=YPc[:, hh, 0, :], rhs=YTc[:, hh, :],
                                                              start=True, stop=True), r=[YPb[j][cur], YTb[j][cur]], w=[b2b])
                    self.copy(mk.act, YTn[:, 0:4, :].rearrange("p h c -> p (h c)"), b2[:, :], r=[b2b], w=[YTb[j][nxt]])
                for half in range(2):
                    bk, bkb = b1[half]
                    v = bk[:, :].rearrange("p (h a c) -> p h a c", a=2, c=128)
                    hs = slice(2 * half, 2 * half + 2)
                    if not last:
                        self.copy(mk.dve, YPn[:, hs, 0, :], v[:, :, 0, :], r=[bkb], w=[YPb[j][nxt]])
                    if not last:
                        mk.op(mk.dve, lambda: nc.vector.tensor_tensor(out=YPn[:, hs, 1, :], in0=v[:, :, 1, :], in1=YPc[:, hs, 1, :].bitcast(F32), op=ALU.add),
                              r=[bkb, YPb[j][cur]], w=[YPb[j][nxt]])
                    else:
                        mk.op(mk.dve, lambda: nc.vector.tensor_tensor(out=TT[j][:, hs, :], in0=v[:, :, 1, :], in1=YPc[:, hs, 1, :].bitcast(F32), op=ALU.add),
                              r=[bkb, YPb[j][cur]], w=[TTb[j]])
                c.cur = nxt

            def ph_uw(c):
                j = c.j
                bU, bUb = self.ps()
                bW, bWb = self.ps()
                for hh in range(4):
                    mk.op(mk.pe, lambda: nc.tensor.matmul(bU[:, hh * 128:(hh + 1) * 128], lhsT=TT[j][:, hh, :], rhs=vbt[j][:, hh, :],
                                                          start=True, stop=True), r=[TTb[j], vbb[j]], w=[bUb])
                    mk.op(mk.pe, lambda: nc.tensor.matmul(bW[:, hh * 128:(hh + 1) * 128], lhsT=kbg[j][:, hh, :], rhs=TT[j][:, hh, :],
                                                          start=True, stop=True), r=[TTb[j], kbgb[j]], w=[bWb])
                self.copy(mk.act, f4(usb[j]), bU[:, :], r=[bUb], w=[usbb[j]])
                self.copy(mk.dve, f4(wT[j]), bW[:, :], r=[bWb], w=[wTb[j]])

            def ph_state1(c):
                j, hg, H0 = c.j, c.hg, c.H0
                bWS, bWSb = self.ps()
                for hh in range(4):
                    mk.op(mk.pe, lambda: nc.tensor.matmul(bWS[:, hh * 128:(hh + 1) * 128], lhsT=wT[j][:, hh, :], rhs=Sb[:, H0 + hh, :],
                                                          start=True, stop=True), r=[wTb[j], Sbb[hg]], w=[bWSb])
                mk.op(mk.dve, lambda: nc.vector.tensor_tensor(out=f4(vnw[j]), in0=f4(usb[j]), in1=bWS[:, :], op=ALU.subtract),
                      r=[usbb[j], bWSb], w=[vnwb[j]])

            def ph_s2a(c):
                j, hg, H0 = c.j, c.hg, c.H0
                c.bAV, c.bAVb = self.ps()
                c.bDS, c.bDSb = self.ps()
                for hh in range(4):
                    mk.op(mk.pe, lambda: nc.tensor.matmul(c.bDS[:, hh * 128:(hh + 1) * 128], lhsT=kdc[j][:, hh, :], rhs=vnw[j][:, hh, :],
                                                          start=True, stop=True), r=[kdcb[j], vnwb[j]], w=[c.bDSb])
                for hh in range(4):
                    mk.op(mk.pe, lambda: nc.tensor.matmul(c.bAV[:, hh * 128:(hh + 1) * 128], lhsT=atT[j][:, hh, :], rhs=vnw[j][:, hh, :],
                                                          start=True, stop=True), r=[atTb[j], vnwb[j]], w=[c.bAVb])

            def ph_s2b(c):
                j, hg, H0 = c.j, c.hg, c.H0
                G_ = c.GT_
                Sv = Sf[:, H0:H0 + 4, :]
                mk.op(mk.dve, lambda: nc.vector.tensor_tensor(out=Sv, in0=Sv, in1=G_[:, 4, H0:H0 + 4].unsqueeze(2).to_broadcast([128, 4, 128]),
                                                              op=ALU.mult), r=[c.cbb, Sfb[hg]], w=[Sfb[hg]])
                mk.op(mk.dve, lambda: nc.vector.tensor_tensor(out=Sv, in0=Sv, in1=c.bDS[:, :].rearrange("p (h c) -> p h c", c=128), op=ALU.add),
                      r=[c.bDSb, Sfb[hg]], w=[Sfb[hg]])
                O, Ob = osb[j], osbb[j]
                self.copy(mk.act, f4(O), c.bAV[:, :], r=[c.bAVb], w=[Ob])
                c.bQS, c.bQSb = self.ps()
                for hh in range(4):
                    mk.op(mk.pe, lambda: nc.tensor.matmul(c.bQS[:, hh * 128:(hh + 1) * 128], lhsT=qTt[j][:, hh // 2, :], rhs=Sb[:, H0 + hh, :],
                                                          start=True, stop=True), r=[qTb[j], Sbb[hg]], w=[c.bQSb])

            def ph_s2c(c):
                j, hg, H0 = c.j, c.hg, c.H0
                G_ = c.GT_
                O, Ob = osb[j], osbb[j]
                self.copy(mk.act, Sb[:, H0:H0 + 4, :], Sf[:, H0:H0 + 4, :], r=[Sfb[hg]], w=[Sbb[hg]])
                J, Jb = junk[j], junkb[j]
                mk.op(mk.dve, lambda: nc.vector.tensor_tensor(out=J[:], in0=c.bQS[:, :].rearrange("p (h c) -> p h c", c=128),
                                                              in1=G_[:, 1, H0:H0 + 4].unsqueeze(2).to_broadcast([128, 4, 128]), op=ALU.mult),
                      r=[c.bQSb, c.cbb], w=[Jb])
                mk.op(mk.dve, lambda: nc.vector.tensor_tensor(out=O[:], in0=O[:], in1=J[:], op=ALU.add), r=[Ob, Jb], w=[Ob])

            def ph_s2d(c):
                j = c.j
                O, Ob = osb[j], osbb[j]
                mk.op(mk.act, lambda: nc.scalar.activation(out=junk[j][:], in_=O[:], func=AF.Square, scale=128 ** -0.5), r=[Ob], w=[junkb[j]])

            def ph_s2e(c):
                j = c.j
                ST, STb = stt_[j], sttb[j]
                mk.op(mk.dve, lambda: nc.vector.tensor_reduce(out=ST[:, 0:4], in_=junk[j][:], axis=AX.X, op=ALU.add), r=[junkb[j]], w=[STb])

            def ph_s2f(c):
                j = c.j
                ST, STb = stt_[j], sttb[j]
                self.rsqrt(ST[:, 4:8], ST[:, 0:4], EPS, [STb], [STb])

            def ph_s2g(c):
                j, hg, rows = c.j, c.hg, c.rows
                O, Ob = osb[j], osbb[j]
                ST, STb = stt_[j], sttb[j]
                mk.op(mk.dve, lambda: nc.vector.tensor_tensor(out=O[:], in0=O[:], in1=ST[:, 4:8].unsqueeze(2).to_broadcast([128, 4, 128]), op=ALU.mult),
                      r=[Ob, STb], w=[Ob])
                mk.op(mk.dve, lambda: nc.vector.tensor_tensor(out=O[:], in0=O[:], in1=NG[:].unsqueeze(1).to_broadcast([128, 4, 128]), op=ALU.mult),
                      r=[Ob, cb], w=[Ob])
                mk.op(mk.pool, lambda: nc.gpsimd.tensor_tensor(out=yo[j][:], in0=f4(O), in1=zst[j][:], op=ALU.mult),
                      r=[Ob, zsb[j]], w=[yob[j]])
                mk.dma(mk.pool, gdn_o[rows, hg * 512:(hg + 1) * 512], yo[j][:], r=[yob[j]])

            phases = ([ph_load, ph_gram1, ph_gram2, ph_gram3, ph_gram4, ph_gram5] + [(lambda c, l=l: ph_level(c, l)) for l in range(7)]
                      + [ph_uw, ph_state1, ph_s2a, ph_s2b, ph_s2c, ph_s2d, ph_s2e, ph_s2f, ph_s2g])
            for n in range(NCH):
                cj = n % 2
                rows = slice(n * 128, (n + 1) * 128)
                GT_, XX_, XT_, LA_, cbb = GTt[cj], XXt[cj], XT[cj], LA[cj], chb[cj]
                mk.dma(mk.sp, GT_[:], gt[rows], w=[cbb])
                mk.dma(mk.sp, XX_[:], X[rows], w=[cbb])
                bk, bkb = self.ps()
                for i in range(3):
                    mk.op(mk.pe, lambda: nc.tensor.transpose(out=bk[:64, i * 128:(i + 1) * 128], in_=XX_[:, i, :], identity=self.ident_f[0][:]),
                          r=[cbb, self.ident_f[1]], w=[bkb])
                self.copy(mk.dve, XT_[:], bk[:64, 0:384].rearrange("p (i c) -> p i c", c=128), r=[bkb], w=[cbb])
                mk.op(mk.dve, lambda: nc.vector.tensor_tensor(out=LA_[:], in0=XT_[:, 0, :].unsqueeze(1).to_broadcast([64, 32, 128]),
                                                              in1=OH[:].unsqueeze(2).to_broadcast([64, 32, 128]), op=ALU.mult),
                      r=[cbb, cb], w=[cbb])
                for hp in range(0, 8, NS):
                    ctxs = []
                    for k in range(min(NS, 8 - hp)):
                        c = Ctx()
                        c.j, c.hg, c.H0, c.rows = k, hp + k, (hp + k) * 4, rows
                        c.GT_, c.XT_, c.LA_, c.cbb = GT_, XT_, LA_, cbb
                        ctxs.append(c)
                    for ph in phases:
                        for c in ctxs:
                            ph(c)
            mk.barrier()


for _n, _f in list(ProgGDN.__dict__.items()):
    if callable(_f):
        setattr(Prog, _n, _f)


INPUT_SHAPES = dict(
    rel_bias=(32, 8), ln_mix=(4, D), ln_mem=(4, D), ln_ffn=(4, D), hyb_w_in=(2, D, HYB_IN), hyb_w_out=(2, 2048, D),
    nsa_q_gain=(2, 128), nsa_k_gain=(2, 3, 128), nsa_cmp_pos=(2, 2, 32, 128), nsa_cmp_w1=(2, 2, 4096, 256),
    nsa_cmp_w2=(2, 2, 256, 128), ret_gn_gain=(2, 1024), gdn_w_in=(2, D, GDN_IN), gdn_conv_w=(2, 4, 8192),
    gdn_a_log=(2, 32), gdn_dt_bias=(2, 32), gdn_norm_gain=(2, 128), gdn_w_out=(2, 4096, D), xa_wq=(4, D, 512),
    xa_wkv=(4, D, 1024), xa_q_gain=(4, 128), xa_k_gain=(4, 128), xa_mem_gain=(4, D), xa_wo=(4, 512, D),
    ffn_w_in=(4, D, 2 * FFN_H), ffn_w_out=(4, FFN_H, D))
STAGE_ORDER = ("hyb_w_in", "gdn_w_in", "nsa_cmp_w1", "nsa_cmp_w2", "hyb_w_out", "gdn_w_out", "xa_wkv", "xa_wq", "xa_wo",
               "ffn_w_in", "ffn_w_out")
CAST = ("hyb_w_in", "hyb_w_out", "nsa_cmp_w1", "nsa_cmp_w2", "gdn_w_in", "gdn_w_out", "xa_wq", "xa_wkv", "xa_wo",
        "ffn_w_in", "ffn_w_out")


def build_program(S=4096, M=256, depth=4, dbg_layers=False):
    P = Prog(S=S, M=M, dbg_out=())
    nc, mk = P.nc, P.mk
    x = P.inp("x", [S, D])
    mem = P.inp("mem", [M, D])
    h = nc.dram_tensor("out", [S, D], F32, kind="ExternalOutput").ap()
    A = {}
    for k, shp in INPUT_SHAPES.items():
        if k != "rel_bias":
            A[k] = P.inp(k, shp)
    toep_i, cmpx_i, OFF = nsa_index_tables(S)
    rel_toep = P.inp("rel_toep", [128, 10, 8, 128])
    rel_cmpx = P.inp("rel_cmpx", [cmpx_i.shape[0], 8, 128])
    P.idx = (toep_i, cmpx_i)
    W = {}
    P.wreg = {}
    first, later = [], []
    def layer_of(k, i):
        if k.startswith("hyb") or k.startswith("nsa"):
            return 2 * i
        if k.startswith("gdn"):
            return 2 * i + 1
        return i
    for k in CAST:
        src = A[k]
        dst = P.scr(k + "_bf", list(src.shape), BF16)
        W[k] = dst
        nl = src.shape[0]
        per = int(np.prod(src.shape[1:]))
        bufs = [Buf(f"{k}{i}") for i in range(nl)]
        P.wreg[dst.tensor.name] = (per, bufs)
        for i in range(nl):
            s2, d2 = src[i], dst[i]
            if len(s2.shape) == 3:
                s2, d2 = s2.rearrange("a b c -> (a b) c"), d2.rearrange("a b c -> (a b) c")
            R_, C_ = s2.shape
            rp = max(1, min(R_, (2 << 20) // C_))
            for r0 in range(0, R_, rp):
                r1 = min(R_, r0 + rp)
                prio = layer_of(k, i) + STAGE_ORDER.index(k) / 100.0
                item = (prio, d2[r0:r1, :], s2[r0:r1, :], bufs[i])
                (first if (k == "hyb_w_in" and i == 0) else later).append(item)
    for _, d_, s_, b_ in first:
        mk.dma(mk.pool, d_, s_, w=[b_])
    later.sort(key=lambda t: t[0])
    P.cast_pending = list(later)
    mk.on_barrier = lambda: P.emit_casts(7)
    mk.barrier()
    mix_in = P.scr("mix_in", [S, 2048])
    gdn_o = P.scr("gdn_o", [S, 4096])
    dbg = []
    for layer in range(depth):
        if layer > 0:
            P.emit_casts_upto(layer)
        if layer % 2 == 0:
            e = layer // 2
            hsrc = x if layer == 0 else h
            P.hyb_in(hsrc, A["ln_mix"][layer], W["hyb_w_in"][e], A["nsa_q_gain"][e], A["nsa_k_gain"][e])
            P.nsa_cmp(A["nsa_cmp_pos"][e], W["nsa_cmp_w1"][e], W["nsa_cmp_w2"][e], A["nsa_k_gain"][e][0])
            P.nsa_attn(rel_toep, rel_cmpx, OFF, mix_in)
            P.retention(A["ret_gn_gain"][e], mix_in)

            def jobs_o(st, TB, e=e, hsrc=hsrc):
                return [dict(W=W["hyb_w_out"][e], c0=0, c1=D, layout="TM", epi=P.epi_residual(st, h, TB, src=hsrc))]
            P.linear("hybo", mix_in, S, 2048, None, jobs_o, src_bf16=True)
        else:
            o = layer // 2
            P.gdn_in(h, A["ln_mix"][layer], W["gdn_w_in"][o], A["gdn_conv_w"][o])
            P.gdn_gates(A["gdn_a_log"][o], A["gdn_dt_bias"][o])
            P.gdn_chunks(A["gdn_norm_gain"][o], gdn_o)

            def jobs_g(st, TB, o=o):
                return [dict(W=W["gdn_w_out"][o], c0=0, c1=D, layout="TM", epi=P.epi_residual(st, h, TB, width=256))]
            P.linear("gdno", gdn_o, S, 4096, None, jobs_g, src_bf16=True)
        P.xa(h, mem, A["ln_mem"][layer], A["xa_mem_gain"][layer], W["xa_wq"][layer], W["xa_wkv"][layer],
             A["xa_q_gain"][layer], A["xa_k_gain"][layer], W["xa_wo"][layer])
        P.ffn(h, A["ln_ffn"][layer], W["ffn_w_in"][layer], W["ffn_w_out"][layer])
        if dbg_layers:
            d = nc.dram_tensor(f"dbg_h{layer}", [S, D], F32, kind="ExternalOutput").ap()
            for r0 in range(0, S, 512):
                mk.dma(mk.sp, d[r0:r0 + 512, :], h[r0:r0 + 512, :])
            mk.barrier()
    P.finish()
    return P


def host_inputs(P, inputs, b):
    toep_i, cmpx_i = P.idx
    rel_ext = np.concatenate([np.asarray(inputs["rel_bias"], np.float32), np.full((1, 8), -BIG, np.float32)], 0)
    im = dict(P.consts)
    im["x"] = np.ascontiguousarray(inputs["x"][b])
    im["mem"] = np.ascontiguousarray(inputs["mem"][b])
    im["rel_toep"] = np.ascontiguousarray(rel_ext[toep_i].transpose(0, 1, 3, 2))
    im["rel_cmpx"] = np.ascontiguousarray(rel_ext[cmpx_i].transpose(0, 2, 1))
    for k in INPUT_SHAPES:
        if k != "rel_bias":
            im[k] = np.ascontiguousarray(np.asarray(inputs[k], np.float32))
    return im


def kernel(**inputs):
    inputs = {k: np.asarray(v) for k, v in inputs.items()}
    B, S, _ = inputs["x"].shape
    P = build_program(S=S, M=inputs["mem"].shape[1])
    shared = host_inputs(P, inputs, 0)
    in_maps = []
    for b in range(B):
        im = dict(shared)
        im["x"] = np.ascontiguousarray(inputs["x"][b])
        im["mem"] = np.ascontiguousarray(inputs["mem"][b])
        in_maps.append(im)
    res = run_bass_kernel_spmd(P.nc, in_maps, core_ids=list(range(B)))
    return np.stack([np.asarray(r["out"], np.float32) for r in res.results], 0)
```

```python
import math
from contextlib import ExitStack

import numpy as np
import ml_dtypes
import concourse.bass as bass
import concourse.mybir as mybir
from concourse.bass_utils import run_bass_kernel_spmd

F32 = mybir.dt.float32
BF16 = mybir.dt.bfloat16
AF = mybir.ActivationFunctionType
ALU = mybir.AluOpType
AX = mybir.AxisListType


class Ev:
    __slots__ = ("eng", "sem", "val")

    def __init__(self, eng, sem=None, val=None):
        self.eng, self.sem, self.val = eng, sem, val


class Buf:
    __slots__ = ("w", "rd", "name")

    def __init__(self, name=""):
        self.w = None
        self.rd = {}
        self.name = name


class Eng:
    def __init__(self, name, h, sem, is_pe=False):
        self.name, self.h, self.sem, self.is_pe = name, h, sem, is_pe
        self.n = 0
        self.pending = []
        self.last = None
        self.waited = {}
        self.ring = []
        self.ring_cnt = []
        self.dma_i = 0


class MK:
    RING = 12

    def __init__(self, nc, es):
        self.nc = nc
        self.es = es
        mk = lambda n: es.enter_context(nc.semaphore(n))
        self.pe = Eng("pe", nc.tensor, mk("s_pe"), is_pe=True)
        self.act = Eng("act", nc.scalar, mk("s_act"))
        self.dve = Eng("dve", nc.vector, mk("s_dve"))
        self.pool = Eng("pool", nc.gpsimd, mk("s_pool"))
        self.sp = Eng("sp", nc.sync, mk("s_sp"))
        self.engs = [self.pe, self.act, self.dve, self.pool, self.sp]
        for e in (self.sp, self.pool, self.act):
            r = self.RING if e is self.sp else 6
            e.ring = [mk(f"r_{e.name}{i}") for i in range(r)]
            e.ring_cnt = [0] * r
        self.ninst = 0
        self.castq = Eng("castq", nc.gpsimd, None)
        self.castq.ring = [mk(f"r_cast{i}") for i in range(8)]
        self.castq.ring_cnt = [0] * 8
        self.on_barrier = None

    def _resolve(self, ev):
        if ev.val is None:
            p = ev.eng
            p.n += 1
            p.last.then_inc(p.sem, 1)
            for e in p.pending:
                e.val = p.n
                e.sem = p.sem
            p.pending = []

    def _wait(self, eng, ev):
        if ev.eng is eng and eng.is_pe:
            return
        self._resolve(ev)
        k = id(ev.sem)
        if eng.waited.get(k, 0) < ev.val:
            eng.h.wait_ge(ev.sem, ev.val)
            eng.waited[k] = ev.val
            self.ninst += 1

    def _deps(self, r, w):
        deps = []
        for b in r:
            if b.w is not None:
                deps.append(b.w)
        for b in w:
            if b.w is not None:
                deps.append(b.w)
            deps.extend(b.rd.values())
        return deps

    def _record(self, ev, key, r, w):
        for b in r:
            b.rd[key] = ev
        for b in w:
            b.w = ev
            b.rd = {}

    def op(self, eng, fn, r=(), w=()):
        for ev in self._deps(r, w):
            self._wait(eng, ev)
        inst = fn()
        self.ninst += 1
        eng.last = inst
        if eng.is_pe:
            ev = Ev(eng)
            eng.pending.append(ev)
        else:
            eng.n += 1
            inst.then_inc(eng.sem, 1)
            ev = Ev(eng, eng.sem, eng.n)
        self._record(ev, id(eng), r, w)
        return inst

    def dma(self, q, out, in_, r=(), w=(), **kw):
        for ev in self._deps(r, w):
            self._wait(q, ev)
        i = q.dma_i
        q.dma_i += 1
        slot = i % len(q.ring)
        s = q.ring[slot]
        base = q.ring_cnt[slot]
        if base > 0 and q.waited.get(id(s), 0) < base:
            q.h.wait_ge(s, base)
            q.waited[id(s)] = base
        q.h.dma_start(out=out, in_=in_, **kw).then_inc(s, 16)
        self.ninst += 1
        q.ring_cnt[slot] = base + 16
        ev = Ev(None, s, base + 16)
        self._record(ev, (id(s), base + 16), r, w)
        return ev

    def barrier(self):
        evs = []
        for e in self.engs:
            if e.pending:
                self._resolve(e.pending[-1])
            if e.n > 0:
                evs.append(Ev(e, e.sem, e.n))
            for s, c in zip(e.ring, e.ring_cnt):
                if c > 0:
                    evs.append(Ev(None, s, c))
        for e in self.engs:
            for ev in evs:
                if ev.eng is e:
                    continue
                k = id(ev.sem)
                if e.waited.get(k, 0) < ev.val:
                    e.h.wait_ge(ev.sem, ev.val)
                    e.waited[k] = ev.val
        if self.on_barrier is not None:
            self.on_barrier()


D = 2048
EPS = 1e-6
HYB_IN = 6680
GDN_IN = 12352
FFN_H = 5632
BIG = 30000.0


class Prog:
    def __init__(self, S=4096, M=256, dbg_out=()):
        self.S, self.M = S, M
        self.nc = nc = bass.Bass("TRN2", target_bir_lowering=False)
        self.es = ExitStack()
        self.mk = MK(nc, self.es)
        self.dram = {}
        self.dbg_out = set(dbg_out)
        self.in_names = []
        self.pb = [self.es.enter_context(nc.psum_tensor(f"pb{i}", [128, 512], F32)) for i in range(8)]
        self.pbb = [Buf(f"pb{i}") for i in range(8)]
        self.rot = list(range(8))
        self.rot_i = 0
        self.uid = 0
        self.ident_bf = self.const_sb("ident_bf", np.eye(128, dtype=np.float32).astype(ml_dtypes.bfloat16), BF16)
        self.ident_f = self.const_sb("ident_f", np.eye(128, dtype=np.float32), F32)
        self.ones_mean = self.const_sb("ones_mean", np.full((128, 128), 1.0 / 128, np.float32).astype(ml_dtypes.bfloat16), BF16)
        self.ones_bf = self.const_sb("ones_bf", np.ones((128, 128), np.float32).astype(ml_dtypes.bfloat16), BF16)
        self.epsc = {}
        for i, e in enumerate((1e-6, 1e-5)):
            t = self.es.enter_context(nc.sbuf_tensor(f"c_eps{i}", [128, 1], F32))
            self.mk.op(self.mk.pool, lambda: nc.gpsimd.memset(t[:], float(e)), w=[Buf()])
            self.epsc[float(e)] = t
        self.mk.barrier()

    def inp(self, name, shape, dt=F32):
        t = self.nc.dram_tensor(name, list(shape), dt, kind="ExternalInput").ap()
        self.dram[name] = t
        self.in_names.append(name)
        return t

    def scr(self, name, shape, dt=BF16):
        if name in self.dram:
            return self.dram[name]
        kind = "ExternalOutput" if name in self.dbg_out else "Internal"
        t = self.nc.dram_tensor(name, list(shape), dt, kind=kind).ap()
        self.dram[name] = t
        return t

    consts = {}

    def const_sb(self, name, arr, dt):
        self.consts = dict(self.consts)
        self.consts[name] = np.ascontiguousarray(arr)
        d = self.inp(name, arr.shape, dt)
        t = self.es.enter_context(self.nc.sbuf_tensor("c_" + name, list(arr.shape), dt))
        b = Buf(name)
        self.mk.dma(self.mk.sp, t[:], d, w=[b])
        return (t, b)

    def const_dram(self, name, arr, dt):
        self.consts = dict(self.consts)
        self.consts[name] = np.ascontiguousarray(arr)
        return self.inp(name, arr.shape, dt)

    def ps(self):
        i = self.rot[self.rot_i % len(self.rot)]
        self.rot_i += 1
        return self.pb[i], self.pbb[i]

    def sbt(self, st, name, shape, dt):
        self.uid += 1
        return st.enter_context(self.nc.sbuf_tensor(f"{name}_{self.uid}", list(shape), dt))

    def evac_eng(self):
        self.uid += 1
        return self.mk.act if self.uid % 2 else self.mk.dve

    def copy(self, eng, out, in_, r, w):
        nc = self.nc
        if eng is self.mk.act:
            return self.mk.op(eng, lambda: nc.scalar.copy(out=out, in_=in_), r=r, w=w)
        return self.mk.op(eng, lambda: eng.h.tensor_copy(out=out, in_=in_), r=r, w=w)

    def rsqrt(self, out, in_, eps, r, w):
        nc, mk = self.nc, self.mk
        mk.op(mk.act, lambda: nc.scalar.activation(out=out, in_=in_, func=AF.Ln, bias=self.eps_col(eps)), r=r, w=w)
        mk.op(mk.act, lambda: nc.scalar.activation(out=out, in_=out, func=AF.Exp, scale=-0.5), r=w, w=w)

    def eps_col(self, eps):
        return self.epsc[float(eps)][:, 0:1]

    def wdep(self, ap):
        reg = getattr(self, "wreg", {})
        ent = reg.get(ap.tensor.name)
        if ent is None:
            return []
        per, bufs = ent
        return [bufs[int(ap.offset) // per]]

    def emit_casts(self, n=6):
        q = getattr(self, "cast_pending", [])
        for _ in range(min(n, len(q))):
            _, dst, src, buf = q.pop(0)
            self.mk.dma(self.mk.castq, dst, src, w=[buf])

    def emit_casts_upto(self, layer):
        q = getattr(self, "cast_pending", [])
        while q and q[0][0] <= layer:
            _, dst, src, buf = q.pop(0)
            self.mk.dma(self.mk.castq, dst, src, w=[buf])

    def defer(self, fn):
        self._deferred = getattr(self, "_deferred", [])
        self._deferred.append(fn)

    def take(self):
        q = getattr(self, "_deferred", [])
        self._deferred = []
        return q

    def flush(self):
        while getattr(self, "_deferred", []):
            for fn in self.take():
                fn()

    def finish(self):
        self.flush()
        self.mk.barrier()
        self.es.close()

    def precast(self, name, src, rows_per=None):
        shp = list(src.shape)
        dst = self.scr(name + "_bf", shp, BF16)
        s2 = src if len(shp) == 2 else src.rearrange("a b c -> (a b) c")
        d2 = dst if len(shp) == 2 else dst.rearrange("a b c -> (a b) c")
        R, C = s2.shape
        rp = max(1, min(R, (2 << 20) // C))
        b = Buf()
        for r0 in range(0, R, rp):
            r1 = min(R, r0 + rp)
            self.mk.dma(self.mk.pool, d2[r0:r1, :], s2[r0:r1, :])
        return dst

    class Pro:
        pass

    def make_pro(self, st, name, src, K, gain, src_bf16, TB):
        nc, mk = self.nc, self.mk
        p = Prog.Pro()
        p.K, p.KC, p.TB, p.src, p.gain, p.src_bf16 = K, K // 128, TB, src, gain, src_bf16
        p.xT = [self.sbt(st, name + "xT", [128, p.KC, TB], BF16) for _ in range(2)]
        p.xTb = [Buf(), Buf()]
        p.xin = [self.sbt(st, name + "xin", [128, K], BF16 if src_bf16 else F32) for _ in range(2)]
        p.xinb = [Buf(), Buf()]
        p.cnt = 0
        if gain is not None:
            p.xn = [self.sbt(st, name + "xn", [128, K], BF16) for _ in range(2)]
            p.xnb = [Buf(), Buf()]
            p.gt = self.sbt(st, name + "g", [128, K], F32)
            p.gb = Buf()
            mk.dma(mk.sp, p.gt[:], gain.partition_broadcast(128), w=[p.gb])
            p.junk = self.sbt(st, name + "junk", [128, K], BF16)
            p.junkb = Buf()
            p.stat = [self.sbt(st, name + "stat", [128, 2], F32) for _ in range(2)]
            p.statb = [Buf(), Buf()]
        return p

    def prologue(self, p, tb, slot):
        nc, mk = self.nc, self.mk
        K, KC, TB = p.K, p.KC, p.TB
        for t in range(TB // 128):
            j = p.cnt % 2
            p.cnt += 1
            row0 = tb * TB + t * 128
            xin, xinb = p.xin[j], p.xinb[j]
            mk.dma(mk.sp, xin[:], p.src[row0:row0 + 128, :], w=[xinb])
            if p.gain is not None:
                stt, sttb = p.stat[j], p.statb[j]
                mk.op(mk.act, lambda: nc.scalar.activation(out=p.junk[:], in_=xin[:], func=AF.Square,
                                                           scale=float(K) ** -0.5, accum_out=stt[:, 0:1]),
                      r=[xinb], w=[p.junkb, sttb])
                self.rsqrt(stt[:, 1:2], stt[:, 0:1], EPS, [sttb], [sttb])
                xn, xnb = p.xn[j], p.xnb[j]
                mk.op(mk.dve, lambda: nc.vector.scalar_tensor_tensor(out=xn[:], in0=xin[:], scalar=stt[:, 1:2],
                                                                     in1=p.gt[:], op0=ALU.mult, op1=ALU.mult),
                      r=[xinb, sttb, p.gb], w=[xnb])
                srcT, srcb = xn, xnb
            else:
                srcT, srcb = xin, xinb
            for k0 in range(0, KC, 8):
                nk = min(8, KC - k0)
                bank, bankb = self.ps()
                bv = bank[:].bitcast(BF16)
                for kk in range(nk):
                    mk.op(mk.pe, lambda kk=kk: nc.tensor.transpose(out=bv[:, kk * 128:(kk + 1) * 128],
                                                                   in_=srcT[:, (k0 + kk) * 128:(k0 + kk + 1) * 128],
                                                                   identity=self.ident_bf[0][:]),
                          r=[srcb, self.ident_bf[1]], w=[bankb])
                self.copy(self.evac_eng(), p.xT[slot][:, k0:k0 + nk, t * 128:(t + 1) * 128],
                          bv[:, :nk * 128].rearrange("p (k c) -> p k c", c=128), r=[bankb], w=[p.xTb[slot]])

    class Pan:
        pass

    def make_pan(self, st, name, KC, PW, nslots=2):
        pn = Prog.Pan()
        pn.KC, pn.PW = KC, PW
        pn.t = [self.sbt(st, name + "wp", [128, KC, PW], BF16) for _ in range(nslots)]
        pn.b = [Buf() for _ in range(nslots)]
        pn.cnt = 0
        return pn

    def run_mm(self, pn, jobs, xT, xTb, TBW, mid_hook=None):
        nc, mk = self.nc, self.mk
        KC, PW = pn.KC, pn.PW
        panels = []
        for jb in jobs:
            g = jb.get("group", 1)
            step = PW
            for p0 in range(jb["c0"], jb["c1"], step):
                panels.append((jb, p0, min(step, jb["c1"] - p0)))

        def load(i):
            jb, p0, pw = panels[i]
            slot = pn.cnt % len(pn.t)
            pn.cnt += 1
            Wv = jb["W"].rearrange("(k p) n -> p k n", p=128)
            mk.dma(mk.sp, pn.t[slot][:, :, :pw], Wv[:, :, p0:p0 + pw], r=self.wdep(jb["W"]), w=[pn.b[slot]])
            return slot

        slots = {0: load(0)}
        for i, (jb, p0, pw) in enumerate(panels):
            if i + 1 < len(panels):
                slots[i + 1] = load(i + 1)
            slot = slots.pop(i)
            wt, wb = pn.t[slot], pn.b[slot]
            if jb["layout"] == "FM":
                grp = []
                for c0 in range(0, pw, 128):
                    cw = min(128, pw - c0)
                    bank, bankb = self.ps()
                    for kc in range(KC):
                        mk.op(mk.pe, lambda kc=kc: nc.tensor.matmul(bank[:cw, :TBW], lhsT=wt[:, kc, c0:c0 + cw],
                                                                     rhs=xT[:, kc, :], start=(kc == 0),
                                                                     stop=(kc == KC - 1)),
                              r=[wb, xTb], w=[bankb])
                    grp.append((bank[:cw, :TBW], bankb, p0 + c0, cw))
                    if len(grp) == jb.get("group", 1):
                        pend = self.take()
                        jb["epi"](grp)
                        for fn in pend:
                            fn()
                        grp = []
                assert not grp
            else:
                for t in range(TBW // 128):
                    bank, bankb = self.ps()
                    for kc in range(KC):
                        mk.op(mk.pe, lambda kc=kc: nc.tensor.matmul(bank[:, :pw], lhsT=xT[:, kc, t * 128:(t + 1) * 128],
                                                                     rhs=wt[:, kc, :pw], start=(kc == 0),
                                                                     stop=(kc == KC - 1)),
                              r=[wb, xTb], w=[bankb])
                    pend = self.take()
                    jb["epi"]([(bank[:, :pw], bankb, p0, pw, t)])
                    for fn in pend:
                        fn()
            if i == min(1, len(panels) - 1) and mid_hook is not None:
                self.flush()
                mid_hook()

    def epi_tm_store(self, st, dst, col_shift, TB, dt=BF16, func=None, width=512, row_of=None):
        nc, mk = self.nc, self.mk
        stg = [self.sbt(st, "stg", [128, width], dt) for _ in range(2)]
        stgb = [Buf(), Buf()]
        cnt = [0]

        def epi(grp, tb):
            (ps, psb, col, w, t), = grp
            j = cnt[0] % 2
            cnt[0] += 1
            if func is None:
                self.copy(self.evac_eng(), stg[j][:, :w], ps, r=[psb], w=[stgb[j]])
            else:
                mk.op(mk.act, lambda: nc.scalar.activation(out=stg[j][:, :w], in_=ps, func=func), r=[psb], w=[stgb[j]])
            row0 = tb * TB + t * 128
            mk.dma(mk.pool, dst[row0:row0 + 128, col + col_shift:col + col_shift + w], stg[j][:, :w], r=[stgb[j]])
        return epi

    def epi_fm_store(self, st, dst, row_shift, TB, dt=BF16, func=None):
        nc, mk = self.nc, self.mk
        stg = [self.sbt(st, "stgf", [128, TB], dt) for _ in range(2)]
        stgb = [Buf(), Buf()]
        cnt = [0]

        def epi(grp, tb):
            (ps, psb, col, cw), = grp
            j = cnt[0] % 2
            cnt[0] += 1
            if func is None:
                self.copy(self.evac_eng(), stg[j][:cw, :], ps, r=[psb], w=[stgb[j]])
            else:
                mk.op(mk.act, lambda: nc.scalar.activation(out=stg[j][:cw, :], in_=ps, func=func), r=[psb], w=[stgb[j]])
            mk.dma(mk.pool, dst[col + row_shift:col + row_shift + cw, tb * TB:(tb + 1) * TB], stg[j][:cw, :], r=[stgb[j]])
        return epi

    def epi_residual(self, st, h, TB, width=512):
        nc, mk = self.nc, self.mk
        hr = [self.sbt(st, "hres", [128, width], F32) for _ in range(3)]
        hrb = [Buf() for _ in range(3)]
        cnt = [0]

        def epi(grp, tb):
            (ps, psb, col, w, t), = grp
            j = cnt[0] % 3
            cnt[0] += 1
            row0 = tb * TB + t * 128
            mk.dma(mk.sp, hr[j][:, :w], h[row0:row0 + 128, col:col + w], w=[hrb[j]])
            mk.op(mk.dve, lambda: nc.vector.tensor_tensor(out=hr[j][:, :w], in0=ps, in1=hr[j][:, :w], op=ALU.add),
                  r=[psb, hrb[j]], w=[hrb[j]])
            mk.dma(mk.pool, h[row0:row0 + 128, col:col + w], hr[j][:, :w], r=[hrb[j]])
        return epi

    def epi_fm_headnorm(self, st, dst, row_shift, TB, gcol, gcolb, mean=True, eps=EPS):
        nc, mk = self.nc, self.mk
        sq = [self.sbt(st, "hn_sq", [128, TB], BF16) for _ in range(2)]
        sqb = [Buf(), Buf()]
        rs = [self.sbt(st, "hn_rs", [128, TB], F32) for _ in range(2)]
        rsb = [Buf(), Buf()]
        stg = [self.sbt(st, "hn_o", [128, TB], BF16) for _ in range(2)]
        stgb = [Buf(), Buf()]
        cnt = [0]
        ones = self.ones_mean if mean else self.ones_bf

        def epi(grp, tb, gi=0, store=None, deep=False):
            (ps, psb, col, cw), = grp
            assert cw == 128
            j = cnt[0] % 2
            cnt[0] += 1
            mk.op(mk.act, lambda: nc.scalar.activation(out=sq[j][:], in_=ps, func=AF.Square), r=[psb], w=[sqb[j]])

            def part_b1():
                b2, b2b = self.ps()
                mk.op(mk.pe, lambda: nc.tensor.matmul(b2[:, :TB], lhsT=ones[0][:], rhs=sq[j][:], start=True, stop=True),
                      r=[sqb[j], ones[1]], w=[b2b])
                self.rsqrt(rs[j][:], b2[:, :TB], eps, [b2b], [rsb[j]])

            def part_b2():
                mk.op(mk.dve, lambda: nc.vector.scalar_tensor_tensor(out=stg[j][:], in0=ps, scalar=gcol[:, gi:gi + 1],
                                                                     in1=rs[j][:], op0=ALU.mult, op1=ALU.mult),
                      r=[psb, rsb[j], gcolb], w=[stgb[j]])
                if store is None:
                    mk.dma(mk.pool, dst[col + row_shift:col + row_shift + 128, tb * TB:(tb + 1) * TB], stg[j][:], r=[stgb[j]])
                else:
                    store(stg[j], stgb[j])

            if deep:
                def b1_then_defer():
                    part_b1()
                    self.defer(part_b2)
                self.defer(b1_then_defer)
            else:
                def part_b():
                    part_b1()
                    part_b2()
                self.defer(part_b)
        return epi

    def linear(self, name, src, ntok, K, gain, jobs_fn, src_bf16=False, TB=512, PW=None):
        TB = min(TB, ntok)
        KC = K // 128
        if PW is None:
            PW = 512 if KC <= 16 else 256
        with ExitStack() as st:
            p = self.make_pro(st, name, src, K, gain, src_bf16, TB)
            pn = self.make_pan(st, name, KC, PW)
            jobs = jobs_fn(st, TB)
            NB = ntok // TB
            self.prologue(p, 0, 0)
            for tb in range(NB):
                slot = tb % 2
                jj = [dict(jb, epi=(lambda grp, e=jb["epi"], tb=tb: e(grp, tb))) for jb in jobs]
                hook = (lambda tb=tb: self.prologue(p, tb + 1, (tb + 1) % 2)) if tb + 1 < NB else None
                self.run_mm(pn, jj, p.xT[slot], p.xTb[slot], TB, mid_hook=hook)
            self.flush()
            self.mk.barrier()

    def ffn(self, h, gain, w_in, w_out):
        nc, mk = self.nc, self.mk
        S, TB = self.S, min(512, self.S)
        HC = FFN_H // 128
        with ExitStack() as st:
            p = self.make_pro(st, "ffn", h, D, gain, False, TB)
            pn1 = self.make_pan(st, "ffn1", 16, 256, nslots=4)
            pn2 = self.make_pan(st, "ffn2", HC, 256)
            hid = self.sbt(st, "hid", [128, HC, TB], BF16)
            hidb = Buf()
            sg = [self.sbt(st, "sg", [128, TB], F32) for _ in range(2)]
            sgb = [Buf(), Buf()]
            cnt = [0]
            res = self.epi_residual(st, h, TB, width=256)
            NB = S // TB
            self.prologue(p, 0, 0)
            for tb in range(NB):
                slot = tb % 2
                gate_ps = {}

                def epi_gate(grp, tb=tb):
                    (ps, psb, col, cw), = grp
                    gate_ps[col] = (ps, psb)

                def epi_up(grp, tb=tb):
                    (ps, psb, col, cw), = grp
                    gps, gpsb = gate_ps.pop(col - FFN_H)
                    j = cnt[0] % 2
                    cnt[0] += 1
                    mk.op(mk.act, lambda: nc.scalar.activation(out=sg[j][:], in_=gps, func=AF.Silu), r=[gpsb], w=[sgb[j]])
                    hc = (col - FFN_H) // 128
                    mk.op(mk.dve, lambda: nc.vector.tensor_tensor(out=hid[:, hc, :], in0=ps, in1=sg[j][:], op=ALU.mult),
                          r=[psb, sgb[j]], w=[hidb])
                jobs = []
                for c0 in range(0, FFN_H, 256):
                    jobs.append(dict(W=w_in, c0=c0, c1=c0 + 256, layout="FM", epi=epi_gate))
                    jobs.append(dict(W=w_in, c0=FFN_H + c0, c1=FFN_H + c0 + 256, layout="FM", epi=epi_up))
                hook = (lambda tb=tb: self.prologue(p, tb + 1, (tb + 1) % 2)) if tb + 1 < NB else None
                self.run_mm(pn1, jobs, p.xT[slot], p.xTb[slot], TB, mid_hook=hook)
                jobs2 = [dict(W=w_out, c0=0, c1=D, layout="TM", epi=lambda grp, tb=tb: res(grp, tb))]
                self.run_mm(pn2, jobs2, hid, hidb, TB)
            mk.barrier()

    def load_col(self, st, src128, scale=None):
        nc, mk = self.nc, self.mk
        t = self.sbt(st, "col", [128, 1], F32)
        b = Buf()
        mk.dma(mk.sp, t[:], src128.rearrange("(p o) -> p o", o=1), w=[b])
        if scale is not None:
            mk.op(mk.dve, lambda: nc.vector.tensor_scalar(out=t[:], in0=t[:], scalar1=float(scale), scalar2=None,
                                                          op0=ALU.mult), r=[b], w=[b])
        return t, b

    def xa(self, h, mem, ln_mem, mem_gain, wq, wkv, q_gain, k_gain, wo):
        nc, mk = self.nc, self.mk
        S, M = self.S, self.M
        kT = self.scr("xa_kT", [512, M])
        v = self.scr("xa_v", [M, 512])
        qT = self.scr("xa_qT", [512, S])
        o = self.scr("xa_o", [S, 512])

        def jobs_kv(st, TB):
            gc, gcb = self.load_col(st, k_gain)
            hn = self.epi_fm_headnorm(st, kT, 0, TB, gc, gcb)
            tm = self.epi_tm_store(st, v, -512, TB)
            return [dict(W=wkv, c0=0, c1=512, layout="FM", epi=hn),
                    dict(W=wkv, c0=512, c1=1024, layout="TM", epi=tm)]
        self.linear("xakv", mem, M, D, mem_gain, jobs_kv, TB=256)

        def jobs_q(st, TB):
            gc, gcb = self.load_col(st, q_gain, scale=128 ** -0.5)
            hn = self.epi_fm_headnorm(st, qT, 0, TB, gc, gcb)
            return [dict(W=wq, c0=0, c1=512, layout="FM", epi=hn)]
        self.linear("xaq", h, S, D, ln_mem, jobs_q)

        MT = M // 128
        with ExitStack() as st:
            KT = self.sbt(st, "xaKT", [128, 4, M], BF16)
            KTb = Buf()
            mk.dma(mk.sp, KT[:], kT.rearrange("(h p) m -> p h m", p=128), w=[KTb])
            V = self.sbt(st, "xaV", [128, MT, 4, 129], BF16)
            Vb = Buf()
            mk.op(mk.pool, lambda: nc.gpsimd.memset(V[:], 1.0), w=[Vb])
            for kt in range(MT):
                mk.dma(mk.sp, V[:, kt, :, 0:128], v[kt * 128:(kt + 1) * 128, :].rearrange("p (h d) -> p h d", d=128), w=[Vb])
            TB = min(512, S)
            NQ = TB // 128
            qt = [self.sbt(st, "xaq", [128, 4, TB], BF16) for _ in range(2)]
            qtb = [Buf(), Buf()]
            pT = [self.sbt(st, "xapT", [128, TB], BF16) for _ in range(2 * MT)]
            pTb = [Buf() for _ in range(2 * MT)]
            ot = [self.sbt(st, "xaot", [128, NQ, 512], BF16) for _ in range(2)]
            otb = [Buf(), Buf()]
            rc = [self.sbt(st, "xarc", [128, 1], F32) for _ in range(4)]
            rcb = [Buf() for _ in range(4)]
            pc = 0
            rcc = 0

            def score(tb, hh):
                nonlocal pc
                j = tb % 2
                if hh == 0:
                    mk.dma(mk.sp, qt[j][:], qT[:, tb * TB:(tb + 1) * TB].rearrange("(h p) s -> p h s", p=128), w=[qtb[j]])
                cur = []
                for kt in range(MT):
                    bank, bankb = self.ps()
                    mk.op(mk.pe, lambda: nc.tensor.matmul(bank[:, :TB], lhsT=KT[:, hh, kt * 128:(kt + 1) * 128],
                                                          rhs=qt[j][:, hh, :], start=True, stop=True),
                          r=[KTb, qtb[j]], w=[bankb])
                    pi = pc % (2 * MT)
                    pc += 1
                    mk.op(mk.act, lambda: nc.scalar.activation(out=pT[pi][:], in_=bank[:, :TB], func=AF.Exp),
                          r=[bankb], w=[pTb[pi]])
                    cur.append(pi)
                return cur

            def pv(tb, hh, cur):
                nonlocal rcc
                j = tb % 2
                for qi in range(NQ):
                    bank, bankb = self.ps()
                    for kt in range(MT):
                        pi = cur[kt]
                        mk.op(mk.pe, lambda: nc.tensor.matmul(bank[:, :129], lhsT=pT[pi][:, qi * 128:(qi + 1) * 128],
                                                              rhs=V[:, kt, hh, :], start=(kt == 0), stop=(kt == MT - 1)),
                              r=[pTb[pi], Vb], w=[bankb])
                    ri = rcc % 4
                    rcc += 1
                    mk.op(mk.dve, lambda: nc.vector.reciprocal(out=rc[ri][:], in_=bank[:, 128:129]), r=[bankb], w=[rcb[ri]])
                    mk.op(mk.dve, lambda: nc.vector.tensor_scalar(out=ot[j][:, qi, hh * 128:(hh + 1) * 128], in0=bank[:, 0:128],
                                                                  scalar1=rc[ri][:, 0:1], scalar2=None, op0=ALU.mult),
                          r=[bankb, rcb[ri]], w=[otb[j]])
                if hh == 3:
                    for qi in range(NQ):
                        r0 = tb * TB + qi * 128
                        mk.dma(mk.pool, o[r0:r0 + 128, :], ot[j][:, qi, :], r=[otb[j]])

            prev = None
            for tb in range(S // TB):
                for hh in range(4):
                    cur = score(tb, hh)
                    if prev is not None:
                        pv(*prev)
                    prev = (tb, hh, cur)
            pv(*prev)
            mk.barrier()

        def jobs_o(st, TB):
            res = self.epi_residual(st, h, TB)
            return [dict(W=wo, c0=0, c1=D, layout="TM", epi=res)]
        self.linear("xao", o, S, 512, None, jobs_o, src_bf16=True)

    def rot_tables(self):
        if "rot_cos" in self.dram:
            return self.dram["rot_cos"], self.dram["rot_sin"]
        S = self.S
        inv = (np.float32(10000.0) ** (-np.arange(128, dtype=np.float32) / np.float32(128))).astype(np.float32)
        ang = (np.arange(S, dtype=np.float32)[None, :] * inv[:, None]).astype(np.float32)
        c = self.const_dram("rot_cos", np.cos(ang).astype(np.float32), F32)
        s = self.const_dram("rot_sin", np.sin(ang).astype(np.float32), F32)
        return c, s

    def epi_rotary(self, st, dst, row_shift, TB):
        nc, mk = self.nc, self.mk
        cosd, sind = self.rot_tables()
        cs = [self.sbt(st, "rotcs", [128, 2, TB], F32) for _ in range(2)]
        csb = [Buf(), Buf()]
        cs_tb = [None, None]
        tmp = [self.sbt(st, "rott", [128, 4, TB], F32) for _ in range(2)]
        tmpb = [Buf(), Buf()]
        og = [self.sbt(st, "roto", [128, 2, TB], BF16) for _ in range(2)]
        ogb = [Buf(), Buf()]
        cnt = [0]

        def epi(grp, tb):
            (p1, p1b, col, _), (p2, p2b, _, _) = grp
            cj = tb % 2
            if cs_tb[cj] != tb:
                mk.dma(mk.sp, cs[cj][:, 0, :], cosd[:, tb * TB:(tb + 1) * TB], w=[csb[cj]])
                mk.dma(mk.sp, cs[cj][:, 1, :], sind[:, tb * TB:(tb + 1) * TB], w=[csb[cj]])
                cs_tb[cj] = tb
            j = cnt[0] % 2
            cnt[0] += 1
            T, Tb = tmp[j], tmpb[j]
            for k, (pp, ppb, ci) in enumerate(((p1, p1b, 0), (p2, p2b, 1), (p1, p1b, 1), (p2, p2b, 0))):
                mk.op(mk.dve, lambda: nc.vector.tensor_tensor(out=T[:, k, :], in0=pp, in1=cs[cj][:, ci, :], op=ALU.mult),
                      r=[ppb, csb[cj]], w=[Tb])
            mk.op(mk.pool, lambda: nc.gpsimd.tensor_tensor(out=og[j][:, 0, :], in0=T[:, 0, :], in1=T[:, 1, :], op=ALU.subtract),
                  r=[Tb], w=[ogb[j]])
            mk.op(mk.pool, lambda: nc.gpsimd.tensor_tensor(out=og[j][:, 1, :], in0=T[:, 2, :], in1=T[:, 3, :], op=ALU.add),
                  r=[Tb], w=[ogb[j]])
            r0 = col + row_shift
            mk.dma(mk.pool, dst[r0:r0 + 256, tb * TB:(tb + 1) * TB].rearrange("(c p) s -> p c s", p=128), og[j][:], r=[ogb[j]])
        return epi

    def hyb_in(self, h, ln, w_in, q_gain, k_gain):
        S = self.S
        qT = self.scr("nsa_qT", [1024, S])
        kcT = self.scr("nsa_kcT", [256, S])
        vcT = self.scr("nsa_vcT", [256, S])
        ksT = self.scr("nsa_ksT", [256, S])
        vs = self.scr("nsa_vs", [S, 256])
        kwT = self.scr("nsa_kwT", [256, S])
        vw = self.scr("nsa_vw", [S, 256])
        gate = self.scr("nsa_gate", [S, 24], F32)
        rqT = self.scr("ret_qT", [1024, S])
        rkT = self.scr("ret_kT", [1024, S])
        rv = self.scr("ret_v", [S, 1024])
        rg = self.scr("ret_g", [S, 1024])

        def jobs(st, TB):
            gq, gqb = self.load_col(st, q_gain, scale=128 ** -0.5)
            gks, gksb = self.load_col(st, k_gain[1])
            gkw, gkwb = self.load_col(st, k_gain[2])
            J = []
            J.append(dict(W=w_in, c0=0, c1=1024, layout="FM", epi=self.epi_fm_headnorm(st, qT, 0, TB, gq, gqb)))
            J.append(dict(W=w_in, c0=1024, c1=1280, layout="FM", epi=self.epi_fm_store(st, kcT, -1024, TB)))
            J.append(dict(W=w_in, c0=1280, c1=1536, layout="FM", epi=self.epi_fm_store(st, vcT, -1280, TB)))
            J.append(dict(W=w_in, c0=1536, c1=1792, layout="FM", epi=self.epi_fm_headnorm(st, ksT, -1536, TB, gks, gksb)))
            J.append(dict(W=w_in, c0=1792, c1=2048, layout="TM", epi=self.epi_tm_store(st, vs, -1792, TB)))
            J.append(dict(W=w_in, c0=2048, c1=2304, layout="FM", epi=self.epi_fm_headnorm(st, kwT, -2048, TB, gkw, gkwb)))
            J.append(dict(W=w_in, c0=2304, c1=2560, layout="TM", epi=self.epi_tm_store(st, vw, -2304, TB)))
            J.append(dict(W=w_in, c0=2560, c1=2584, layout="TM",
                          epi=self.epi_tm_store(st, gate, -2560, TB, dt=F32, func=AF.Sigmoid, width=24)))
            J.append(dict(W=w_in, c0=2584, c1=3608, layout="FM", group=2, epi=self.epi_rotary(st, rqT, -2584, TB)))
            J.append(dict(W=w_in, c0=3608, c1=4632, layout="FM", group=2, epi=self.epi_rotary(st, rkT, -3608, TB)))
            J.append(dict(W=w_in, c0=4632, c1=5656, layout="TM", epi=self.epi_tm_store(st, rv, -4632, TB)))
            J.append(dict(W=w_in, c0=5656, c1=6680, layout="TM", epi=self.epi_tm_store(st, rg, -5656, TB, func=AF.Silu)))
            return J
        self.linear("hyb", h, S, D, ln, jobs)

    def retention(self, gn_gain, mix_in):
        nc, mk = self.nc, self.mk
        S = self.S
        NCH = S // 128
        rqT, rkT, rv, rg = (self.dram[k] for k in ("ret_qT", "ret_kT", "ret_v", "ret_g"))
        gam = [1.0 - 2.0 ** (-5.0 - hh) for hh in range(4)]
        idx = np.arange(128, dtype=np.float64)
        dm = np.zeros((128, 4, 128), np.float32)
        qd = np.zeros((128, 4), np.float32)
        kd = np.zeros((128, 4), np.float32)
        for hh in range(4):
            lg = np.float32(np.log(np.float32(gam[hh])))
            rel = idx[None, :] - idx[:, None]
            dm[:, hh, :] = np.where(rel >= 0, np.exp(lg * np.maximum(rel, 0.0)), 0.0) / 16.0
            qd[:, hh] = np.exp(lg * (idx + 1.0)) / 16.0
            kd[:, hh] = np.exp(lg * (127.0 - idx))
        cdec = [float(np.exp(np.float32(np.log(np.float32(g))) * 128.0)) for g in gam]
        with ExitStack() as st:
            if "ret_dm" not in self.dram:
                self.const_dram("ret_dm", dm, F32)
                self.const_dram("ret_qd", qd, F32)
                self.const_dram("ret_kd", kd, F32)
            DM = self.sbt(st, "retDM", [128, 4, 128], F32)
            QD = self.sbt(st, "retQD", [128, 4], F32)
            KD = self.sbt(st, "retKD", [128, 4], F32)
            GN = self.sbt(st, "retGN", [128, 1024], F32)
            cb = Buf()
            mk.dma(mk.sp, DM[:], self.dram["ret_dm"], w=[cb])
            mk.dma(mk.sp, QD[:], self.dram["ret_qd"], w=[cb])
            mk.dma(mk.sp, KD[:], self.dram["ret_kd"], w=[cb])
            mk.dma(mk.sp, GN[:], gn_gain.partition_broadcast(128), w=[cb])
            stf = self.sbt(st, "retSf", [128, 4, 2, 256], F32)
            stb = self.sbt(st, "retSb", [128, 4, 2, 256], BF16)
            stfb, stbb = Buf(), Buf()
            mk.op(mk.pool, lambda: nc.gpsimd.memset(stf[:], 0.0), w=[stfb])
            mk.op(mk.pool, lambda: nc.gpsimd.memset(stb[:], 0.0), w=[stbb])
            qt = [self.sbt(st, "retq", [128, 8, 128], BF16) for _ in range(2)]
            kt = [self.sbt(st, "retk", [128, 8, 128], BF16) for _ in range(2)]
            vt = [self.sbt(st, "retv", [128, 1024], BF16) for _ in range(2)]
            gt = [self.sbt(st, "retg", [128, 1024], BF16) for _ in range(2)]
            inb = [Buf(), Buf()]
            ktm = [self.sbt(st, "retktm", [128, 256], BF16) for _ in range(2)]
            ktmb = [Buf(), Buf()]
            innT = [self.sbt(st, "retin", [128, 128], BF16) for _ in range(2)]
            innTb = [Buf(), Buf()]
            vd = [self.sbt(st, "retvd", [128, 256], BF16) for _ in range(2)]
            vdb = [Buf(), Buf()]
            osb = [self.sbt(st, "reto", [128, 4, 256], F32) for _ in range(2)]
            osbb = [Buf(), Buf()]
            junk = self.sbt(st, "retjunk", [128, 4, 256], F32)
            junkb = Buf()
            stat = [self.sbt(st, "retstat", [128, 16], F32) for _ in range(2)]
            statb = [Buf(), Buf()]
            yo = [self.sbt(st, "rety", [128, 1024], BF16) for _ in range(2)]
            yob = [Buf(), Buf()]
            c2 = 0
            for n in range(NCH):
                j = n % 2
                cs = slice(n * 128, (n + 1) * 128)
                mk.dma(mk.sp, qt[j][:], rqT[:, cs].rearrange("(c p) s -> p c s", p=128), w=[inb[j]])
                mk.dma(mk.sp, kt[j][:], rkT[:, cs].rearrange("(c p) s -> p c s", p=128), w=[inb[j]])
                mk.dma(mk.sp, vt[j][:], rv[cs, :], w=[inb[j]])
                mk.dma(mk.sp, gt[j][:], rg[cs, :], w=[inb[j]])
                O, Ob = osb[j], osbb[j]
                for hh in range(4):
                    i2 = c2 % 2
                    c2 += 1
                    bt, btb = self.ps()
                    btv = bt[:].bitcast(BF16)
                    for dc in range(2):
                        mk.op(mk.pe, lambda: nc.tensor.transpose(out=btv[:, dc * 128:(dc + 1) * 128], in_=kt[j][:, hh * 2 + dc, :],
                                                                 identity=self.ident_bf[0][:]),
                              r=[inb[j], self.ident_bf[1]], w=[btb])
                    self.copy(mk.act, ktm[i2][:], btv[:, 0:256], r=[btb], w=[ktmb[i2]])
                    bi, bib = self.ps()
                    for dc in range(2):
                        mk.op(mk.pe, lambda: nc.tensor.matmul(bi[:, :128], lhsT=kt[j][:, hh * 2 + dc, :], rhs=qt[j][:, hh * 2 + dc, :],
                                                              start=(dc == 0), stop=(dc == 1)), r=[inb[j]], w=[bib])
                    mk.op(mk.dve, lambda: nc.vector.tensor_tensor(out=innT[i2][:], in0=bi[:, :128], in1=DM[:, hh, :], op=ALU.mult),
                          r=[bib, cb], w=[innTb[i2]])
                    bA, bAb = self.ps()
                    mk.op(mk.pe, lambda: nc.tensor.matmul(bA[:, :256], lhsT=innT[i2][:], rhs=vt[j][:, hh * 256:(hh + 1) * 256],
                                                          start=True, stop=True), r=[innTb[i2], inb[j]], w=[bAb])
                    bB, bBb = self.ps()
                    for dc in range(2):
                        mk.op(mk.pe, lambda: nc.tensor.matmul(bB[:, :256], lhsT=qt[j][:, hh * 2 + dc, :], rhs=stb[:, hh, dc, :],
                                                              start=(dc == 0), stop=(dc == 1)), r=[inb[j], stbb], w=[bBb])
                    self.copy(mk.act, O[:, hh, :], bA[:, :256], r=[bAb], w=[Ob])
                    mk.op(mk.dve, lambda: nc.vector.scalar_tensor_tensor(out=O[:, hh, :], in0=bB[:, :256], scalar=QD[:, hh:hh + 1],
                                                                         in1=O[:, hh, :], op0=ALU.mult, op1=ALU.add),
                          r=[bBb, cb, Ob], w=[Ob])
                    mk.op(mk.pool, lambda: nc.gpsimd.tensor_scalar(out=vd[i2][:], in0=vt[j][:, hh * 256:(hh + 1) * 256],
                                                                   scalar1=KD[:, hh:hh + 1], scalar2=None, op0=ALU.mult),
                          r=[inb[j], cb], w=[vdb[i2]])
                    for dc in range(2):
                        bS, bSb = self.ps()
                        mk.op(mk.pe, lambda: nc.tensor.matmul(bS[:, :256], lhsT=ktm[i2][:, dc * 128:(dc + 1) * 128], rhs=vd[i2][:],
                                                              start=True, stop=True), r=[ktmb[i2], vdb[i2]], w=[bSb])
                        mk.op(mk.dve, lambda: nc.vector.scalar_tensor_tensor(out=stf[:, hh, dc, :], in0=stf[:, hh, dc, :],
                                                                             scalar=cdec[hh], in1=bS[:, :256],
                                                                             op0=ALU.mult, op1=ALU.add),
                              r=[bSb, stfb], w=[stfb])
                    self.copy(mk.act, stb[:, hh, :, :], stf[:, hh, :, :], r=[stfb], w=[stbb])
                Sx, Sxb = stat[j], statb[j]
                mk.op(mk.dve, lambda: nc.vector.tensor_reduce(out=Sx[:, 0:4], in_=O[:], axis=AX.X, op=ALU.add), r=[Ob], w=[Sxb])
                mk.op(mk.act, lambda: nc.scalar.activation(out=junk[:], in_=O[:], func=AF.Square), r=[Ob], w=[junkb])
                mk.op(mk.dve, lambda: nc.vector.tensor_reduce(out=Sx[:, 4:8], in_=junk[:], axis=AX.X, op=ALU.add), r=[junkb, Sxb], w=[Sxb])
                mk.op(mk.dve, lambda: nc.vector.tensor_scalar(out=Sx[:, 0:4], in0=Sx[:, 0:4], scalar1=1.0 / 256, scalar2=None,
                                                              op0=ALU.mult), r=[Sxb], w=[Sxb])
                mk.op(mk.dve, lambda: nc.vector.tensor_tensor(out=Sx[:, 8:12], in0=Sx[:, 0:4], in1=Sx[:, 0:4], op=ALU.mult),
                      r=[Sxb], w=[Sxb])
                mk.op(mk.dve, lambda: nc.vector.scalar_tensor_tensor(out=Sx[:, 8:12], in0=Sx[:, 4:8], scalar=1.0 / 256,
                                                                     in1=Sx[:, 8:12], op0=ALU.mult, op1=ALU.subtract),
                      r=[Sxb], w=[Sxb])
                self.rsqrt(Sx[:, 12:16], Sx[:, 8:12], 1e-5, [Sxb], [Sxb])
                for hh in range(4):
                    mk.op(mk.dve, lambda: nc.vector.tensor_scalar(out=O[:, hh, :], in0=O[:, hh, :], scalar1=Sx[:, hh:hh + 1],
                                                                  scalar2=Sx[:, 12 + hh:13 + hh], op0=ALU.subtract, op1=ALU.mult),
                          r=[Ob, Sxb], w=[Ob])
                Of = O[:].rearrange("p h d -> p (h d)")
                mk.op(mk.pool, lambda: nc.gpsimd.tensor_tensor(out=Of, in0=Of, in1=GN[:], op=ALU.mult), r=[Ob, cb], w=[Ob])
                mk.op(mk.dve, lambda: nc.vector.tensor_tensor(out=yo[j][:], in0=Of, in1=gt[j][:], op=ALU.mult),
                      r=[Ob, inb[j]], w=[yob[j]])
                mk.dma(mk.pool, mix_in[cs, 1024:2048], yo[j][:], r=[yob[j]])
            mk.barrier()


def _rel_bucket_np(dist):
    import jax
    import jax.numpy as jnp
    with jax.default_device(jax.devices("cpu")[0]):
        d = jnp.maximum(jnp.asarray(dist, jnp.int32), 0)
        max_exact = 16
        scaled = (jnp.log(jnp.maximum(d, max_exact).astype(jnp.float32) / max_exact)
                  / math.log(1024 / max_exact) * (32 - max_exact))
        large = jnp.minimum(max_exact + scaled.astype(jnp.int32), 31)
        return np.asarray(jnp.where(d < max_exact, d, large))


def nsa_index_tables(S):
    NQT = S // 128
    NCBP = S // 16
    i = np.arange(128)
    toep = np.zeros((128, 10, 128), np.int32)
    for dlt in range(9):
        dist = 128 * dlt + i[None, :] - i[:, None]
        b = _rel_bucket_np(dist)
        toep[:, dlt, :] = np.where(dist >= 0, b, 32)
    dist = 128 * 4 + i[None, :] - i[:, None]
    toep[:, 9, :] = np.where(dist < 512, _rel_bucket_np(dist), 32)
    OFF = 8 * (NQT - 1)
    RE = OFF + NCBP
    npp = np.arange(RE) - OFF
    dist = i[None, :] - 16 * npp[:, None] - 31
    cmpx = np.where(dist >= 0, _rel_bucket_np(dist), 32).astype(np.int32)
    return toep, cmpx, OFF


def nsa_const_tables(S):
    NSB = S // 64
    NCBP = S // 16
    NCB = NCBP - 1
    n = np.arange(NCBP)
    j = np.arange(NSB)
    cs, ce = n * 16, n * 16 + 31
    cover = ((cs[:, None] < j[None, :] * 64 + 64) & (ce[:, None] >= j[None, :] * 64)).astype(np.float32)
    cover[NCB:] = 0.0
    t = np.arange(S)
    cur = (t // 64)[:, None]
    forced = (j[None, :] == 0) | (j[None, :] == cur) | (j[None, :] == cur - 1)
    future = j[None, :] > cur
    keep = np.where(forced | future, 0.0, 1.0).astype(np.float32)
    add = np.where(future, -1e6 - j[None, :], np.where(forced, 1e6 + j[None, :], 0.0)).astype(np.float32)
    NQT = S // 128
    E = np.zeros((64, NQT, 128), np.float32)
    for kt in range(NQT):
        E[2 * kt, kt, :64] = 1.0
        E[2 * kt + 1, kt, 64:] = 1.0
    return cover, keep, add, E


class ProgNSA:
    def nsa_cmp(self, cmp_pos, w1, w2, k_gain0):
        nc, mk = self.nc, self.mk
        S = self.S
        NCBP = S // 16
        NCB = NCBP - 1
        NKT = max(1, NCBP // 128)
        kcmpT = self.scr("nsa_kcmpT", [2, 128, NCBP])
        vcmp = self.scr("nsa_vcmp", [2, NCBP, 128])
        srcs = (self.dram["nsa_kcT"], self.dram["nsa_vcT"])
        with ExitStack() as st:
            gk, gkb = self.load_col(st, k_gain0)
            hn = self.epi_fm_headnorm(st, None, 0, NCBP, gk, gkb)
            W1 = self.sbt(st, "cW1", [128, 32, 256], BF16)
            W2 = self.sbt(st, "cW2", [128, 2, 128], BF16)
            posf = self.sbt(st, "cposf", [32, 128], F32)
            posb = self.sbt(st, "cposb", [32, 128], BF16)
            posT = self.sbt(st, "cposT", [128, 32], BF16)
            xT = [self.sbt(st, "cx", [128, S], BF16) for _ in range(2)]
            hid = self.sbt(st, "chid", [128, 2, NCBP], BF16)
            bias = self.sbt(st, "cbias", [128, 2], F32)
            vo = self.sbt(st, "cvo", [128, NKT, 128], BF16)
            wb, pb_, xb, hb, bb, vob = Buf(), Buf(), [Buf(), Buf()], Buf(), Buf(), Buf()
            mk.op(mk.pool, lambda: nc.gpsimd.memset(hid[:], 0.0), w=[hb])
            xi = 0
            for kv in range(2):
                mk.dma(mk.sp, W1[:], w1[kv].rearrange("(l d) n -> d l n", d=128), r=self.wdep(w1[kv]), w=[wb])
                mk.dma(mk.sp, W2[:], w2[kv].rearrange("(c p) n -> p c n", p=128), r=self.wdep(w2[kv]), w=[wb])
                mk.dma(mk.sp, posf[:], cmp_pos[kv], w=[pb_])
                self.copy(mk.dve, posb[:], posf[:], r=[pb_], w=[pb_])
                bt, btb = self.ps()
                btv = bt[:].bitcast(BF16)
                mk.op(mk.pe, lambda: nc.tensor.transpose(out=btv[:, 0:32], in_=posb[:, :], identity=self.ident_bf[0][0:32, 0:32]),
                      r=[pb_, self.ident_bf[1]], w=[btb])
                self.copy(mk.dve, posT[:], btv[:, 0:32], r=[btb], w=[pb_])
                for hc in range(2):
                    bk, bkb = self.ps()
                    for l in range(32):
                        mk.op(mk.pe, lambda: nc.tensor.matmul(bk[:, 0:1], lhsT=W1[:, l, hc * 128:(hc + 1) * 128], rhs=posT[:, l:l + 1],
                                                              start=(l == 0), stop=(l == 31)), r=[wb, pb_], w=[bkb])
                    self.copy(mk.dve, bias[:, hc:hc + 1], bk[:, 0:1], r=[bkb], w=[bb])
                for g in range(2):
                    x, xbb = xT[xi % 2], xb[xi % 2]
                    xi += 1
                    mk.dma(mk.sp, x[:], srcs[kv][g * 128:(g + 1) * 128, :], w=[xbb])
                    for hc in range(2):
                        bk, bkb = self.ps()
                        for l in range(32):
                            mk.op(mk.pe, lambda: nc.tensor.matmul(bk[:, :NCB], lhsT=W1[:, l, hc * 128:(hc + 1) * 128],
                                                                  rhs=x[:, l:l + 16 * (NCB - 1) + 1:16],
                                                                  start=(l == 0), stop=(l == 31)), r=[wb, xbb], w=[bkb])
                        mk.op(mk.act, lambda: nc.scalar.activation(out=hid[:, hc, :NCB], in_=bk[:, :NCB], func=AF.Silu,
                                                                   bias=bias[:, hc:hc + 1]), r=[bkb, bb], w=[hb])
                    if kv == 0:
                        bk, bkb = self.ps()
                        for hc in range(2):
                            mk.op(mk.pe, lambda: nc.tensor.matmul(bk[:, :NCBP], lhsT=W2[:, hc, :], rhs=hid[:, hc, :],
                                                                  start=(hc == 0), stop=(hc == 1)), r=[wb, hb], w=[bkb])
                        hn([(bk[:, :NCBP], bkb, 0, 128)], 0,
                           store=lambda t, tb_, g=g: mk.dma(mk.pool, kcmpT[g], t[:], r=[tb_]))
                        self.flush()
                    else:
                        for nt in range(NKT):
                            bk, bkb = self.ps()
                            nn = min(128, NCBP)
                            for hc in range(2):
                                mk.op(mk.pe, lambda: nc.tensor.matmul(bk[:nn, :128], lhsT=hid[:, hc, nt * 128:nt * 128 + nn], rhs=W2[:, hc, :],
                                                                      start=(hc == 0), stop=(hc == 1)), r=[wb, hb], w=[bkb])
                            self.copy(self.evac_eng(), vo[:nn, nt, :], bk[:nn, :128], r=[bkb], w=[vob])
                        nn = min(128, NCBP)
                        mk.dma(mk.pool, vcmp[g].rearrange("(t p) d -> p t d", p=nn), vo[:nn, :, :], r=[vob])
            self.flush()
            mk.barrier()

    def nsa_attn(self, rel_toep, rel_cmpx, cmp_off, mix_in):
        nc, mk = self.nc, self.mk
        S = self.S
        NQT, NSB, NCBP = S // 128, S // 64, S // 16
        NKT = max(1, NCBP // 128)
        KP = min(128, NCBP)
        RW = 129 + NSB
        assert NSB >= 16 and NSB <= 64
        if "nsa_cover" not in self.dram:
            cover, keep, add, E = nsa_const_tables(S)
            self.const_dram("nsa_cover", cover.astype(ml_dtypes.bfloat16), BF16)
            self.const_dram("nsa_keep", keep, F32)
            self.const_dram("nsa_add", add, F32)
            self.const_dram("nsa_E", E.astype(ml_dtypes.bfloat16), BF16)
        qT, ksT, kwT, vs, vw, gate = (self.dram[k] for k in ("nsa_qT", "nsa_ksT", "nsa_kwT", "nsa_vs", "nsa_vw", "nsa_gate"))
        kcmpT, vcmp = self.dram["nsa_kcmpT"], self.dram["nsa_vcmp"]
        save_rot = self.rot
        self.rot = [0, 1, 2, 3]
        accs = [(self.pb[4 + i], self.pbb[4 + i]) for i in range(4)]
        with ExitStack() as st:
            KcT = self.sbt(st, "nKcT", [128, NCBP], BF16)
            VC = self.sbt(st, "nVC", [128, NKT, RW], BF16)
            KsT = self.sbt(st, "nKsT", [128, S], BF16)
            KwT = self.sbt(st, "nKwT", [128, S], BF16)
            Vs = self.sbt(st, "nVs", [128, NQT, 129], BF16)
            Vw = self.sbt(st, "nVw", [128, NQT, 129], BF16)
            TP = self.sbt(st, "nTP", [128, 10, 4, 128], BF16)
            E = self.sbt(st, "nE", [64, NQT, 128], BF16)
            resb = Buf()
            mk.dma(mk.sp, E[:], self.dram["nsa_E"], w=[resb])
            Q = [self.sbt(st, "nQ", [128, 4, 128], BF16) for _ in range(2)]
            G = [self.sbt(st, "nG", [128, 12], F32) for _ in range(2)]
            KA = [self.sbt(st, "nKA", [128, 2, NSB], F32) for _ in range(2)]
            CB = [self.sbt(st, "nCB", [128, NKT, 4, 128], BF16) for _ in range(2)]
            inb = [Buf(), Buf()]
            Pt = [self.sbt(st, "nP", [128, 512], BF16) for _ in range(4)]
            Ptb = [Buf() for _ in range(4)]
            pti = 0
            acc = [self.sbt(st, "nacc", [128, 4, 128], F32) for _ in range(2)]
            accb = [Buf(), Buf()]
            accC = [self.sbt(st, "naccC", [128, 4, RW], F32) for _ in range(2)]
            accCb = [Buf(), Buf()]
            accW = [self.sbt(st, "naccW", [128, 4, 129], F32) for _ in range(2)]
            accWb = [Buf(), Buf()]
            ob = [self.sbt(st, "nob", [128, 512], BF16) for _ in range(2)]
            obb = [Buf(), Buf()]
            sm = [self.sbt(st, "nsm", [128, 16], F32) for _ in range(2)]
            smb = [Buf(), Buf()]
            imp = [self.sbt(st, "nimp", [128, 3, NSB], F32) for _ in range(2)]
            impb = [Buf(), Buf()]
            m8 = [self.sbt(st, "nm8", [128, 16], F32) for _ in range(2)]
            mng = [self.sbt(st, "nmng", [128, NSB], BF16) for _ in range(2)]
            mT4 = [self.sbt(st, "nmT4", [64, 4, 128], BF16) for _ in range(2)]
            mT4b = [Buf(), Buf()]
            for g in range(2):
                mk.dma(mk.sp, KcT[:KP, :] if False else KcT[:, :], kcmpT[g], w=[resb])
                mk.op(mk.pool, lambda: nc.gpsimd.memset(VC[:], 1.0), w=[resb])
                mk.op(mk.pool, lambda: nc.gpsimd.memset(Vs[:], 1.0), w=[resb])
                mk.op(mk.pool, lambda: nc.gpsimd.memset(Vw[:], 1.0), w=[resb])
                mk.dma(mk.sp, VC[:KP, :, 0:128], vcmp[g].rearrange("(t p) d -> p t d", p=KP), w=[resb])
                mk.dma(mk.sp, VC[:KP, :, 129:RW], self.dram["nsa_cover"].rearrange("(t p) j -> p t j", p=KP), w=[resb])
                mk.dma(mk.sp, KsT[:], ksT[g * 128:(g + 1) * 128, :], w=[resb])
                mk.dma(mk.sp, KwT[:], kwT[g * 128:(g + 1) * 128, :], w=[resb])
                mk.dma(mk.sp, Vs[:, :, 0:128], vs[:, g * 128:(g + 1) * 128].rearrange("(t p) d -> p t d", p=128), w=[resb])
                mk.dma(mk.sp, Vw[:, :, 0:128], vw[:, g * 128:(g + 1) * 128].rearrange("(t p) d -> p t d", p=128), w=[resb])
                for ti in range(10):
                    mk.dma(mk.pool, TP[:, ti], rel_toep[:, ti, g * 4:(g + 1) * 4, :], w=[resb])
                for qt in range(NQT):
                    j = qt % 2
                    rows = slice(qt * 128, (qt + 1) * 128)
                    mk.dma(mk.sp, Q[j][:], qT[g * 512:(g + 1) * 512, rows].rearrange("(h p) s -> p h s", p=128), w=[inb[j]])
                    mk.dma(mk.sp, G[j][:], gate[rows, g * 12:(g + 1) * 12], w=[inb[j]])
                    mk.dma(mk.sp, KA[j][:, 0, :], self.dram["nsa_keep"][rows, :], w=[inb[j]])
                    mk.dma(mk.sp, KA[j][:, 1, :], self.dram["nsa_add"][rows, :], w=[inb[j]])
                    r0 = cmp_off - 8 * qt
                    mk.dma(mk.pool, CB[j][:KP], rel_cmpx[r0:r0 + NCBP, g * 4:(g + 1) * 4, :].rearrange("(t p) h i -> p t h i", p=KP),
                           w=[inb[j]])
                    Qf = Q[j][:].rearrange("p h s -> p (h s)")
                    A, Ab = acc[j], accb[j]
                    SM, SMb = sm[j], smb[j]

                    def branch(kts, klhs, extra, Vt, vsl, width, first):
                        nonlocal pti
                        banks = accs
                        def pv(ki, kt, pi):
                            for hh in range(4):
                                bk, bkb = banks[hh]
                                reg = bk[:, 0:width]
                                mk.op(mk.pe, lambda: nc.tensor.matmul(reg, lhsT=Pt[pi][:, hh * 128:(hh + 1) * 128], rhs=vsl(kt),
                                                                      start=(ki == 0), stop=(ki == len(kts) - 1)),
                                      r=[Ptb[pi], resb], w=[bkb])
                        prev = None
                        for ki, kt in enumerate(kts):
                            bank, bankb = self.ps()
                            mms = [(klhs(kt), Qf, [resb, inb[j]])] + extra(kt)
                            for mi, (l_, r_, bufs) in enumerate(mms):
                                mk.op(mk.pe, lambda: nc.tensor.matmul(bank[:, :512], lhsT=l_, rhs=r_, start=(mi == 0),
                                                                      stop=(mi == len(mms) - 1)), r=bufs, w=[bankb])
                            pi = pti % 4
                            pti += 1
                            mk.op(mk.act, lambda: nc.scalar.activation(out=Pt[pi][:], in_=bank[:, :512], func=AF.Exp),
                                  r=[bankb], w=[Ptb[pi]])
                            if prev is not None:
                                pv(*prev)
                            prev = (ki, kt, pi)
                        pv(*prev)
                        return banks

                    def evac(banks, width, dst, dstb):
                        for hh in range(4):
                            bk, bkb = banks[hh]
                            self.copy(mk.act, dst[:, hh, 0:width], bk[:, 0:width], r=[bkb], w=[dstb])

                    def sums(src, srcb, gi):
                        c0 = 4 * gi
                        mk.op(mk.dve, lambda: nc.vector.tensor_scalar(out=SM[:, c0:c0 + 4], in0=src[:, :, 128], scalar1=1e-30,
                                                                      scalar2=None, op0=ALU.max), r=[srcb], w=[SMb])
                        mk.op(mk.dve, lambda: nc.vector.reciprocal(out=SM[:, c0:c0 + 4], in_=SM[:, c0:c0 + 4]), r=[SMb], w=[SMb])
                        mk.op(mk.dve, lambda: nc.vector.tensor_tensor(out=SM[:, 12:16], in0=SM[:, c0:c0 + 4],
                                                                      in1=G[j][:].rearrange("p (h k) -> p h k", k=3)[:, :, gi], op=ALU.mult),
                              r=[SMb, inb[j]], w=[SMb])

                    def accum(src, srcb, first):
                        for hh in range(4):
                            if first:
                                mk.op(mk.dve, lambda: nc.vector.tensor_scalar(out=A[:, hh, :], in0=src[:, hh, 0:128],
                                                                              scalar1=SM[:, 12 + hh:13 + hh], scalar2=None,
                                                                              op0=ALU.mult), r=[srcb, SMb], w=[Ab])
                            else:
                                mk.op(mk.dve, lambda: nc.vector.scalar_tensor_tensor(out=A[:, hh, :], in0=src[:, hh, 0:128],
                                                                                     scalar=SM[:, 12 + hh:13 + hh], in1=A[:, hh, :],
                                                                                     op0=ALU.mult, op1=ALU.add),
                                      r=[srcb, SMb, Ab], w=[Ab])

                    assert KP == 128
                    kts_c = [kt for kt in range(NKT) if 8 * qt + 6 >= kt * 128]
                    banks = branch(kts_c, lambda kt: KcT[:, kt * KP:(kt + 1) * KP],
                                   lambda kt: [(self.ident_bf[0][:KP, :KP], CB[j][:KP, kt].rearrange("p h s -> p (h s)"),
                                                [self.ident_bf[1], inb[j]])],
                                   VC, lambda kt: VC[:KP, kt, :], RW, True)
                    AC, ACb = accC[j], accCb[j]
                    evac(banks, RW, AC, ACb)
                    sums(AC, ACb, 0)
                    IM, IMb = imp[j], impb[j]
                    for hh in range(4):
                        reg = AC[:, hh, 129:RW]
                        if hh == 0:
                            mk.op(mk.dve, lambda: nc.vector.tensor_scalar(out=IM[:, 0, :], in0=reg, scalar1=SM[:, 0:1], scalar2=None,
                                                                          op0=ALU.mult), r=[ACb, SMb], w=[IMb])
                        else:
                            mk.op(mk.dve, lambda: nc.vector.scalar_tensor_tensor(out=IM[:, 0, :], in0=reg, scalar=SM[:, hh:hh + 1],
                                                                                 in1=IM[:, 0, :], op0=ALU.mult, op1=ALU.add),
                                  r=[ACb, SMb, IMb], w=[IMb])
                    mk.op(mk.dve, lambda: nc.vector.tensor_tensor(out=IM[:, 0, :], in0=IM[:, 0, :], in1=KA[j][:, 0, :], op=ALU.mult),
                          r=[IMb, inb[j]], w=[IMb])
                    mk.op(mk.dve, lambda: nc.vector.tensor_tensor(out=IM[:, 0, :], in0=IM[:, 0, :], in1=KA[j][:, 1, :], op=ALU.add),
                          r=[IMb, inb[j]], w=[IMb])
                    M8 = m8[j]
                    mk.op(mk.dve, lambda: nc.vector.max(out=M8[:, 0:8], in_=IM[:, 0, :]), r=[IMb], w=[IMb])
                    mk.op(mk.dve, lambda: nc.vector.match_replace(out=IM[:, 1, :], in_to_replace=M8[:, 0:8], in_values=IM[:, 0, :],
                                                                  imm_value=-3.0e6), r=[IMb], w=[IMb])
                    mk.op(mk.dve, lambda: nc.vector.max(out=M8[:, 8:16], in_=IM[:, 1, :]), r=[IMb], w=[IMb])
                    mk.op(mk.dve, lambda: nc.vector.tensor_scalar(out=IM[:, 2, :], in0=IM[:, 0, :], scalar1=M8[:, 15:16], scalar2=None,
                                                                  op0=ALU.is_ge), r=[IMb], w=[IMb])
                    mk.op(mk.dve, lambda: nc.vector.tensor_scalar(out=mng[j][:], in0=IM[:, 2, :], scalar1=1.0, scalar2=BIG,
                                                                  op0=ALU.subtract, op1=ALU.mult), r=[IMb], w=[IMb])
                    accum(AC, ACb, True)
                    banks = branch(list(range(max(0, qt - 4), qt + 1)), lambda kt: KwT[:, kt * 128:(kt + 1) * 128],
                                   lambda kt: [(self.ident_bf[0][:], TP[:, (qt - kt) if qt - kt < 4 else 9].rearrange("p h s -> p (h s)"),
                                                [self.ident_bf[1], resb])],
                                   Vw, lambda kt: Vw[:, kt, :], 129, False)
                    AW, AWb = accW[j], accWb[j]
                    evac(banks, 129, AW, AWb)
                    bt, btb = self.ps()
                    btv = bt[:].bitcast(BF16)
                    mk.op(mk.pe, lambda: nc.tensor.transpose(out=btv[:NSB, 0:128], in_=mng[j][:, :], identity=self.ident_bf[0][:]),
                          r=[IMb, self.ident_bf[1]], w=[btb])
                    mk.op(mk.dve, lambda: nc.vector.tensor_copy(out=mT4[j][:NSB], in_=btv[:NSB, 0:128].unsqueeze(1).to_broadcast([NSB, 4, 128])),
                          r=[btb], w=[mT4b[j]])
                    mTf = mT4[j][:NSB].rearrange("p h s -> p (h s)")
                    sums(AW, AWb, 2)
                    accum(AW, AWb, False)
                    banks = branch(list(range(qt + 1)), lambda kt: KsT[:, kt * 128:(kt + 1) * 128],
                                   lambda kt: [(self.ident_bf[0][:], TP[:, min(qt - kt, 8)].rearrange("p h s -> p (h s)"),
                                                [self.ident_bf[1], resb]),
                                               (E[:NSB, kt, :], mTf, [resb, mT4b[j]])],
                                   Vs, lambda kt: Vs[:, kt, :], 129, False)
                    AS, ASb = accW[j], accWb[j]
                    evac(banks, 129, AS, ASb)
                    sums(AS, ASb, 1)
                    accum(AS, ASb, False)
                    self.copy(mk.act, ob[j][:], A[:].rearrange("p h d -> p (h d)"), r=[Ab], w=[obb[j]])
                    mk.dma(mk.pool, mix_in[rows, g * 512:(g + 1) * 512], ob[j][:], r=[obb[j]])
                mk.barrier()
        self.rot = save_rot


for _n, _f in list(ProgNSA.__dict__.items()):
    if callable(_f):
        setattr(Prog, _n, _f)


class ProgGDN:
    def gdn_in(self, h, ln, w_in, conv_w):
        nc, mk = self.nc, self.mk
        S = self.S
        qT = self.scr("gdn_qT", [2048, S])
        kT = self.scr("gdn_kT", [2048, S])
        vT = self.scr("gdn_vT", [4096, S])
        zs = self.scr("gdn_zs", [S, 4096])
        ba = self.scr("gdn_ba", [S, 64], F32)

        def jobs(st, TB):
            cw4 = self.sbt(st, "cw4", [4, 8192], F32)
            CW = self.sbt(st, "CW", [128, 64, 4], F32)
            cwb = Buf()
            mk.dma(mk.sp, cw4[:], conv_w, w=[cwb])
            for c0 in range(0, 64, 32):
                bk, bkb = self.ps()
                for c in range(c0, c0 + 32):
                    mk.op(mk.pe, lambda: nc.tensor.transpose(out=bk[:, (c - c0) * 4:(c - c0) * 4 + 4], in_=cw4[0:4, c * 128:(c + 1) * 128],
                                                             identity=self.ident_f[0][0:4, 0:4]), r=[cwb, self.ident_f[1]], w=[bkb])
                self.copy(mk.dve, CW[:, c0:c0 + 32, :], bk[:, 0:128].rearrange("p (c k) -> p c k", k=4), r=[bkb], w=[cwb])
            HL = self.sbt(st, "HL", [128, 64, 3], F32)
            hlb = Buf()
            mk.op(mk.pool, lambda: nc.gpsimd.memset(HL[:], 0.0), w=[hlb])
            tmp = [self.sbt(st, "cvt", [128, TB + 3], F32) for _ in range(2)]
            tmpb = [Buf(), Buf()]
            accs = [self.sbt(st, "cva", [128, TB], F32) for _ in range(2)]
            accb = [Buf(), Buf()]
            sil = [self.sbt(st, "cvs", [128, TB], F32) for _ in range(4)]
            silb = [Buf() for _ in range(4)]
            silh = [self.sbt(st, "cvsh", [128, TB], BF16) for _ in range(2)]
            silhb = [Buf(), Buf()]
            one = self.sbt(st, "onec", [128, 2], F32)
            oneb = Buf()
            mk.op(mk.pool, lambda: nc.gpsimd.memset(one[:, 0:1], 128 ** -0.5), w=[oneb])
            mk.op(mk.pool, lambda: nc.gpsimd.memset(one[:, 1:2], 1.0), w=[oneb])
            hn = self.epi_fm_headnorm(st, None, 0, TB, one, oneb, mean=False, eps=EPS)
            cnt = [0]

            def conv(grp, tb):
                (ps, psb, col, cw), = grp
                c = col // 128
                j = cnt[0] % 2
                cnt[0] += 1
                T, Tb = tmp[j], tmpb[j]
                self.copy(mk.act, T[:, 3:TB + 3], ps, r=[psb], w=[Tb])

                def stage2():
                    mk.op(mk.dve, lambda: nc.vector.tensor_copy(out=T[:, 0:3], in_=HL[:, c, :]), r=[hlb], w=[Tb])
                    mk.op(mk.dve, lambda: nc.vector.tensor_copy(out=HL[:, c, :], in_=T[:, TB:TB + 3]), r=[Tb], w=[hlb])
                    A, Ab = accs[j], accb[j]
                    mk.op(mk.dve, lambda: nc.vector.tensor_scalar(out=A[:], in0=T[:, 3:TB + 3], scalar1=CW[:, c, 3:4], scalar2=None,
                                                                  op0=ALU.mult), r=[Tb, cwb], w=[Ab])
                    for k in (2, 1, 0):
                        mk.op(mk.dve, lambda: nc.vector.scalar_tensor_tensor(out=A[:], in0=T[:, k:k + TB], scalar=CW[:, c, k:k + 1], in1=A[:],
                                                                             op0=ALU.mult, op1=ALU.add), r=[Tb, cwb, Ab], w=[Ab])

                    def stage3():
                        if c < 32:
                            j4 = c % 4
                            mk.op(mk.act, lambda: nc.scalar.activation(out=sil[j4][:], in_=A[:], func=AF.Silu), r=[Ab], w=[silb[j4]])
                            dst = qT if c < 16 else kT
                            r0 = (c % 16) * 128
                            hn([(sil[j4][:], silb[j4], 0, 128)], tb, gi=(0 if c < 16 else 1), deep=True,
                               store=lambda t, tb_: mk.dma(mk.pool, dst[r0:r0 + 128, tb * TB:(tb + 1) * TB], t[:], r=[tb_]))
                        else:
                            mk.op(mk.act, lambda: nc.scalar.activation(out=silh[j][:], in_=A[:], func=AF.Silu), r=[Ab], w=[silhb[j]])
                            r0 = (c - 32) * 128
                            mk.dma(mk.pool, vT[r0:r0 + 128, tb * TB:(tb + 1) * TB], silh[j][:], r=[silhb[j]])
                    self.defer(stage3)
                self.defer(stage2)
            return [dict(W=w_in, c0=0, c1=8192, layout="FM", epi=conv),
                    dict(W=w_in, c0=8192, c1=12288, layout="TM", epi=self.epi_tm_store(st, zs, -8192, TB, func=AF.Silu)),
                    dict(W=w_in, c0=12288, c1=12352, layout="TM", epi=self.epi_tm_store(st, ba, -12288, TB, dt=F32, width=64))]
        self.linear("gdn", h, S, D, ln, jobs)

    def gdn_gates(self, a_log, dt_bias):
        nc, mk = self.nc, self.mk
        S = self.S
        ba = self.dram["gdn_ba"]
        gt = self.scr("gdn_gt", [S, 5, 32], F32)
        X = self.scr("gdn_X", [S, 3, 64], F32)
        if "gdn_tri" not in self.dram:
            m = np.arange(128)
            self.const_dram("gdn_tri", (m[:, None] <= m[None, :]).astype(np.float32), F32)
            l = np.zeros((128, 128), np.float32)
            l[127, :] = 1.0
            self.const_dram("gdn_l127", l, F32)
        with ExitStack() as st:
            TRI = self.sbt(st, "gTRI", [128, 128], F32)
            L127 = self.sbt(st, "gL127", [128, 128], F32)
            AL = self.sbt(st, "gAL", [128, 32], F32)
            DTB = self.sbt(st, "gDTB", [128, 32], F32)
            cb = Buf()
            mk.dma(mk.sp, TRI[:], self.dram["gdn_tri"], w=[cb])
            mk.dma(mk.sp, L127[:], self.dram["gdn_l127"], w=[cb])
            mk.dma(mk.sp, AL[:], a_log.partition_broadcast(128), w=[cb])
            mk.dma(mk.sp, DTB[:], dt_bias.partition_broadcast(128), w=[cb])
            mk.op(mk.act, lambda: nc.scalar.activation(out=AL[:], in_=AL[:], func=AF.Exp), r=[cb], w=[cb])
            inn = [self.sbt(st, "gin", [128, 64], F32) for _ in range(2)]
            innb = [Buf(), Buf()]
            W = [self.sbt(st, "gW", [128, 8, 32], F32) for _ in range(2)]
            Wb = [Buf(), Buf()]
            GT = [self.sbt(st, "gGT", [128, 5, 32], F32) for _ in range(2)]
            GTb = [Buf(), Buf()]
            XX = [self.sbt(st, "gXX", [128, 3, 64], F32) for _ in range(2)]
            XXb = [Buf(), Buf()]
            for n in range(S // 128):
                j = n % 2
                rows = slice(n * 128, (n + 1) * 128)
                I, Ib, w, wb, G, Gb, Xt, Xb = inn[j], innb[j], W[j], Wb[j], GT[j], GTb[j], XX[j], XXb[j]
                mk.dma(mk.sp, I[:], ba[rows, :], w=[Ib])
                mk.op(mk.act, lambda: nc.scalar.activation(out=w[:, 0, :], in_=I[:, 0:32], func=AF.Exp, scale=-1.0), r=[Ib], w=[wb])
                mk.op(mk.dve, lambda: nc.vector.tensor_scalar(out=w[:, 0, :], in0=w[:, 0, :], scalar1=1.0, scalar2=None, op0=ALU.add),
                      r=[wb], w=[wb])
                mk.op(mk.dve, lambda: nc.vector.reciprocal(out=G[:, 0, :], in_=w[:, 0, :]), r=[wb], w=[Gb])
                mk.op(mk.act, lambda: nc.scalar.activation(out=w[:, 1, :], in_=w[:, 0, :], func=AF.Ln), r=[wb], w=[wb])
                mk.op(mk.dve, lambda: nc.vector.tensor_tensor(out=w[:, 2, :], in0=I[:, 32:64], in1=DTB[:], op=ALU.add), r=[Ib, cb], w=[wb])
                mk.op(mk.act, lambda: nc.scalar.activation(out=w[:, 2, :], in_=w[:, 2, :], func=AF.Exp), r=[wb], w=[wb])
                mk.op(mk.dve, lambda: nc.vector.tensor_scalar(out=w[:, 2, :], in0=w[:, 2, :], scalar1=1.0, scalar2=None, op0=ALU.add),
                      r=[wb], w=[wb])
                mk.op(mk.act, lambda: nc.scalar.activation(out=w[:, 2, :], in_=w[:, 2, :], func=AF.Ln), r=[wb], w=[wb])
                mk.op(mk.dve, lambda: nc.vector.scalar_tensor_tensor(out=w[:, 3, :], in0=w[:, 2, :], scalar=-1.0, in1=AL[:],
                                                                     op0=ALU.mult, op1=ALU.mult), r=[wb, cb], w=[wb])
                bk, bkb = self.ps()
                mk.op(mk.pe, lambda: nc.tensor.matmul(bk[:, 0:32], lhsT=TRI[:], rhs=w[:, 3, :], start=True, stop=True), r=[cb, wb], w=[bkb])
                self.copy(mk.dve, w[:, 4, :], bk[:, 0:32], r=[bkb], w=[wb])
                bk2, bk2b = self.ps()
                mk.op(mk.pe, lambda: nc.tensor.matmul(bk2[:, 0:32], lhsT=L127[:], rhs=w[:, 4, :], start=True, stop=True), r=[cb, wb], w=[bk2b])
                self.copy(mk.dve, w[:, 5, :], bk2[:, 0:32], r=[bk2b], w=[wb])
                mk.op(mk.act, lambda: nc.scalar.activation(out=G[:, 1, :], in_=w[:, 4, :], func=AF.Exp), r=[wb], w=[Gb])
                mk.op(mk.dve, lambda: nc.vector.tensor_tensor(out=G[:, 2, :], in0=G[:, 0, :], in1=G[:, 1, :], op=ALU.mult), r=[Gb], w=[Gb])
                mk.op(mk.dve, lambda: nc.vector.tensor_tensor(out=w[:, 6, :], in0=w[:, 5, :], in1=w[:, 4, :], op=ALU.subtract), r=[wb], w=[wb])
                mk.op(mk.act, lambda: nc.scalar.activation(out=G[:, 3, :], in_=w[:, 6, :], func=AF.Exp), r=[wb], w=[Gb])
                mk.op(mk.act, lambda: nc.scalar.activation(out=G[:, 4, :], in_=w[:, 5, :], func=AF.Exp), r=[wb], w=[Gb])
                mk.dma(mk.pool, gt[rows], G[:], r=[Gb])
                mk.op(mk.pool, lambda: nc.gpsimd.memset(Xt[:], 1.0), w=[Xb])
                mk.op(mk.dve, lambda: nc.vector.tensor_scalar(out=Xt[:, 0, 0:32], in0=w[:, 4, :], scalar1=-1.0, scalar2=None, op0=ALU.mult),
                      r=[wb], w=[Xb])
                self.copy(mk.dve, Xt[:, 1, 32:64], w[:, 4, :], r=[wb], w=[Xb])
                mk.op(mk.dve, lambda: nc.vector.tensor_tensor(out=Xt[:, 2, 32:64], in0=w[:, 4, :], in1=w[:, 1, :], op=ALU.subtract),
                      r=[wb], w=[Xb])
                mk.dma(mk.pool, X[rows], Xt[:], r=[Xb])
            mk.barrier()

    def gdn_chunks(self, norm_gain, gdn_o):
        nc, mk = self.nc, self.mk
        S = self.S
        NCH = S // 128
        F32R = F32
        qT, kT, vT, zs, gt, X = (self.dram[k] for k in ("gdn_qT", "gdn_kT", "gdn_vT", "gdn_zs", "gdn_gt", "gdn_X"))
        if "gdn_mstrict" not in self.dram:
            m = np.arange(128)
            ms = np.where(m[None, :] > m[:, None], 0.0, -1e4).astype(np.float32)
            mi = np.where(m[None, :] >= m[:, None], 0.0, -1e4).astype(np.float32)
            self.const_dram("gdn_mstrict", np.tile(ms[:, None, :], (1, 4, 1)), F32)
            self.const_dram("gdn_mincl", np.tile(mi[:, None, :], (1, 4, 1)), F32)
            oh = np.zeros((64, 32), np.float32)
            for k in range(64):
                oh[k, k % 32] = 1.0
            self.const_dram("gdn_oh", oh, F32)
            self.const_dram("gdn_i4", np.tile(np.eye(128, dtype=np.float32)[:, None, :], (1, 4, 1)), F32)
        with ExitStack() as st:
            MS = self.sbt(st, "dMS", [128, 4, 128], F32)
            MI = self.sbt(st, "dMI", [128, 4, 128], F32)
            OH = self.sbt(st, "dOH", [64, 32], F32)
            I4 = self.sbt(st, "dI4", [128, 4, 128], F32)
            NG = self.sbt(st, "dNG", [128, 128], F32)
            cb = Buf()
            mk.dma(mk.sp, MS[:], self.dram["gdn_mstrict"], w=[cb])
            mk.dma(mk.sp, MI[:], self.dram["gdn_mincl"], w=[cb])
            mk.dma(mk.sp, OH[:], self.dram["gdn_oh"], w=[cb])
            mk.dma(mk.sp, I4[:], self.dram["gdn_i4"], w=[cb])
            mk.dma(mk.sp, NG[:], norm_gain.partition_broadcast(128), w=[cb])
            Sf = self.sbt(st, "dSf", [128, 32, 128], F32)
            Sb = self.sbt(st, "dSb", [128, 32, 128], BF16)
            Sfb = [Buf() for _ in range(8)]
            Sbb = [Buf() for _ in range(8)]
            mk.op(mk.pool, lambda: nc.gpsimd.memset(Sf[:], 0.0), w=Sfb)
            mk.op(mk.pool, lambda: nc.gpsimd.memset(Sb[:], 0.0), w=Sbb)
            GTt = [self.sbt(st, "dGT", [128, 5, 32], F32) for _ in range(2)]
            XXt = [self.sbt(st, "dXX", [128, 3, 64], F32) for _ in range(2)]
            XT = [self.sbt(st, "dXT", [64, 3, 128], F32) for _ in range(2)]
            LA = [self.sbt(st, "dLA", [64, 32, 128], F32) for _ in range(2)]
            chb = [Buf(), Buf()]
            NS = 4

            def two(name, shape, dt):
                return [self.sbt(st, name, shape, dt) for _ in range(NS)], [Buf() for _ in range(NS)]
            kTt, kTb = two("dkT", [128, 2, 128], BF16)
            qTt, qTb = two("dqT", [128, 2, 128], BF16)
            vTt, vTb = two("dvT", [128, 4, 128], BF16)
            zst, zsb = two("dzs", [128, 512], BF16)
            vbt, vbb = two("dvb", [128, 4, 128], BF16)
            kbg, kbgb = two("dkbg", [128, 4, 128], BF16)
            kdc, kdcb = two("dkdc", [128, 4, 128], BF16)
            EYt, EYb = two("dEY", [128, 4, 128], F32)
            EAt, EAb = two("dEA", [128, 4, 128], F32)
            YPt = [[self.sbt(st, "dYP", [128, 4, 2, 128], F32R) for _ in range(2)] for _ in range(NS)]
            YTt = [[self.sbt(st, "dYT", [128, 5, 128], F32R) for _ in range(2)] for _ in range(NS)]
            YPb = [[Buf(), Buf()] for _ in range(NS)]
            YTb = [[Buf(), Buf()] for _ in range(NS)]
            for sl in range(NS):
                for pp in range(2):
                    self.copy(mk.dve, YTt[sl][pp][:], Sf[:, 0:5, :], r=[Sfb[0]], w=[YTb[sl][pp]])
            atT, atTb = two("datT", [128, 4, 128], BF16)
            TT, TTb = two("dTT", [128, 4, 128], BF16)
            usb, usbb = two("du", [128, 4, 128], F32)
            wT, wTb = two("dwT", [128, 4, 128], BF16)
            vnw, vnwb = two("dvn", [128, 4, 128], BF16)
            osb, osbb = two("do", [128, 4, 128], F32)
            junk, junkb = two("djunk", [128, 4, 128], F32)
            stt_, sttb = two("dst", [128, 8], F32)
            yo, yob = two("dyo", [128, 512], BF16)

            def f4(t):
                return t[:].rearrange("p h c -> p (h c)")

            class Ctx:
                pass

            def ph_load(c):
                j, hg, rows = c.j, c.hg, c.rows
                mk.dma(mk.sp, kTt[j][:], kT[hg * 256:(hg + 1) * 256, rows].rearrange("(h p) s -> p h s", p=128), w=[kTb[j]])
                mk.dma(mk.sp, qTt[j][:], qT[hg * 256:(hg + 1) * 256, rows].rearrange("(h p) s -> p h s", p=128), w=[qTb[j]])
                mk.dma(mk.sp, vTt[j][:], vT[hg * 512:(hg + 1) * 512, rows].rearrange("(h p) s -> p h s", p=128), w=[vTb[j]])
                mk.dma(mk.sp, zst[j][:], zs[rows, hg * 512:(hg + 1) * 512], w=[zsb[j]])
                bk, bkb = self.ps()
                bv = bk[:].bitcast(BF16)
                for qh in range(2):
                    mk.op(mk.pe, lambda: nc.tensor.transpose(out=bv[:, qh * 128:(qh + 1) * 128], in_=kTt[j][:, qh, :],
                                                             identity=self.ident_bf[0][:]), r=[kTb[j], self.ident_bf[1]], w=[bkb])
                for hh in range(4):
                    mk.op(mk.pe, lambda: nc.tensor.transpose(out=bv[:, (2 + hh) * 128:(3 + hh) * 128], in_=vTt[j][:, hh, :],
                                                             identity=self.ident_bf[0][:]), r=[vTb[j], self.ident_bf[1]], w=[bkb])
                G_ = c.GT_
                H0 = c.H0
                kv4 = bv[:, 0:256].rearrange("p (h d) -> p h d", d=128).unsqueeze(2).to_broadcast([128, 2, 2, 128])
                vv4 = bv[:, 256:768].rearrange("p (h d) -> p h d", d=128)

                def sc(i):
                    return G_[:, i, H0:H0 + 4]
                mk.op(mk.dve, lambda: nc.vector.tensor_tensor(out=vbt[j][:], in0=vv4, in1=sc(0).unsqueeze(2).to_broadcast([128, 4, 128]),
                                                              op=ALU.mult), r=[bkb, c.cbb], w=[vbb[j]])
                mk.op(mk.dve, lambda: nc.vector.tensor_tensor(out=kbg[j][:].rearrange("p (a b) d -> p a b d", b=2), in0=kv4,
                                                              in1=sc(2).rearrange("p (a b) -> p a b", b=2).unsqueeze(3).to_broadcast([128, 2, 2, 128]),
                                                              op=ALU.mult), r=[bkb, c.cbb], w=[kbgb[j]])
                mk.op(mk.dve, lambda: nc.vector.tensor_tensor(out=kdc[j][:].rearrange("p (a b) d -> p a b d", b=2), in0=kv4,
                                                              in1=sc(3).rearrange("p (a b) -> p a b", b=2).unsqueeze(3).to_broadcast([128, 2, 2, 128]),
                                                              op=ALU.mult), r=[bkb, c.cbb], w=[kdcb[j]])

            def ph_gram1(c):
                j = c.j
                c.bDY, c.bDYb = self.ps()
                c.bDA, c.bDAb = self.ps()
                for hh in range(4):
                    mk.op(mk.pe, lambda: nc.tensor.matmul(c.bDY[:, hh * 128:(hh + 1) * 128], lhsT=c.LA_[:, c.H0 + hh, :], rhs=c.XT_[:, 2, :],
                                                          start=True, stop=True), r=[c.cbb], w=[c.bDYb])
                    mk.op(mk.pe, lambda: nc.tensor.matmul(c.bDA[:, hh * 128:(hh + 1) * 128], lhsT=c.LA_[:, c.H0 + hh, :], rhs=c.XT_[:, 1, :],
                                                          start=True, stop=True), r=[c.cbb], w=[c.bDAb])

            def ph_gram2(c):
                j = c.j
                EY, EA = EYt[j], EAt[j]
                mk.op(mk.dve, lambda: nc.vector.scalar_tensor_tensor(out=f4(EY), in0=c.bDY[:, :], scalar=0.0, in1=f4(MS),
                                                                     op0=ALU.min, op1=ALU.add), r=[c.bDYb, cb], w=[EYb[j]])
                mk.op(mk.dve, lambda: nc.vector.scalar_tensor_tensor(out=f4(EA), in0=c.bDA[:, :], scalar=0.0, in1=f4(MI),
                                                                     op0=ALU.min, op1=ALU.add), r=[c.bDAb, cb], w=[EAb[j]])

            def ph_gram3(c):
                j = c.j
                EY, EA = EYt[j], EAt[j]
                mk.op(mk.act, lambda: nc.scalar.activation(out=f4(EY), in_=f4(EY), func=AF.Exp), r=[EYb[j]], w=[EYb[j]])
                mk.op(mk.act, lambda: nc.scalar.activation(out=f4(EA), in_=f4(EA), func=AF.Exp), r=[EAb[j]], w=[EAb[j]])
                self.copy(mk.act, YPt[j][0][:, :, 1, :], I4[:], r=[cb], w=[YPb[j][0]])
                c.bG, c.bGb = self.ps()
                bG, bGb = c.bG, c.bGb
                for qh in range(2):
                    mk.op(mk.pe, lambda: nc.tensor.matmul(bG[:, qh * 128:(qh + 1) * 128], lhsT=kTt[j][:, qh, :], rhs=kTt[j][:, qh, :],
                                                          start=True, stop=True), r=[kTb[j]], w=[bGb])
                    mk.op(mk.pe, lambda: nc.tensor.matmul(bG[:, (2 + qh) * 128:(3 + qh) * 128], lhsT=kTt[j][:, qh, :], rhs=qTt[j][:, qh, :],
                                                          start=True, stop=True), r=[kTb[j], qTb[j]], w=[bGb])

            def ph_gram4(c):
                j = c.j
                EY, EA = EYt[j], EAt[j]
                bG, bGb = c.bG, c.bGb
                YP0 = YPt[j][0]
                for qh in range(2):
                    Gv = bG[:, qh * 128:(qh + 1) * 128].unsqueeze(1).to_broadcast([128, 2, 128])
                    mk.op(mk.dve, lambda: nc.vector.scalar_tensor_tensor(out=YP0[:, 2 * qh:2 * qh + 2, 0, :], in0=Gv, scalar=-1.0,
                                                                         in1=EY[:, 2 * qh:2 * qh + 2, :], op0=ALU.mult, op1=ALU.mult),
                          r=[bGb, EYb[j]], w=[YPb[j][0]])
                    Av = bG[:, (2 + qh) * 128:(3 + qh) * 128].unsqueeze(1).to_broadcast([128, 2, 128])
                    mk.op(mk.dve, lambda: nc.vector.tensor_tensor(out=atT[j][:, 2 * qh:2 * qh + 2, :], in0=Av, in1=EA[:, 2 * qh:2 * qh + 2, :],
                                                                  op=ALU.mult), r=[bGb, EAb[j]], w=[atTb[j]])

            def ph_gram5(c):
                j = c.j
                YP0 = YPt[j][0]
                bk, bkb = self.ps()
                for hh in range(4):
                    mk.op(mk.pe, lambda: nc.tensor.transpose(out=bk[:, hh * 128:(hh + 1) * 128], in_=YP0[:, hh, 0, :].bitcast(F32), identity=self.ident_f[0][:]),
                          r=[YPb[j][0], self.ident_f[1]], w=[bkb])
                self.copy(mk.act, YTt[j][0][:, 0:4, :].rearrange("p h c -> p (h c)"), bk[:, :], r=[bkb], w=[YTb[j][0]])
                c.cur = 0

            def ph_level(c, lvl):
                j = c.j
                cur, nxt = c.cur, 1 - c.cur
                YPc, YTc, YPn, YTn = YPt[j][cur], YTt[j][cur], YPt[j][nxt], YTt[j][nxt]
                first, last = (lvl == 0), (lvl == 6)
                b1 = [self.ps(), self.ps()]
                for hh in range(4):
                    bk, bkb = b1[hh // 2]
                    mk.op(mk.pe, lambda: nc.tensor.matmul(bk[:, (hh % 2) * 256:(hh % 2) * 256 + 256], lhsT=YTc[:, hh, :],
                                                          rhs=YPc[:, hh, :, :].rearrange("p a c -> p (a c)"),
                                                          start=True, stop=True), r=[YPb[j][cur], YTb[j][cur]], w=[bkb])
                if not last:
                    b2, b2b = self.ps()
                    for hh in range(4):
                        mk.op(mk.pe, lambda: nc.tensor.matmul(b2[:, hh * 128:(hh + 1) * 128], lhsT=YPc[:, hh, 0, :], rhs=YTc[:, hh, :],
                                                              start=True, stop=True), r=[YPb[j][cur], YTb[j][cur]], w=[b2b])
                    self.copy(mk.act, YTn[:, 0:4, :].rearrange("p h c -> p (h c)"), b2[:, :], r=[b2b], w=[YTb[j][nxt]])
                for half in range(2):
                    bk, bkb = b1[half]
                    v = bk[:, :].rearrange("p (h a c) -> p h a c", a=2, c=128)
                    hs = slice(2 * half, 2 * half + 2)
                    if not last:
                        self.copy(mk.dve, YPn[:, hs, 0, :], v[:, :, 0, :], r=[bkb], w=[YPb[j][nxt]])
                    if not last:
                        mk.op(mk.dve, lambda: nc.vector.tensor_tensor(out=YPn[:, hs, 1, :], in0=v[:, :, 1, :], in1=YPc[:, hs, 1, :].bitcast(F32), op=ALU.add),
                              r=[bkb, YPb[j][cur]], w=[YPb[j][nxt]])
                    else:
                        mk.op(mk.dve, lambda: nc.vector.tensor_tensor(out=TT[j][:, hs, :], in0=v[:, :, 1, :], in1=YPc[:, hs, 1, :].bitcast(F32), op=ALU.add),
                              r=[bkb, YPb[j][cur]], w=[TTb[j]])
                c.cur = nxt

            def ph_uw(c):
                j = c.j
                bU, bUb = self.ps()
                bW, bWb = self.ps()
                for hh in range(4):
                    mk.op(mk.pe, lambda: nc.tensor.matmul(bU[:, hh * 128:(hh + 1) * 128], lhsT=TT[j][:, hh, :], rhs=vbt[j][:, hh, :],
                                                          start=True, stop=True), r=[TTb[j], vbb[j]], w=[bUb])
                    mk.op(mk.pe, lambda: nc.tensor.matmul(bW[:, hh * 128:(hh + 1) * 128], lhsT=kbg[j][:, hh, :], rhs=TT[j][:, hh, :],
                                                          start=True, stop=True), r=[TTb[j], kbgb[j]], w=[bWb])
                self.copy(mk.act, f4(usb[j]), bU[:, :], r=[bUb], w=[usbb[j]])
                self.copy(mk.dve, f4(wT[j]), bW[:, :], r=[bWb], w=[wTb[j]])

            def ph_state1(c):
                j, hg, H0 = c.j, c.hg, c.H0
                bWS, bWSb = self.ps()
                for hh in range(4):
                    mk.op(mk.pe, lambda: nc.tensor.matmul(bWS[:, hh * 128:(hh + 1) * 128], lhsT=wT[j][:, hh, :], rhs=Sb[:, H0 + hh, :],
                                                          start=True, stop=True), r=[wTb[j], Sbb[hg]], w=[bWSb])
                mk.op(mk.dve, lambda: nc.vector.tensor_tensor(out=f4(vnw[j]), in0=f4(usb[j]), in1=bWS[:, :], op=ALU.subtract),
                      r=[usbb[j], bWSb], w=[vnwb[j]])

            def ph_s2a(c):
                j, hg, H0 = c.j, c.hg, c.H0
                c.bAV, c.bAVb = self.ps()
                c.bDS, c.bDSb = self.ps()
                for hh in range(4):
                    mk.op(mk.pe, lambda: nc.tensor.matmul(c.bDS[:, hh * 128:(hh + 1) * 128], lhsT=kdc[j][:, hh, :], rhs=vnw[j][:, hh, :],
                                                          start=True, stop=True), r=[kdcb[j], vnwb[j]], w=[c.bDSb])
                for hh in range(4):
                    mk.op(mk.pe, lambda: nc.tensor.matmul(c.bAV[:, hh * 128:(hh + 1) * 128], lhsT=atT[j][:, hh, :], rhs=vnw[j][:, hh, :],
                                                          start=True, stop=True), r=[atTb[j], vnwb[j]], w=[c.bAVb])

            def ph_s2b(c):
                j, hg, H0 = c.j, c.hg, c.H0
                G_ = c.GT_
                Sv = Sf[:, H0:H0 + 4, :]
                mk.op(mk.dve, lambda: nc.vector.tensor_tensor(out=Sv, in0=Sv, in1=G_[:, 4, H0:H0 + 4].unsqueeze(2).to_broadcast([128, 4, 128]),
                                                              op=ALU.mult), r=[c.cbb, Sfb[hg]], w=[Sfb[hg]])
                mk.op(mk.dve, lambda: nc.vector.tensor_tensor(out=Sv, in0=Sv, in1=c.bDS[:, :].rearrange("p (h c) -> p h c", c=128), op=ALU.add),
                      r=[c.bDSb, Sfb[hg]], w=[Sfb[hg]])
                O, Ob = osb[j], osbb[j]
                self.copy(mk.act, f4(O), c.bAV[:, :], r=[c.bAVb], w=[Ob])
                c.bQS, c.bQSb = self.ps()
                for hh in range(4):
                    mk.op(mk.pe, lambda: nc.tensor.matmul(c.bQS[:, hh * 128:(hh + 1) * 128], lhsT=qTt[j][:, hh // 2, :], rhs=Sb[:, H0 + hh, :],
                                                          start=True, stop=True), r=[qTb[j], Sbb[hg]], w=[c.bQSb])

            def ph_s2c(c):
                j, hg, H0 = c.j, c.hg, c.H0
                G_ = c.GT_
                O, Ob = osb[j], osbb[j]
                self.copy(mk.act, Sb[:, H0:H0 + 4, :], Sf[:, H0:H0 + 4, :], r=[Sfb[hg]], w=[Sbb[hg]])
                J, Jb = junk[j], junkb[j]
                mk.op(mk.dve, lambda: nc.vector.tensor_tensor(out=J[:], in0=c.bQS[:, :].rearrange("p (h c) -> p h c", c=128),
                                                              in1=G_[:, 1, H0:H0 + 4].unsqueeze(2).to_broadcast([128, 4, 128]), op=ALU.mult),
                      r=[c.bQSb, c.cbb], w=[Jb])
                mk.op(mk.dve, lambda: nc.vector.tensor_tensor(out=O[:], in0=O[:], in1=J[:], op=ALU.add), r=[Ob, Jb], w=[Ob])

            def ph_s2d(c):
                j = c.j
                O, Ob = osb[j], osbb[j]
                mk.op(mk.act, lambda: nc.scalar.activation(out=junk[j][:], in_=O[:], func=AF.Square, scale=128 ** -0.5), r=[Ob], w=[junkb[j]])

            def ph_s2e(c):
                j = c.j
                ST, STb = stt_[j], sttb[j]
                mk.op(mk.dve, lambda: nc.vector.tensor_reduce(out=ST[:, 0:4], in_=junk[j][:], axis=AX.X, op=ALU.add), r=[junkb[j]], w=[STb])

            def ph_s2f(c):
                j = c.j
                ST, STb = stt_[j], sttb[j]
                self.rsqrt(ST[:, 4:8], ST[:, 0:4], EPS, [STb], [STb])

            def ph_s2g(c):
                j, hg, rows = c.j, c.hg, c.rows
                O, Ob = osb[j], osbb[j]
                ST, STb = stt_[j], sttb[j]
                mk.op(mk.dve, lambda: nc.vector.tensor_tensor(out=O[:], in0=O[:], in1=ST[:, 4:8].unsqueeze(2).to_broadcast([128, 4, 128]), op=ALU.mult),
                      r=[Ob, STb], w=[Ob])
                mk.op(mk.dve, lambda: nc.vector.tensor_tensor(out=O[:], in0=O[:], in1=NG[:].unsqueeze(1).to_broadcast([128, 4, 128]), op=ALU.mult),
                      r=[Ob, cb], w=[Ob])
                mk.op(mk.pool, lambda: nc.gpsimd.tensor_tensor(out=yo[j][:], in0=f4(O), in1=zst[j][:], op=ALU.mult),
                      r=[Ob, zsb[j]], w=[yob[j]])
                mk.dma(mk.pool, gdn_o[rows, hg * 512:(hg + 1) * 512], yo[j][:], r=[yob[j]])

            phases = ([ph_load, ph_gram1, ph_gram2, ph_gram3, ph_gram4, ph_gram5] + [(lambda c, l=l: ph_level(c, l)) for l in range(7)]
                      + [ph_uw, ph_state1, ph_s2a, ph_s2b, ph_s2c, ph_s2d, ph_s2e, ph_s2f, ph_s2g])
            for n in range(NCH):
                cj = n % 2
                rows = slice(n * 128, (n + 1) * 128)
                GT_, XX_, XT_, LA_, cbb = GTt[cj], XXt[cj], XT[cj], LA[cj], chb[cj]
                mk.dma(mk.sp, GT_[:], gt[rows], w=[cbb])
                mk.dma(mk.sp, XX_[:], X[rows], w=[cbb])
                bk, bkb = self.ps()
                for i in range(3):
                    mk.op(mk.pe, lambda: nc.tensor.transpose(out=bk[:64, i * 128:(i + 1) * 128], in_=XX_[:, i, :], identity=self.ident_f[0][:]),
                          r=[cbb, self.ident_f[1]], w=[bkb])
                self.copy(mk.dve, XT_[:], bk[:64, 0:384].rearrange("p (i c) -> p i c", c=128), r=[bkb], w=[cbb])
                mk.op(mk.dve, lambda: nc.vector.tensor_tensor(out=LA_[:], in0=XT_[:, 0, :].unsqueeze(1).to_broadcast([64, 32, 128]),
                                                              in1=OH[:].unsqueeze(2).to_broadcast([64, 32, 128]), op=ALU.mult),
                      r=[cbb, cb], w=[cbb])
                for hp in range(0, 8, NS):
                    ctxs = []
                    for k in range(min(NS, 8 - hp)):
                        c = Ctx()
                        c.j, c.hg, c.H0, c.rows = k, hp + k, (hp + k) * 4, rows
                        c.GT_, c.XT_, c.LA_, c.cbb = GT_, XT_, LA_, cbb
                        ctxs.append(c)
                    for ph in phases:
                        for c in ctxs:
                            ph(c)
            mk.barrier()


for _n, _f in list(ProgGDN.__dict__.items()):
    if callable(_f):
        setattr(Prog, _n, _f)


INPUT_SHAPES = dict(
    rel_bias=(32, 8), ln_mix=(4, D), ln_mem=(4, D), ln_ffn=(4, D), hyb_w_in=(2, D, HYB_IN), hyb_w_out=(2, 2048, D),
    nsa_q_gain=(2, 128), nsa_k_gain=(2, 3, 128), nsa_cmp_pos=(2, 2, 32, 128), nsa_cmp_w1=(2, 2, 4096, 256),
    nsa_cmp_w2=(2, 2, 256, 128), ret_gn_gain=(2, 1024), gdn_w_in=(2, D, GDN_IN), gdn_conv_w=(2, 4, 8192),
    gdn_a_log=(2, 32), gdn_dt_bias=(2, 32), gdn_norm_gain=(2, 128), gdn_w_out=(2, 4096, D), xa_wq=(4, D, 512),
    xa_wkv=(4, D, 1024), xa_q_gain=(4, 128), xa_k_gain=(4, 128), xa_mem_gain=(4, D), xa_wo=(4, 512, D),
    ffn_w_in=(4, D, 2 * FFN_H), ffn_w_out=(4, FFN_H, D))
CAST = ("hyb_w_in", "hyb_w_out", "nsa_cmp_w1", "nsa_cmp_w2", "gdn_w_in", "gdn_w_out", "xa_wq", "xa_wkv", "xa_wo",
        "ffn_w_in", "ffn_w_out")


def build_program(S=4096, M=256, depth=4, dbg_layers=False):
    P = Prog(S=S, M=M, dbg_out=())
    nc, mk = P.nc, P.mk
    x = P.inp("x", [S, D])
    mem = P.inp("mem", [M, D])
    h = nc.dram_tensor("out", [S, D], F32, kind="ExternalOutput").ap()
    A = {}
    for k, shp in INPUT_SHAPES.items():
        if k != "rel_bias":
            A[k] = P.inp(k, shp)
    toep_i, cmpx_i, OFF = nsa_index_tables(S)
    rel_toep = P.inp("rel_toep", [128, 10, 8, 128])
    rel_cmpx = P.inp("rel_cmpx", [cmpx_i.shape[0], 8, 128])
    P.idx = (toep_i, cmpx_i)
    W = {}
    P.wreg = {}
    first, later = [], []
    def layer_of(k, i):
        if k.startswith("hyb") or k.startswith("nsa"):
            return 2 * i
        if k.startswith("gdn"):
            return 2 * i + 1
        return i
    for k in CAST:
        src = A[k]
        dst = P.scr(k + "_bf", list(src.shape), BF16)
        W[k] = dst
        nl = src.shape[0]
        per = int(np.prod(src.shape[1:]))
        bufs = [Buf(f"{k}{i}") for i in range(nl)]
        P.wreg[dst.tensor.name] = (per, bufs)
        for i in range(nl):
            s2, d2 = src[i], dst[i]
            if len(s2.shape) == 3:
                s2, d2 = s2.rearrange("a b c -> (a b) c"), d2.rearrange("a b c -> (a b) c")
            R_, C_ = s2.shape
            rp = max(1, min(R_, (2 << 20) // C_))
            for r0 in range(0, R_, rp):
                r1 = min(R_, r0 + rp)
                item = (layer_of(k, i), d2[r0:r1, :], s2[r0:r1, :], bufs[i])
                (first if item[0] == 0 else later).append(item)
    for _, d_, s_, b_ in first:
        mk.dma(mk.pool, d_, s_, w=[b_])
    later.sort(key=lambda t: t[0])
    P.cast_pending = list(later)
    mk.on_barrier = lambda: P.emit_casts(7)
    for r0 in range(0, S, 512):
        mk.dma(mk.sp, h[r0:r0 + 512, :], x[r0:r0 + 512, :])
    mk.barrier()
    mix_in = P.scr("mix_in", [S, 2048])
    gdn_o = P.scr("gdn_o", [S, 4096])
    dbg = []
    for layer in range(depth):
        P.emit_casts_upto(layer)
        if layer % 2 == 0:
            e = layer // 2
            P.hyb_in(h, A["ln_mix"][layer], W["hyb_w_in"][e], A["nsa_q_gain"][e], A["nsa_k_gain"][e])
            P.nsa_cmp(A["nsa_cmp_pos"][e], W["nsa_cmp_w1"][e], W["nsa_cmp_w2"][e], A["nsa_k_gain"][e][0])
            P.nsa_attn(rel_toep, rel_cmpx, OFF, mix_in)
            P.retention(A["ret_gn_gain"][e], mix_in)

            def jobs_o(st, TB, e=e):
                return [dict(W=W["hyb_w_out"][e], c0=0, c1=D, layout="TM", epi=P.epi_residual(st, h, TB))]
            P.linear("hybo", mix_in, S, 2048, None, jobs_o, src_bf16=True)
        else:
            o = layer // 2
            P.gdn_in(h, A["ln_mix"][layer], W["gdn_w_in"][o], A["gdn_conv_w"][o])
            P.gdn_gates(A["gdn_a_log"][o], A["gdn_dt_bias"][o])
            P.gdn_chunks(A["gdn_norm_gain"][o], gdn_o)

            def jobs_g(st, TB, o=o):
                return [dict(W=W["gdn_w_out"][o], c0=0, c1=D, layout="TM", epi=P.epi_residual(st, h, TB, width=256))]
            P.linear("gdno", gdn_o, S, 4096, None, jobs_g, src_bf16=True)
        P.xa(h, mem, A["ln_mem"][layer], A["xa_mem_gain"][layer], W["xa_wq"][layer], W["xa_wkv"][layer],
             A["xa_q_gain"][layer], A["xa_k_gain"][layer], W["xa_wo"][layer])
        P.ffn(h, A["ln_ffn"][layer], W["ffn_w_in"][layer], W["ffn_w_out"][layer])
        if dbg_layers:
            d = nc.dram_tensor(f"dbg_h{layer}", [S, D], F32, kind="ExternalOutput").ap()
            for r0 in range(0, S, 512):
                mk.dma(mk.sp, d[r0:r0 + 512, :], h[r0:r0 + 512, :])
            mk.barrier()
    P.finish()
    return P


def host_inputs(P, inputs, b):
    toep_i, cmpx_i = P.idx
    rel_ext = np.concatenate([np.asarray(inputs["rel_bias"], np.float32), np.full((1, 8), -BIG, np.float32)], 0)
    im = dict(P.consts)
    im["x"] = np.ascontiguousarray(inputs["x"][b])
    im["mem"] = np.ascontiguousarray(inputs["mem"][b])
    im["rel_toep"] = np.ascontiguousarray(rel_ext[toep_i].transpose(0, 1, 3, 2))
    im["rel_cmpx"] = np.ascontiguousarray(rel_ext[cmpx_i].transpose(0, 2, 1))
    for k in INPUT_SHAPES:
        if k != "rel_bias":
            im[k] = np.ascontiguousarray(np.asarray(inputs[k], np.float32))
    return im


def kernel(**inputs):
    inputs = {k: np.asarray(v) for k, v in inputs.items()}
    B, S, _ = inputs["x"].shape
    P = build_program(S=S, M=inputs["mem"].shape[1])
    shared = host_inputs(P, inputs, 0)
    in_maps = []
    for b in range(B):
        im = dict(shared)
        im["x"] = np.ascontiguousarray(inputs["x"][b])
        im["mem"] = np.ascontiguousarray(inputs["mem"][b])
        in_maps.append(im)
    res = run_bass_kernel_spmd(P.nc, in_maps, core_ids=list(range(B)))
    return np.stack([np.asarray(r["out"], np.float32) for r in res.results], 0)
```

```python
import math
from contextlib import ExitStack

import numpy as np
import ml_dtypes
import concourse.bass as bass
import concourse.mybir as mybir
from concourse.bass_utils import run_bass_kernel_spmd

F32 = mybir.dt.float32
BF16 = mybir.dt.bfloat16
AF = mybir.ActivationFunctionType
ALU = mybir.AluOpType
AX = mybir.AxisListType


class Ev:
    __slots__ = ("eng", "sem", "val")

    def __init__(self, eng, sem=None, val=None):
        self.eng, self.sem, self.val = eng, sem, val


class Buf:
    __slots__ = ("w", "rd", "name")

    def __init__(self, name=""):
        self.w = None
        self.rd = {}
        self.name = name


class Eng:
    def __init__(self, name, h, sem, is_pe=False):
        self.name, self.h, self.sem, self.is_pe = name, h, sem, is_pe
        self.n = 0
        self.pending = []
        self.last = None
        self.waited = {}
        self.ring = []
        self.ring_cnt = []
        self.dma_i = 0


class MK:
    RING = 12

    def __init__(self, nc, es):
        self.nc = nc
        self.es = es
        mk = lambda n: es.enter_context(nc.semaphore(n))
        self.pe = Eng("pe", nc.tensor, mk("s_pe"), is_pe=True)
        self.act = Eng("act", nc.scalar, mk("s_act"))
        self.dve = Eng("dve", nc.vector, mk("s_dve"))
        self.pool = Eng("pool", nc.gpsimd, mk("s_pool"))
        self.sp = Eng("sp", nc.sync, mk("s_sp"))
        self.engs = [self.pe, self.act, self.dve, self.pool, self.sp]
        for e in (self.sp, self.pool, self.act):
            r = self.RING if e is self.sp else 6
            e.ring = [mk(f"r_{e.name}{i}") for i in range(r)]
            e.ring_cnt = [0] * r
        self.ninst = 0
        self.castq = Eng("castq", nc.gpsimd, None)
        self.castq.ring = [mk(f"r_cast{i}") for i in range(8)]
        self.castq.ring_cnt = [0] * 8
        self.on_barrier = None

    def _resolve(self, ev):
        if ev.val is None:
            p = ev.eng
            p.n += 1
            p.last.then_inc(p.sem, 1)
            for e in p.pending:
                e.val = p.n
                e.sem = p.sem
            p.pending = []

    def _wait(self, eng, ev):
        if ev.eng is eng and eng.is_pe:
            return
        self._resolve(ev)
        k = id(ev.sem)
        if eng.waited.get(k, 0) < ev.val:
            eng.h.wait_ge(ev.sem, ev.val)
            eng.waited[k] = ev.val
            self.ninst += 1

    def _deps(self, r, w):
        deps = []
        for b in r:
            if b.w is not None:
                deps.append(b.w)
        for b in w:
            if b.w is not None:
                deps.append(b.w)
            deps.extend(b.rd.values())
        return deps

    def _record(self, ev, key, r, w):
        for b in r:
            b.rd[key] = ev
        for b in w:
            b.w = ev
            b.rd = {}

    def op(self, eng, fn, r=(), w=()):
        for ev in self._deps(r, w):
            self._wait(eng, ev)
        inst = fn()
        self.ninst += 1
        eng.last = inst
        if eng.is_pe:
            ev = Ev(eng)
            eng.pending.append(ev)
        else:
            eng.n += 1
            inst.then_inc(eng.sem, 1)
            ev = Ev(eng, eng.sem, eng.n)
        self._record(ev, id(eng), r, w)
        return inst

    def dma(self, q, out, in_, r=(), w=(), **kw):
        for ev in self._deps(r, w):
            self._wait(q, ev)
        i = q.dma_i
        q.dma_i += 1
        slot = i % len(q.ring)
        s = q.ring[slot]
        base = q.ring_cnt[slot]
        if base > 0 and q.waited.get(id(s), 0) < base:
            q.h.wait_ge(s, base)
            q.waited[id(s)] = base
        q.h.dma_start(out=out, in_=in_, **kw).then_inc(s, 16)
        self.ninst += 1
        q.ring_cnt[slot] = base + 16
        ev = Ev(None, s, base + 16)
        self._record(ev, (id(s), base + 16), r, w)
        return ev

    def barrier(self):
        evs = []
        for e in self.engs:
            if e.pending:
                self._resolve(e.pending[-1])
            if e.n > 0:
                evs.append(Ev(e, e.sem, e.n))
            for s, c in zip(e.ring, e.ring_cnt):
                if c > 0:
                    evs.append(Ev(None, s, c))
        for e in self.engs:
            for ev in evs:
                if ev.eng is e:
                    continue
                k = id(ev.sem)
                if e.waited.get(k, 0) < ev.val:
                    e.h.wait_ge(ev.sem, ev.val)
                    e.waited[k] = ev.val
        if self.on_barrier is not None:
            self.on_barrier()


D = 2048
EPS = 1e-6
HYB_IN = 6680
GDN_IN = 12352
FFN_H = 5632
BIG = 30000.0


class Prog:
    def __init__(self, S=4096, M=256, dbg_out=()):
        self.S, self.M = S, M
        self.nc = nc = bass.Bass("TRN2", target_bir_lowering=False)
        self.es = ExitStack()
        self.mk = MK(nc, self.es)
        self.dram = {}
        self.dbg_out = set(dbg_out)
        self.in_names = []
        self.pb = [self.es.enter_context(nc.psum_tensor(f"pb{i}", [128, 512], F32)) for i in range(8)]
        self.pbb = [Buf(f"pb{i}") for i in range(8)]
        self.rot = list(range(8))
        self.rot_i = 0
        self.uid = 0
        self.ident_bf = self.const_sb("ident_bf", np.eye(128, dtype=np.float32).astype(ml_dtypes.bfloat16), BF16)
        self.ident_f = self.const_sb("ident_f", np.eye(128, dtype=np.float32), F32)
        self.ones_mean = self.const_sb("ones_mean", np.full((128, 128), 1.0 / 128, np.float32).astype(ml_dtypes.bfloat16), BF16)
        self.ones_bf = self.const_sb("ones_bf", np.ones((128, 128), np.float32).astype(ml_dtypes.bfloat16), BF16)
        self.epsc = {}
        for i, e in enumerate((1e-6, 1e-5)):
            t = self.es.enter_context(nc.sbuf_tensor(f"c_eps{i}", [128, 1], F32))
            self.mk.op(self.mk.pool, lambda: nc.gpsimd.memset(t[:], float(e)), w=[Buf()])
            self.epsc[float(e)] = t
        self.mk.barrier()

    def inp(self, name, shape, dt=F32):
        t = self.nc.dram_tensor(name, list(shape), dt, kind="ExternalInput").ap()
        self.dram[name] = t
        self.in_names.append(name)
        return t

    def scr(self, name, shape, dt=BF16):
        if name in self.dram:
            return self.dram[name]
        kind = "ExternalOutput" if name in self.dbg_out else "Internal"
        t = self.nc.dram_tensor(name, list(shape), dt, kind=kind).ap()
        self.dram[name] = t
        return t

    consts = {}

    def const_sb(self, name, arr, dt):
        self.consts = dict(self.consts)
        self.consts[name] = np.ascontiguousarray(arr)
        d = self.inp(name, arr.shape, dt)
        t = self.es.enter_context(self.nc.sbuf_tensor("c_" + name, list(arr.shape), dt))
        b = Buf(name)
        self.mk.dma(self.mk.sp, t[:], d, w=[b])
        return (t, b)

    def const_dram(self, name, arr, dt):
        self.consts = dict(self.consts)
        self.consts[name] = np.ascontiguousarray(arr)
        return self.inp(name, arr.shape, dt)

    def ps(self):
        i = self.rot[self.rot_i % len(self.rot)]
        self.rot_i += 1
        return self.pb[i], self.pbb[i]

    def sbt(self, st, name, shape, dt):
        self.uid += 1
        return st.enter_context(self.nc.sbuf_tensor(f"{name}_{self.uid}", list(shape), dt))

    def evac_eng(self):
        self.uid += 1
        return self.mk.act if self.uid % 2 else self.mk.dve

    def copy(self, eng, out, in_, r, w):
        nc = self.nc
        if eng is self.mk.act:
            return self.mk.op(eng, lambda: nc.scalar.copy(out=out, in_=in_), r=r, w=w)
        return self.mk.op(eng, lambda: eng.h.tensor_copy(out=out, in_=in_), r=r, w=w)

    def rsqrt(self, out, in_, eps, r, w):
        nc, mk = self.nc, self.mk
        mk.op(mk.act, lambda: nc.scalar.activation(out=out, in_=in_, func=AF.Ln, bias=self.eps_col(eps)), r=r, w=w)
        mk.op(mk.act, lambda: nc.scalar.activation(out=out, in_=out, func=AF.Exp, scale=-0.5), r=w, w=w)

    def eps_col(self, eps):
        return self.epsc[float(eps)][:, 0:1]

    def wdep(self, ap):
        reg = getattr(self, "wreg", {})
        ent = reg.get(ap.tensor.name)
        if ent is None:
            return []
        per, bufs = ent
        return [bufs[int(ap.offset) // per]]

    def emit_casts(self, n=6):
        q = getattr(self, "cast_pending", [])
        for _ in range(min(n, len(q))):
            _, dst, src, buf = q.pop(0)
            self.mk.dma(self.mk.castq, dst, src, w=[buf])

    def emit_casts_upto(self, layer):
        q = getattr(self, "cast_pending", [])
        while q and q[0][0] <= layer:
            _, dst, src, buf = q.pop(0)
            self.mk.dma(self.mk.castq, dst, src, w=[buf])

    def defer(self, fn):
        self._deferred = getattr(self, "_deferred", [])
        self._deferred.append(fn)

    def take(self):
        q = getattr(self, "_deferred", [])
        self._deferred = []
        return q

    def flush(self):
        while getattr(self, "_deferred", []):
            for fn in self.take():
                fn()

    def finish(self):
        self.flush()
        self.mk.barrier()
        self.es.close()

    def precast(self, name, src, rows_per=None):
        shp = list(src.shape)
        dst = self.scr(name + "_bf", shp, BF16)
        s2 = src if len(shp) == 2 else src.rearrange("a b c -> (a b) c")
        d2 = dst if len(shp) == 2 else dst.rearrange("a b c -> (a b) c")
        R, C = s2.shape
        rp = max(1, min(R, (2 << 20) // C))
        b = Buf()
        for r0 in range(0, R, rp):
            r1 = min(R, r0 + rp)
            self.mk.dma(self.mk.pool, d2[r0:r1, :], s2[r0:r1, :])
        return dst

    class Pro:
        pass

    def make_pro(self, st, name, src, K, gain, src_bf16, TB):
        nc, mk = self.nc, self.mk
        p = Prog.Pro()
        p.K, p.KC, p.TB, p.src, p.gain, p.src_bf16 = K, K // 128, TB, src, gain, src_bf16
        p.xT = [self.sbt(st, name + "xT", [128, p.KC, TB], BF16) for _ in range(2)]
        p.xTb = [Buf(), Buf()]
        p.xin = [self.sbt(st, name + "xin", [128, K], BF16 if src_bf16 else F32) for _ in range(2)]
        p.xinb = [Buf(), Buf()]
        p.cnt = 0
        if gain is not None:
            p.xn = [self.sbt(st, name + "xn", [128, K], BF16) for _ in range(2)]
            p.xnb = [Buf(), Buf()]
            p.gt = self.sbt(st, name + "g", [128, K], F32)
            p.gb = Buf()
            mk.dma(mk.sp, p.gt[:], gain.partition_broadcast(128), w=[p.gb])
            p.junk = self.sbt(st, name + "junk", [128, K], BF16)
            p.junkb = Buf()
            p.stat = [self.sbt(st, name + "stat", [128, 2], F32) for _ in range(2)]
            p.statb = [Buf(), Buf()]
        return p

    def prologue(self, p, tb, slot):
        nc, mk = self.nc, self.mk
        K, KC, TB = p.K, p.KC, p.TB
        for t in range(TB // 128):
            j = p.cnt % 2
            p.cnt += 1
            row0 = tb * TB + t * 128
            xin, xinb = p.xin[j], p.xinb[j]
            mk.dma(mk.sp, xin[:], p.src[row0:row0 + 128, :], w=[xinb])
            if p.gain is not None:
                stt, sttb = p.stat[j], p.statb[j]
                mk.op(mk.act, lambda: nc.scalar.activation(out=p.junk[:], in_=xin[:], func=AF.Square,
                                                           scale=float(K) ** -0.5, accum_out=stt[:, 0:1]),
                      r=[xinb], w=[p.junkb, sttb])
                self.rsqrt(stt[:, 1:2], stt[:, 0:1], EPS, [sttb], [sttb])
                xn, xnb = p.xn[j], p.xnb[j]
                mk.op(mk.dve, lambda: nc.vector.scalar_tensor_tensor(out=xn[:], in0=xin[:], scalar=stt[:, 1:2],
                                                                     in1=p.gt[:], op0=ALU.mult, op1=ALU.mult),
                      r=[xinb, sttb, p.gb], w=[xnb])
                srcT, srcb = xn, xnb
            else:
                srcT, srcb = xin, xinb
            for k0 in range(0, KC, 8):
                nk = min(8, KC - k0)
                bank, bankb = self.ps()
                bv = bank[:].bitcast(BF16)
                for kk in range(nk):
                    mk.op(mk.pe, lambda kk=kk: nc.tensor.transpose(out=bv[:, kk * 128:(kk + 1) * 128],
                                                                   in_=srcT[:, (k0 + kk) * 128:(k0 + kk + 1) * 128],
                                                                   identity=self.ident_bf[0][:]),
                          r=[srcb, self.ident_bf[1]], w=[bankb])
                self.copy(self.evac_eng(), p.xT[slot][:, k0:k0 + nk, t * 128:(t + 1) * 128],
                          bv[:, :nk * 128].rearrange("p (k c) -> p k c", c=128), r=[bankb], w=[p.xTb[slot]])

    class Pan:
        pass

    def make_pan(self, st, name, KC, PW, nslots=2):
        pn = Prog.Pan()
        pn.KC, pn.PW = KC, PW
        pn.t = [self.sbt(st, name + "wp", [128, KC, PW], BF16) for _ in range(nslots)]
        pn.b = [Buf() for _ in range(nslots)]
        pn.cnt = 0
        return pn

    def run_mm(self, pn, jobs, xT, xTb, TBW, mid_hook=None):
        nc, mk = self.nc, self.mk
        KC, PW = pn.KC, pn.PW
        panels = []
        for jb in jobs:
            g = jb.get("group", 1)
            step = PW
            for p0 in range(jb["c0"], jb["c1"], step):
                panels.append((jb, p0, min(step, jb["c1"] - p0)))

        def load(i):
            jb, p0, pw = panels[i]
            slot = pn.cnt % len(pn.t)
            pn.cnt += 1
            Wv = jb["W"].rearrange("(k p) n -> p k n", p=128)
            mk.dma(mk.sp, pn.t[slot][:, :, :pw], Wv[:, :, p0:p0 + pw], r=self.wdep(jb["W"]), w=[pn.b[slot]])
            return slot

        slots = {0: load(0)}
        for i, (jb, p0, pw) in enumerate(panels):
            if i + 1 < len(panels):
                slots[i + 1] = load(i + 1)
            slot = slots.pop(i)
            wt, wb = pn.t[slot], pn.b[slot]
            if jb["layout"] == "FM":
                grp = []
                for c0 in range(0, pw, 128):
                    cw = min(128, pw - c0)
                    bank, bankb = self.ps()
                    for kc in range(KC):
                        mk.op(mk.pe, lambda kc=kc: nc.tensor.matmul(bank[:cw, :TBW], lhsT=wt[:, kc, c0:c0 + cw],
                                                                     rhs=xT[:, kc, :], start=(kc == 0),
                                                                     stop=(kc == KC - 1)),
                              r=[wb, xTb], w=[bankb])
                    grp.append((bank[:cw, :TBW], bankb, p0 + c0, cw))
                    if len(grp) == jb.get("group", 1):
                        pend = self.take()
                        jb["epi"](grp)
                        for fn in pend:
                            fn()
                        grp = []
                assert not grp
            else:
                for t in range(TBW // 128):
                    bank, bankb = self.ps()
                    for kc in range(KC):
                        mk.op(mk.pe, lambda kc=kc: nc.tensor.matmul(bank[:, :pw], lhsT=xT[:, kc, t * 128:(t + 1) * 128],
                                                                     rhs=wt[:, kc, :pw], start=(kc == 0),
                                                                     stop=(kc == KC - 1)),
                              r=[wb, xTb], w=[bankb])
                    pend = self.take()
                    jb["epi"]([(bank[:, :pw], bankb, p0, pw, t)])
                    for fn in pend:
                        fn()
            if i == min(1, len(panels) - 1) and mid_hook is not None:
                self.flush()
                mid_hook()

    def epi_tm_store(self, st, dst, col_shift, TB, dt=BF16, func=None, width=512, row_of=None):
        nc, mk = self.nc, self.mk
        stg = [self.sbt(st, "stg", [128, width], dt) for _ in range(2)]
        stgb = [Buf(), Buf()]
        cnt = [0]

        def epi(grp, tb):
            (ps, psb, col, w, t), = grp
            j = cnt[0] % 2
            cnt[0] += 1
            if func is None:
                self.copy(self.evac_eng(), stg[j][:, :w], ps, r=[psb], w=[stgb[j]])
            else:
                mk.op(mk.act, lambda: nc.scalar.activation(out=stg[j][:, :w], in_=ps, func=func), r=[psb], w=[stgb[j]])
            row0 = tb * TB + t * 128
            mk.dma(mk.pool, dst[row0:row0 + 128, col + col_shift:col + col_shift + w], stg[j][:, :w], r=[stgb[j]])
        return epi

    def epi_fm_store(self, st, dst, row_shift, TB, dt=BF16, func=None):
        nc, mk = self.nc, self.mk
        stg = [self.sbt(st, "stgf", [128, TB], dt) for _ in range(2)]
        stgb = [Buf(), Buf()]
        cnt = [0]

        def epi(grp, tb):
            (ps, psb, col, cw), = grp
            j = cnt[0] % 2
            cnt[0] += 1
            if func is None:
                self.copy(self.evac_eng(), stg[j][:cw, :], ps, r=[psb], w=[stgb[j]])
            else:
                mk.op(mk.act, lambda: nc.scalar.activation(out=stg[j][:cw, :], in_=ps, func=func), r=[psb], w=[stgb[j]])
            mk.dma(mk.pool, dst[col + row_shift:col + row_shift + cw, tb * TB:(tb + 1) * TB], stg[j][:cw, :], r=[stgb[j]])
        return epi

    def epi_residual(self, st, h, TB, width=512):
        nc, mk = self.nc, self.mk
        hr = [self.sbt(st, "hres", [128, width], F32) for _ in range(3)]
        hrb = [Buf() for _ in range(3)]
        cnt = [0]

        def epi(grp, tb):
            (ps, psb, col, w, t), = grp
            j = cnt[0] % 3
            cnt[0] += 1
            row0 = tb * TB + t * 128
            mk.dma(mk.sp, hr[j][:, :w], h[row0:row0 + 128, col:col + w], w=[hrb[j]])
            mk.op(mk.dve, lambda: nc.vector.tensor_tensor(out=hr[j][:, :w], in0=ps, in1=hr[j][:, :w], op=ALU.add),
                  r=[psb, hrb[j]], w=[hrb[j]])
            mk.dma(mk.pool, h[row0:row0 + 128, col:col + w], hr[j][:, :w], r=[hrb[j]])
        return epi

    def epi_fm_headnorm(self, st, dst, row_shift, TB, gcol, gcolb, mean=True, eps=EPS):
        nc, mk = self.nc, self.mk
        sq = [self.sbt(st, "hn_sq", [128, TB], BF16) for _ in range(2)]
        sqb = [Buf(), Buf()]
        rs = [self.sbt(st, "hn_rs", [128, TB], F32) for _ in range(2)]
        rsb = [Buf(), Buf()]
        stg = [self.sbt(st, "hn_o", [128, TB], BF16) for _ in range(2)]
        stgb = [Buf(), Buf()]
        cnt = [0]
        ones = self.ones_mean if mean else self.ones_bf

        def epi(grp, tb, gi=0, store=None, deep=False):
            (ps, psb, col, cw), = grp
            assert cw == 128
            j = cnt[0] % 2
            cnt[0] += 1
            mk.op(mk.act, lambda: nc.scalar.activation(out=sq[j][:], in_=ps, func=AF.Square), r=[psb], w=[sqb[j]])

            def part_b1():
                b2, b2b = self.ps()
                mk.op(mk.pe, lambda: nc.tensor.matmul(b2[:, :TB], lhsT=ones[0][:], rhs=sq[j][:], start=True, stop=True),
                      r=[sqb[j], ones[1]], w=[b2b])
                self.rsqrt(rs[j][:], b2[:, :TB], eps, [b2b], [rsb[j]])

            def part_b2():
                mk.op(mk.dve, lambda: nc.vector.scalar_tensor_tensor(out=stg[j][:], in0=ps, scalar=gcol[:, gi:gi + 1],
                                                                     in1=rs[j][:], op0=ALU.mult, op1=ALU.mult),
                      r=[psb, rsb[j], gcolb], w=[stgb[j]])
                if store is None:
                    mk.dma(mk.pool, dst[col + row_shift:col + row_shift + 128, tb * TB:(tb + 1) * TB], stg[j][:], r=[stgb[j]])
                else:
                    store(stg[j], stgb[j])

            if deep:
                def b1_then_defer():
                    part_b1()
                    self.defer(part_b2)
                self.defer(b1_then_defer)
            else:
                def part_b():
                    part_b1()
                    part_b2()
                self.defer(part_b)
        return epi

    def linear(self, name, src, ntok, K, gain, jobs_fn, src_bf16=False, TB=512, PW=None):
        TB = min(TB, ntok)
        KC = K // 128
        if PW is None:
            PW = 512 if KC <= 16 else 256
        with ExitStack() as st:
            p = self.make_pro(st, name, src, K, gain, src_bf16, TB)
            pn = self.make_pan(st, name, KC, PW)
            jobs = jobs_fn(st, TB)
            NB = ntok // TB
            self.prologue(p, 0, 0)
            for tb in range(NB):
                slot = tb % 2
                jj = [dict(jb, epi=(lambda grp, e=jb["epi"], tb=tb: e(grp, tb))) for jb in jobs]
                hook = (lambda tb=tb: self.prologue(p, tb + 1, (tb + 1) % 2)) if tb + 1 < NB else None
                self.run_mm(pn, jj, p.xT[slot], p.xTb[slot], TB, mid_hook=hook)
            self.flush()
            self.mk.barrier()

    def ffn(self, h, gain, w_in, w_out):
        nc, mk = self.nc, self.mk
        S, TB = self.S, min(512, self.S)
        HC = FFN_H // 128
        with ExitStack() as st:
            p = self.make_pro(st, "ffn", h, D, gain, False, TB)
            pn1 = self.make_pan(st, "ffn1", 16, 256, nslots=4)
            pn2 = self.make_pan(st, "ffn2", HC, 256)
            hid = self.sbt(st, "hid", [128, HC, TB], BF16)
            hidb = Buf()
            sg = [self.sbt(st, "sg", [128, TB], F32) for _ in range(2)]
            sgb = [Buf(), Buf()]
            cnt = [0]
            res = self.epi_residual(st, h, TB, width=256)
            NB = S // TB
            self.prologue(p, 0, 0)
            for tb in range(NB):
                slot = tb % 2
                gate_ps = {}

                def epi_gate(grp, tb=tb):
                    (ps, psb, col, cw), = grp
                    gate_ps[col] = (ps, psb)

                def epi_up(grp, tb=tb):
                    (ps, psb, col, cw), = grp
                    gps, gpsb = gate_ps.pop(col - FFN_H)
                    j = cnt[0] % 2
                    cnt[0] += 1
                    mk.op(mk.act, lambda: nc.scalar.activation(out=sg[j][:], in_=gps, func=AF.Silu), r=[gpsb], w=[sgb[j]])
                    hc = (col - FFN_H) // 128
                    mk.op(mk.dve, lambda: nc.vector.tensor_tensor(out=hid[:, hc, :], in0=ps, in1=sg[j][:], op=ALU.mult),
                          r=[psb, sgb[j]], w=[hidb])
                jobs = []
                for c0 in range(0, FFN_H, 256):
                    jobs.append(dict(W=w_in, c0=c0, c1=c0 + 256, layout="FM", epi=epi_gate))
                    jobs.append(dict(W=w_in, c0=FFN_H + c0, c1=FFN_H + c0 + 256, layout="FM", epi=epi_up))
                hook = (lambda tb=tb: self.prologue(p, tb + 1, (tb + 1) % 2)) if tb + 1 < NB else None
                self.run_mm(pn1, jobs, p.xT[slot], p.xTb[slot], TB, mid_hook=hook)
                jobs2 = [dict(W=w_out, c0=0, c1=D, layout="TM", epi=lambda grp, tb=tb: res(grp, tb))]
                self.run_mm(pn2, jobs2, hid, hidb, TB)
            mk.barrier()

    def load_col(self, st, src128, scale=None):
        nc, mk = self.nc, self.mk
        t = self.sbt(st, "col", [128, 1], F32)
        b = Buf()
        mk.dma(mk.sp, t[:], src128.rearrange("(p o) -> p o", o=1), w=[b])
        if scale is not None:
            mk.op(mk.dve, lambda: nc.vector.tensor_scalar(out=t[:], in0=t[:], scalar1=float(scale), scalar2=None,
                                                          op0=ALU.mult), r=[b], w=[b])
        return t, b

    def xa(self, h, mem, ln_mem, mem_gain, wq, wkv, q_gain, k_gain, wo):
        nc, mk = self.nc, self.mk
        S, M = self.S, self.M
        kT = self.scr("xa_kT", [512, M])
        v = self.scr("xa_v", [M, 512])
        qT = self.scr("xa_qT", [512, S])
        o = self.scr("xa_o", [S, 512])

        def jobs_kv(st, TB):
            gc, gcb = self.load_col(st, k_gain)
            hn = self.epi_fm_headnorm(st, kT, 0, TB, gc, gcb)
            tm = self.epi_tm_store(st, v, -512, TB)
            return [dict(W=wkv, c0=0, c1=512, layout="FM", epi=hn),
                    dict(W=wkv, c0=512, c1=1024, layout="TM", epi=tm)]
        self.linear("xakv", mem, M, D, mem_gain, jobs_kv, TB=256)

        def jobs_q(st, TB):
            gc, gcb = self.load_col(st, q_gain, scale=128 ** -0.5)
            hn = self.epi_fm_headnorm(st, qT, 0, TB, gc, gcb)
            return [dict(W=wq, c0=0, c1=512, layout="FM", epi=hn)]
        self.linear("xaq", h, S, D, ln_mem, jobs_q)

        MT = M // 128
        with ExitStack() as st:
            KT = self.sbt(st, "xaKT", [128, 4, M], BF16)
            KTb = Buf()
            mk.dma(mk.sp, KT[:], kT.rearrange("(h p) m -> p h m", p=128), w=[KTb])
            V = self.sbt(st, "xaV", [128, MT, 4, 129], BF16)
            Vb = Buf()
            mk.op(mk.pool, lambda: nc.gpsimd.memset(V[:], 1.0), w=[Vb])
            for kt in range(MT):
                mk.dma(mk.sp, V[:, kt, :, 0:128], v[kt * 128:(kt + 1) * 128, :].rearrange("p (h d) -> p h d", d=128), w=[Vb])
            TB = min(512, S)
            NQ = TB // 128
            qt = [self.sbt(st, "xaq", [128, 4, TB], BF16) for _ in range(2)]
            qtb = [Buf(), Buf()]
            pT = [self.sbt(st, "xapT", [128, TB], BF16) for _ in range(2 * MT)]
            pTb = [Buf() for _ in range(2 * MT)]
            ot = [self.sbt(st, "xaot", [128, NQ, 512], BF16) for _ in range(2)]
            otb = [Buf(), Buf()]
            rc = [self.sbt(st, "xarc", [128, 1], F32) for _ in range(4)]
            rcb = [Buf() for _ in range(4)]
            pc = 0
            rcc = 0

            def score(tb, hh):
                nonlocal pc
                j = tb % 2
                if hh == 0:
                    mk.dma(mk.sp, qt[j][:], qT[:, tb * TB:(tb + 1) * TB].rearrange("(h p) s -> p h s", p=128), w=[qtb[j]])
                cur = []
                for kt in range(MT):
                    bank, bankb = self.ps()
                    mk.op(mk.pe, lambda: nc.tensor.matmul(bank[:, :TB], lhsT=KT[:, hh, kt * 128:(kt + 1) * 128],
                                                          rhs=qt[j][:, hh, :], start=True, stop=True),
                          r=[KTb, qtb[j]], w=[bankb])
                    pi = pc % (2 * MT)
                    pc += 1
                    mk.op(mk.act, lambda: nc.scalar.activation(out=pT[pi][:], in_=bank[:, :TB], func=AF.Exp),
                          r=[bankb], w=[pTb[pi]])
                    cur.append(pi)
                return cur

            def pv(tb, hh, cur):
                nonlocal rcc
                j = tb % 2
                for qi in range(NQ):
                    bank, bankb = self.ps()
                    for kt in range(MT):
                        pi = cur[kt]
                        mk.op(mk.pe, lambda: nc.tensor.matmul(bank[:, :129], lhsT=pT[pi][:, qi * 128:(qi + 1) * 128],
                                                              rhs=V[:, kt, hh, :], start=(kt == 0), stop=(kt == MT - 1)),
                              r=[pTb[pi], Vb], w=[bankb])
                    ri = rcc % 4
                    rcc += 1
                    mk.op(mk.dve, lambda: nc.vector.reciprocal(out=rc[ri][:], in_=bank[:, 128:129]), r=[bankb], w=[rcb[ri]])
                    mk.op(mk.dve, lambda: nc.vector.tensor_scalar(out=ot[j][:, qi, hh * 128:(hh + 1) * 128], in0=bank[:, 0:128],
                                                                  scalar1=rc[ri][:, 0:1], scalar2=None, op0=ALU.mult),
                          r=[bankb, rcb[ri]], w=[otb[j]])
                if hh == 3:
                    for qi in range(NQ):
                        r0 = tb * TB + qi * 128
                        mk.dma(mk.pool, o[r0:r0 + 128, :], ot[j][:, qi, :], r=[otb[j]])

            prev = None
            for tb in range(S // TB):
                for hh in range(4):
                    cur = score(tb, hh)
                    if prev is not None:
                        pv(*prev)
                    prev = (tb, hh, cur)
            pv(*prev)
            mk.barrier()

        def jobs_o(st, TB):
            res = self.epi_residual(st, h, TB)
            return [dict(W=wo, c0=0, c1=D, layout="TM", epi=res)]
        self.linear("xao", o, S, 512, None, jobs_o, src_bf16=True)

    def rot_tables(self):
        if "rot_cos" in self.dram:
            return self.dram["rot_cos"], self.dram["rot_sin"]
        S = self.S
        inv = (np.float32(10000.0) ** (-np.arange(128, dtype=np.float32) / np.float32(128))).astype(np.float32)
        ang = (np.arange(S, dtype=np.float32)[None, :] * inv[:, None]).astype(np.float32)
        c = self.const_dram("rot_cos", np.cos(ang).astype(np.float32), F32)
        s = self.const_dram("rot_sin", np.sin(ang).astype(np.float32), F32)
        return c, s

    def epi_rotary(self, st, dst, row_shift, TB):
        nc, mk = self.nc, self.mk
        cosd, sind = self.rot_tables()
        cs = [self.sbt(st, "rotcs", [128, 2, TB], F32) for _ in range(2)]
        csb = [Buf(), Buf()]
        cs_tb = [None, None]
        tmp = [self.sbt(st, "rott", [128, 4, TB], F32) for _ in range(2)]
        tmpb = [Buf(), Buf()]
        og = [self.sbt(st, "roto", [128, 2, TB], BF16) for _ in range(2)]
        ogb = [Buf(), Buf()]
        cnt = [0]

        def epi(grp, tb):
            (p1, p1b, col, _), (p2, p2b, _, _) = grp
            cj = tb % 2
            if cs_tb[cj] != tb:
                mk.dma(mk.sp, cs[cj][:, 0, :], cosd[:, tb * TB:(tb + 1) * TB], w=[csb[cj]])
                mk.dma(mk.sp, cs[cj][:, 1, :], sind[:, tb * TB:(tb + 1) * TB], w=[csb[cj]])
                cs_tb[cj] = tb
            j = cnt[0] % 2
            cnt[0] += 1
            T, Tb = tmp[j], tmpb[j]
            for k, (pp, ppb, ci) in enumerate(((p1, p1b, 0), (p2, p2b, 1), (p1, p1b, 1), (p2, p2b, 0))):
                mk.op(mk.dve, lambda: nc.vector.tensor_tensor(out=T[:, k, :], in0=pp, in1=cs[cj][:, ci, :], op=ALU.mult),
                      r=[ppb, csb[cj]], w=[Tb])
            mk.op(mk.pool, lambda: nc.gpsimd.tensor_tensor(out=og[j][:, 0, :], in0=T[:, 0, :], in1=T[:, 1, :], op=ALU.subtract),
                  r=[Tb], w=[ogb[j]])
            mk.op(mk.pool, lambda: nc.gpsimd.tensor_tensor(out=og[j][:, 1, :], in0=T[:, 2, :], in1=T[:, 3, :], op=ALU.add),
                  r=[Tb], w=[ogb[j]])
            r0 = col + row_shift
            mk.dma(mk.pool, dst[r0:r0 + 256, tb * TB:(tb + 1) * TB].rearrange("(c p) s -> p c s", p=128), og[j][:], r=[ogb[j]])
        return epi

    def hyb_in(self, h, ln, w_in, q_gain, k_gain):
        S = self.S
        qT = self.scr("nsa_qT", [1024, S])
        kcT = self.scr("nsa_kcT", [256, S])
        vcT = self.scr("nsa_vcT", [256, S])
        ksT = self.scr("nsa_ksT", [256, S])
        vs = self.scr("nsa_vs", [S, 256])
        kwT = self.scr("nsa_kwT", [256, S])
        vw = self.scr("nsa_vw", [S, 256])
        gate = self.scr("nsa_gate", [S, 24], F32)
        rqT = self.scr("ret_qT", [1024, S])
        rkT = self.scr("ret_kT", [1024, S])
        rv = self.scr("ret_v", [S, 1024])
        rg = self.scr("ret_g", [S, 1024])

        def jobs(st, TB):
            gq, gqb = self.load_col(st, q_gain, scale=128 ** -0.5)
            gks, gksb = self.load_col(st, k_gain[1])
            gkw, gkwb = self.load_col(st, k_gain[2])
            J = []
            J.append(dict(W=w_in, c0=0, c1=1024, layout="FM", epi=self.epi_fm_headnorm(st, qT, 0, TB, gq, gqb)))
            J.append(dict(W=w_in, c0=1024, c1=1280, layout="FM", epi=self.epi_fm_store(st, kcT, -1024, TB)))
            J.append(dict(W=w_in, c0=1280, c1=1536, layout="FM", epi=self.epi_fm_store(st, vcT, -1280, TB)))
            J.append(dict(W=w_in, c0=1536, c1=1792, layout="FM", epi=self.epi_fm_headnorm(st, ksT, -1536, TB, gks, gksb)))
            J.append(dict(W=w_in, c0=1792, c1=2048, layout="TM", epi=self.epi_tm_store(st, vs, -1792, TB)))
            J.append(dict(W=w_in, c0=2048, c1=2304, layout="FM", epi=self.epi_fm_headnorm(st, kwT, -2048, TB, gkw, gkwb)))
            J.append(dict(W=w_in, c0=2304, c1=2560, layout="TM", epi=self.epi_tm_store(st, vw, -2304, TB)))
            J.append(dict(W=w_in, c0=2560, c1=2584, layout="TM",
                          epi=self.epi_tm_store(st, gate, -2560, TB, dt=F32, func=AF.Sigmoid, width=24)))
            J.append(dict(W=w_in, c0=2584, c1=3608, layout="FM", group=2, epi=self.epi_rotary(st, rqT, -2584, TB)))
            J.append(dict(W=w_in, c0=3608, c1=4632, layout="FM", group=2, epi=self.epi_rotary(st, rkT, -3608, TB)))
            J.append(dict(W=w_in, c0=4632, c1=5656, layout="TM", epi=self.epi_tm_store(st, rv, -4632, TB)))
            J.append(dict(W=w_in, c0=5656, c1=6680, layout="TM", epi=self.epi_tm_store(st, rg, -5656, TB, func=AF.Silu)))
            return J
        self.linear("hyb", h, S, D, ln, jobs)

    def retention(self, gn_gain, mix_in):
        nc, mk = self.nc, self.mk
        S = self.S
        NCH = S // 128
        rqT, rkT, rv, rg = (self.dram[k] for k in ("ret_qT", "ret_kT", "ret_v", "ret_g"))
        gam = [1.0 - 2.0 ** (-5.0 - hh) for hh in range(4)]
        idx = np.arange(128, dtype=np.float64)
        dm = np.zeros((128, 4, 128), np.float32)
        qd = np.zeros((128, 4), np.float32)
        kd = np.zeros((128, 4), np.float32)
        for hh in range(4):
            lg = np.float32(np.log(np.float32(gam[hh])))
            rel = idx[None, :] - idx[:, None]
            dm[:, hh, :] = np.where(rel >= 0, np.exp(lg * np.maximum(rel, 0.0)), 0.0) / 16.0
            qd[:, hh] = np.exp(lg * (idx + 1.0)) / 16.0
            kd[:, hh] = np.exp(lg * (127.0 - idx))
        cdec = [float(np.exp(np.float32(np.log(np.float32(g))) * 128.0)) for g in gam]
        with ExitStack() as st:
            if "ret_dm" not in self.dram:
                self.const_dram("ret_dm", dm, F32)
                self.const_dram("ret_qd", qd, F32)
                self.const_dram("ret_kd", kd, F32)
            DM = self.sbt(st, "retDM", [128, 4, 128], F32)
            QD = self.sbt(st, "retQD", [128, 4], F32)
            KD = self.sbt(st, "retKD", [128, 4], F32)
            GN = self.sbt(st, "retGN", [128, 1024], F32)
            cb = Buf()
            mk.dma(mk.sp, DM[:], self.dram["ret_dm"], w=[cb])
            mk.dma(mk.sp, QD[:], self.dram["ret_qd"], w=[cb])
            mk.dma(mk.sp, KD[:], self.dram["ret_kd"], w=[cb])
            mk.dma(mk.sp, GN[:], gn_gain.partition_broadcast(128), w=[cb])
            stf = self.sbt(st, "retSf", [128, 4, 2, 256], F32)
            stb = self.sbt(st, "retSb", [128, 4, 2, 256], BF16)
            stfb, stbb = Buf(), Buf()
            mk.op(mk.pool, lambda: nc.gpsimd.memset(stf[:], 0.0), w=[stfb])
            mk.op(mk.pool, lambda: nc.gpsimd.memset(stb[:], 0.0), w=[stbb])
            qt = [self.sbt(st, "retq", [128, 8, 128], BF16) for _ in range(2)]
            kt = [self.sbt(st, "retk", [128, 8, 128], BF16) for _ in range(2)]
            vt = [self.sbt(st, "retv", [128, 1024], BF16) for _ in range(2)]
            gt = [self.sbt(st, "retg", [128, 1024], BF16) for _ in range(2)]
            inb = [Buf(), Buf()]
            ktm = [self.sbt(st, "retktm", [128, 256], BF16) for _ in range(2)]
            ktmb = [Buf(), Buf()]
            innT = [self.sbt(st, "retin", [128, 128], BF16) for _ in range(2)]
            innTb = [Buf(), Buf()]
            vd = [self.sbt(st, "retvd", [128, 256], BF16) for _ in range(2)]
            vdb = [Buf(), Buf()]
            osb = [self.sbt(st, "reto", [128, 4, 256], F32) for _ in range(2)]
            osbb = [Buf(), Buf()]
            junk = self.sbt(st, "retjunk", [128, 4, 256], F32)
            junkb = Buf()
            stat = [self.sbt(st, "retstat", [128, 16], F32) for _ in range(2)]
            statb = [Buf(), Buf()]
            yo = [self.sbt(st, "rety", [128, 1024], BF16) for _ in range(2)]
            yob = [Buf(), Buf()]
            c2 = 0
            for n in range(NCH):
                j = n % 2
                cs = slice(n * 128, (n + 1) * 128)
                mk.dma(mk.sp, qt[j][:], rqT[:, cs].rearrange("(c p) s -> p c s", p=128), w=[inb[j]])
                mk.dma(mk.sp, kt[j][:], rkT[:, cs].rearrange("(c p) s -> p c s", p=128), w=[inb[j]])
                mk.dma(mk.sp, vt[j][:], rv[cs, :], w=[inb[j]])
                mk.dma(mk.sp, gt[j][:], rg[cs, :], w=[inb[j]])
                O, Ob = osb[j], osbb[j]
                for hh in range(4):
                    i2 = c2 % 2
                    c2 += 1
                    bt, btb = self.ps()
                    btv = bt[:].bitcast(BF16)
                    for dc in range(2):
                        mk.op(mk.pe, lambda: nc.tensor.transpose(out=btv[:, dc * 128:(dc + 1) * 128], in_=kt[j][:, hh * 2 + dc, :],
                                                                 identity=self.ident_bf[0][:]),
                              r=[inb[j], self.ident_bf[1]], w=[btb])
                    self.copy(mk.act, ktm[i2][:], btv[:, 0:256], r=[btb], w=[ktmb[i2]])
                    bi, bib = self.ps()
                    for dc in range(2):
                        mk.op(mk.pe, lambda: nc.tensor.matmul(bi[:, :128], lhsT=kt[j][:, hh * 2 + dc, :], rhs=qt[j][:, hh * 2 + dc, :],
                                                              start=(dc == 0), stop=(dc == 1)), r=[inb[j]], w=[bib])
                    mk.op(mk.dve, lambda: nc.vector.tensor_tensor(out=innT[i2][:], in0=bi[:, :128], in1=DM[:, hh, :], op=ALU.mult),
                          r=[bib, cb], w=[innTb[i2]])
                    bA, bAb = self.ps()
                    mk.op(mk.pe, lambda: nc.tensor.matmul(bA[:, :256], lhsT=innT[i2][:], rhs=vt[j][:, hh * 256:(hh + 1) * 256],
                                                          start=True, stop=True), r=[innTb[i2], inb[j]], w=[bAb])
                    bB, bBb = self.ps()
                    for dc in range(2):
                        mk.op(mk.pe, lambda: nc.tensor.matmul(bB[:, :256], lhsT=qt[j][:, hh * 2 + dc, :], rhs=stb[:, hh, dc, :],
                                                              start=(dc == 0), stop=(dc == 1)), r=[inb[j], stbb], w=[bBb])
                    self.copy(mk.act, O[:, hh, :], bA[:, :256], r=[bAb], w=[Ob])
                    mk.op(mk.dve, lambda: nc.vector.scalar_tensor_tensor(out=O[:, hh, :], in0=bB[:, :256], scalar=QD[:, hh:hh + 1],
                                                                         in1=O[:, hh, :], op0=ALU.mult, op1=ALU.add),
                          r=[bBb, cb, Ob], w=[Ob])
                    mk.op(mk.pool, lambda: nc.gpsimd.tensor_scalar(out=vd[i2][:], in0=vt[j][:, hh * 256:(hh + 1) * 256],
                                                                   scalar1=KD[:, hh:hh + 1], scalar2=None, op0=ALU.mult),
                          r=[inb[j], cb], w=[vdb[i2]])
                    for dc in range(2):
                        bS, bSb = self.ps()
                        mk.op(mk.pe, lambda: nc.tensor.matmul(bS[:, :256], lhsT=ktm[i2][:, dc * 128:(dc + 1) * 128], rhs=vd[i2][:],
                                                              start=True, stop=True), r=[ktmb[i2], vdb[i2]], w=[bSb])
                        mk.op(mk.dve, lambda: nc.vector.scalar_tensor_tensor(out=stf[:, hh, dc, :], in0=stf[:, hh, dc, :],
                                                                             scalar=cdec[hh], in1=bS[:, :256],
                                                                             op0=ALU.mult, op1=ALU.add),
                              r=[bSb, stfb], w=[stfb])
                    self.copy(mk.act, stb[:, hh, :, :], stf[:, hh, :, :], r=[stfb], w=[stbb])
                Sx, Sxb = stat[j], statb[j]
                mk.op(mk.dve, lambda: nc.vector.tensor_reduce(out=Sx[:, 0:4], in_=O[:], axis=AX.X, op=ALU.add), r=[Ob], w=[Sxb])
                mk.op(mk.act, lambda: nc.scalar.activation(out=junk[:], in_=O[:], func=AF.Square), r=[Ob], w=[junkb])
                mk.op(mk.dve, lambda: nc.vector.tensor_reduce(out=Sx[:, 4:8], in_=junk[:], axis=AX.X, op=ALU.add), r=[junkb, Sxb], w=[Sxb])
                mk.op(mk.dve, lambda: nc.vector.tensor_scalar(out=Sx[:, 0:4], in0=Sx[:, 0:4], scalar1=1.0 / 256, scalar2=None,
                                                              op0=ALU.mult), r=[Sxb], w=[Sxb])
                mk.op(mk.dve, lambda: nc.vector.tensor_tensor(out=Sx[:, 8:12], in0=Sx[:, 0:4], in1=Sx[:, 0:4], op=ALU.mult),
                      r=[Sxb], w=[Sxb])
                mk.op(mk.dve, lambda: nc.vector.scalar_tensor_tensor(out=Sx[:, 8:12], in0=Sx[:, 4:8], scalar=1.0 / 256,
                                                                     in1=Sx[:, 8:12], op0=ALU.mult, op1=ALU.subtract),
                      r=[Sxb], w=[Sxb])
                self.rsqrt(Sx[:, 12:16], Sx[:, 8:12], 1e-5, [Sxb], [Sxb])
                for hh in range(4):
                    mk.op(mk.dve, lambda: nc.vector.tensor_scalar(out=O[:, hh, :], in0=O[:, hh, :], scalar1=Sx[:, hh:hh + 1],
                                                                  scalar2=Sx[:, 12 + hh:13 + hh], op0=ALU.subtract, op1=ALU.mult),
                          r=[Ob, Sxb], w=[Ob])
                Of = O[:].rearrange("p h d -> p (h d)")
                mk.op(mk.pool, lambda: nc.gpsimd.tensor_tensor(out=Of, in0=Of, in1=GN[:], op=ALU.mult), r=[Ob, cb], w=[Ob])
                mk.op(mk.dve, lambda: nc.vector.tensor_tensor(out=yo[j][:], in0=Of, in1=gt[j][:], op=ALU.mult),
                      r=[Ob, inb[j]], w=[yob[j]])
                mk.dma(mk.pool, mix_in[cs, 1024:2048], yo[j][:], r=[yob[j]])
            mk.barrier()


def _rel_bucket_np(dist):
    import jax
    import jax.numpy as jnp
    with jax.default_device(jax.devices("cpu")[0]):
        d = jnp.maximum(jnp.asarray(dist, jnp.int32), 0)
        max_exact = 16
        scaled = (jnp.log(jnp.maximum(d, max_exact).astype(jnp.float32) / max_exact)
                  / math.log(1024 / max_exact) * (32 - max_exact))
        large = jnp.minimum(max_exact + scaled.astype(jnp.int32), 31)
        return np.asarray(jnp.where(d < max_exact, d, large))


def nsa_index_tables(S):
    NQT = S // 128
    NCBP = S // 16
    i = np.arange(128)
    toep = np.zeros((128, 10, 128), np.int32)
    for dlt in range(9):
        dist = 128 * dlt + i[None, :] - i[:, None]
        b = _rel_bucket_np(dist)
        toep[:, dlt, :] = np.where(dist >= 0, b, 32)
    dist = 128 * 4 + i[None, :] - i[:, None]
    toep[:, 9, :] = np.where(dist < 512, _rel_bucket_np(dist), 32)
    OFF = 8 * (NQT - 1)
    RE = OFF + NCBP
    npp = np.arange(RE) - OFF
    dist = i[None, :] - 16 * npp[:, None] - 31
    cmpx = np.where(dist >= 0, _rel_bucket_np(dist), 32).astype(np.int32)
    return toep, cmpx, OFF


def nsa_const_tables(S):
    NSB = S // 64
    NCBP = S // 16
    NCB = NCBP - 1
    n = np.arange(NCBP)
    j = np.arange(NSB)
    cs, ce = n * 16, n * 16 + 31
    cover = ((cs[:, None] < j[None, :] * 64 + 64) & (ce[:, None] >= j[None, :] * 64)).astype(np.float32)
    cover[NCB:] = 0.0
    t = np.arange(S)
    cur = (t // 64)[:, None]
    forced = (j[None, :] == 0) | (j[None, :] == cur) | (j[None, :] == cur - 1)
    future = j[None, :] > cur
    keep = np.where(forced | future, 0.0, 1.0).astype(np.float32)
    add = np.where(future, -1e6 - j[None, :], np.where(forced, 1e6 + j[None, :], 0.0)).astype(np.float32)
    NQT = S // 128
    E = np.zeros((64, NQT, 128), np.float32)
    for kt in range(NQT):
        E[2 * kt, kt, :64] = 1.0
        E[2 * kt + 1, kt, 64:] = 1.0
    return cover, keep, add, E


class ProgNSA:
    def nsa_cmp(self, cmp_pos, w1, w2, k_gain0):
        nc, mk = self.nc, self.mk
        S = self.S
        NCBP = S // 16
        NCB = NCBP - 1
        NKT = max(1, NCBP // 128)
        kcmpT = self.scr("nsa_kcmpT", [2, 128, NCBP])
        vcmp = self.scr("nsa_vcmp", [2, NCBP, 128])
        srcs = (self.dram["nsa_kcT"], self.dram["nsa_vcT"])
        with ExitStack() as st:
            gk, gkb = self.load_col(st, k_gain0)
            hn = self.epi_fm_headnorm(st, None, 0, NCBP, gk, gkb)
            W1 = self.sbt(st, "cW1", [128, 32, 256], BF16)
            W2 = self.sbt(st, "cW2", [128, 2, 128], BF16)
            posf = self.sbt(st, "cposf", [32, 128], F32)
            posb = self.sbt(st, "cposb", [32, 128], BF16)
            posT = self.sbt(st, "cposT", [128, 32], BF16)
            xT = [self.sbt(st, "cx", [128, S], BF16) for _ in range(2)]
            hid = self.sbt(st, "chid", [128, 2, NCBP], BF16)
            bias = self.sbt(st, "cbias", [128, 2], F32)
            vo = self.sbt(st, "cvo", [128, NKT, 128], BF16)
            wb, pb_, xb, hb, bb, vob = Buf(), Buf(), [Buf(), Buf()], Buf(), Buf(), Buf()
            mk.op(mk.pool, lambda: nc.gpsimd.memset(hid[:], 0.0), w=[hb])
            xi = 0
            for kv in range(2):
                mk.dma(mk.sp, W1[:], w1[kv].rearrange("(l d) n -> d l n", d=128), r=self.wdep(w1[kv]), w=[wb])
                mk.dma(mk.sp, W2[:], w2[kv].rearrange("(c p) n -> p c n", p=128), r=self.wdep(w2[kv]), w=[wb])
                mk.dma(mk.sp, posf[:], cmp_pos[kv], w=[pb_])
                self.copy(mk.dve, posb[:], posf[:], r=[pb_], w=[pb_])
                bt, btb = self.ps()
                btv = bt[:].bitcast(BF16)
                mk.op(mk.pe, lambda: nc.tensor.transpose(out=btv[:, 0:32], in_=posb[:, :], identity=self.ident_bf[0][0:32, 0:32]),
                      r=[pb_, self.ident_bf[1]], w=[btb])
                self.copy(mk.dve, posT[:], btv[:, 0:32], r=[btb], w=[pb_])
                for hc in range(2):
                    bk, bkb = self.ps()
                    for l in range(32):
                        mk.op(mk.pe, lambda: nc.tensor.matmul(bk[:, 0:1], lhsT=W1[:, l, hc * 128:(hc + 1) * 128], rhs=posT[:, l:l + 1],
                                                              start=(l == 0), stop=(l == 31)), r=[wb, pb_], w=[bkb])
                    self.copy(mk.dve, bias[:, hc:hc + 1], bk[:, 0:1], r=[bkb], w=[bb])
                for g in range(2):
                    x, xbb = xT[xi % 2], xb[xi % 2]
                    xi += 1
                    mk.dma(mk.sp, x[:], srcs[kv][g * 128:(g + 1) * 128, :], w=[xbb])
                    for hc in range(2):
                        bk, bkb = self.ps()
                        for l in range(32):
                            mk.op(mk.pe, lambda: nc.tensor.matmul(bk[:, :NCB], lhsT=W1[:, l, hc * 128:(hc + 1) * 128],
                                                                  rhs=x[:, l:l + 16 * (NCB - 1) + 1:16],
                                                                  start=(l == 0), stop=(l == 31)), r=[wb, xbb], w=[bkb])
                        mk.op(mk.act, lambda: nc.scalar.activation(out=hid[:, hc, :NCB], in_=bk[:, :NCB], func=AF.Silu,
                                                                   bias=bias[:, hc:hc + 1]), r=[bkb, bb], w=[hb])
                    if kv == 0:
                        bk, bkb = self.ps()
                        for hc in range(2):
                            mk.op(mk.pe, lambda: nc.tensor.matmul(bk[:, :NCBP], lhsT=W2[:, hc, :], rhs=hid[:, hc, :],
                                                                  start=(hc == 0), stop=(hc == 1)), r=[wb, hb], w=[bkb])
                        hn([(bk[:, :NCBP], bkb, 0, 128)], 0,
                           store=lambda t, tb_, g=g: mk.dma(mk.pool, kcmpT[g], t[:], r=[tb_]))
                        self.flush()
                    else:
                        for nt in range(NKT):
                            bk, bkb = self.ps()
                            nn = min(128, NCBP)
                            for hc in range(2):
                                mk.op(mk.pe, lambda: nc.tensor.matmul(bk[:nn, :128], lhsT=hid[:, hc, nt * 128:nt * 128 + nn], rhs=W2[:, hc, :],
                                                                      start=(hc == 0), stop=(hc == 1)), r=[wb, hb], w=[bkb])
                            self.copy(self.evac_eng(), vo[:nn, nt, :], bk[:nn, :128], r=[bkb], w=[vob])
                        nn = min(128, NCBP)
                        mk.dma(mk.pool, vcmp[g].rearrange("(t p) d -> p t d", p=nn), vo[:nn, :, :], r=[vob])
            self.flush()
            mk.barrier()

    def nsa_attn(self, rel_toep, rel_cmpx, cmp_off, mix_in):
        nc, mk = self.nc, self.mk
        S = self.S
        NQT, NSB, NCBP = S // 128, S // 64, S // 16
        NKT = max(1, NCBP // 128)
        KP = min(128, NCBP)
        RW = 129 + NSB
        assert NSB >= 16 and NSB <= 64
        if "nsa_cover" not in self.dram:
            cover, keep, add, E = nsa_const_tables(S)
            self.const_dram("nsa_cover", cover.astype(ml_dtypes.bfloat16), BF16)
            self.const_dram("nsa_keep", keep, F32)
            self.const_dram("nsa_add", add, F32)
            self.const_dram("nsa_E", E.astype(ml_dtypes.bfloat16), BF16)
        qT, ksT, kwT, vs, vw, gate = (self.dram[k] for k in ("nsa_qT", "nsa_ksT", "nsa_kwT", "nsa_vs", "nsa_vw", "nsa_gate"))
        kcmpT, vcmp = self.dram["nsa_kcmpT"], self.dram["nsa_vcmp"]
        save_rot = self.rot
        self.rot = [0, 1, 2, 3]
        accs = [(self.pb[4 + i], self.pbb[4 + i]) for i in range(4)]
        with ExitStack() as st:
            KcT = self.sbt(st, "nKcT", [128, NCBP], BF16)
            VC = self.sbt(st, "nVC", [128, NKT, RW], BF16)
            KsT = self.sbt(st, "nKsT", [128, S], BF16)
            KwT = self.sbt(st, "nKwT", [128, S], BF16)
            Vs = self.sbt(st, "nVs", [128, NQT, 129], BF16)
            Vw = self.sbt(st, "nVw", [128, NQT, 129], BF16)
            TP = self.sbt(st, "nTP", [128, 10, 4, 128], BF16)
            E = self.sbt(st, "nE", [64, NQT, 128], BF16)
            resb = Buf()
            mk.dma(mk.sp, E[:], self.dram["nsa_E"], w=[resb])
            Q = [self.sbt(st, "nQ", [128, 4, 128], BF16) for _ in range(2)]
            G = [self.sbt(st, "nG", [128, 12], F32) for _ in range(2)]
            KA = [self.sbt(st, "nKA", [128, 2, NSB], F32) for _ in range(2)]
            CB = [self.sbt(st, "nCB", [128, NKT, 4, 128], BF16) for _ in range(2)]
            inb = [Buf(), Buf()]
            Pt = [self.sbt(st, "nP", [128, 512], BF16) for _ in range(4)]
            Ptb = [Buf() for _ in range(4)]
            pti = 0
            acc = [self.sbt(st, "nacc", [128, 4, 128], F32) for _ in range(2)]
            accb = [Buf(), Buf()]
            accC = [self.sbt(st, "naccC", [128, 4, RW], F32) for _ in range(2)]
            accCb = [Buf(), Buf()]
            accW = [self.sbt(st, "naccW", [128, 4, 129], F32) for _ in range(2)]
            accWb = [Buf(), Buf()]
            ob = [self.sbt(st, "nob", [128, 512], BF16) for _ in range(2)]
            obb = [Buf(), Buf()]
            sm = [self.sbt(st, "nsm", [128, 16], F32) for _ in range(2)]
            smb = [Buf(), Buf()]
            imp = [self.sbt(st, "nimp", [128, 3, NSB], F32) for _ in range(2)]
            impb = [Buf(), Buf()]
            m8 = [self.sbt(st, "nm8", [128, 16], F32) for _ in range(2)]
            mng = [self.sbt(st, "nmng", [128, NSB], BF16) for _ in range(2)]
            mT4 = [self.sbt(st, "nmT4", [64, 4, 128], BF16) for _ in range(2)]
            mT4b = [Buf(), Buf()]
            for g in range(2):
                mk.dma(mk.sp, KcT[:KP, :] if False else KcT[:, :], kcmpT[g], w=[resb])
                mk.op(mk.pool, lambda: nc.gpsimd.memset(VC[:], 1.0), w=[resb])
                mk.op(mk.pool, lambda: nc.gpsimd.memset(Vs[:], 1.0), w=[resb])
                mk.op(mk.pool, lambda: nc.gpsimd.memset(Vw[:], 1.0), w=[resb])
                mk.dma(mk.sp, VC[:KP, :, 0:128], vcmp[g].rearrange("(t p) d -> p t d", p=KP), w=[resb])
                mk.dma(mk.sp, VC[:KP, :, 129:RW], self.dram["nsa_cover"].rearrange("(t p) j -> p t j", p=KP), w=[resb])
                mk.dma(mk.sp, KsT[:], ksT[g * 128:(g + 1) * 128, :], w=[resb])
                mk.dma(mk.sp, KwT[:], kwT[g * 128:(g + 1) * 128, :], w=[resb])
                mk.dma(mk.sp, Vs[:, :, 0:128], vs[:, g * 128:(g + 1) * 128].rearrange("(t p) d -> p t d", p=128), w=[resb])
                mk.dma(mk.sp, Vw[:, :, 0:128], vw[:, g * 128:(g + 1) * 128].rearrange("(t p) d -> p t d", p=128), w=[resb])
                for ti in range(10):
                    mk.dma(mk.pool, TP[:, ti], rel_toep[:, ti, g * 4:(g + 1) * 4, :], w=[resb])
                for qt in range(NQT):
                    j = qt % 2
                    rows = slice(qt * 128, (qt + 1) * 128)
                    mk.dma(mk.sp, Q[j][:], qT[g * 512:(g + 1) * 512, rows].rearrange("(h p) s -> p h s", p=128), w=[inb[j]])
                    mk.dma(mk.sp, G[j][:], gate[rows, g * 12:(g + 1) * 12], w=[inb[j]])
                    mk.dma(mk.sp, KA[j][:, 0, :], self.dram["nsa_keep"][rows, :], w=[inb[j]])
                    mk.dma(mk.sp, KA[j][:, 1, :], self.dram["nsa_add"][rows, :], w=[inb[j]])
                    r0 = cmp_off - 8 * qt
                    mk.dma(mk.pool, CB[j][:KP], rel_cmpx[r0:r0 + NCBP, g * 4:(g + 1) * 4, :].rearrange("(t p) h i -> p t h i", p=KP),
                           w=[inb[j]])
                    Qf = Q[j][:].rearrange("p h s -> p (h s)")
                    A, Ab = acc[j], accb[j]
                    SM, SMb = sm[j], smb[j]

                    def branch(kts, klhs, extra, Vt, vsl, width, first):
                        nonlocal pti
                        banks = accs
                        def pv(ki, kt, pi):
                            for hh in range(4):
                                bk, bkb = banks[hh]
                                reg = bk[:, 0:width]
                                mk.op(mk.pe, lambda: nc.tensor.matmul(reg, lhsT=Pt[pi][:, hh * 128:(hh + 1) * 128], rhs=vsl(kt),
                                                                      start=(ki == 0), stop=(ki == len(kts) - 1)),
                                      r=[Ptb[pi], resb], w=[bkb])
                        prev = None
                        for ki, kt in enumerate(kts):
                            bank, bankb = self.ps()
                            mms = [(klhs(kt), Qf, [resb, inb[j]])] + extra(kt)
                            for mi, (l_, r_, bufs) in enumerate(mms):
                                mk.op(mk.pe, lambda: nc.tensor.matmul(bank[:, :512], lhsT=l_, rhs=r_, start=(mi == 0),
                                                                      stop=(mi == len(mms) - 1)), r=bufs, w=[bankb])
                            pi = pti % 4
                            pti += 1
                            mk.op(mk.act, lambda: nc.scalar.activation(out=Pt[pi][:], in_=bank[:, :512], func=AF.Exp),
                                  r=[bankb], w=[Ptb[pi]])
                            if prev is not None:
                                pv(*prev)
                            prev = (ki, kt, pi)
                        pv(*prev)
                        return banks

                    def evac(banks, width, dst, dstb):
                        for hh in range(4):
                            bk, bkb = banks[hh]
                            self.copy(mk.act, dst[:, hh, 0:width], bk[:, 0:width], r=[bkb], w=[dstb])

                    def sums(src, srcb, gi):
                        c0 = 4 * gi
                        mk.op(mk.dve, lambda: nc.vector.tensor_scalar(out=SM[:, c0:c0 + 4], in0=src[:, :, 128], scalar1=1e-30,
                                                                      scalar2=None, op0=ALU.max), r=[srcb], w=[SMb])
                        mk.op(mk.dve, lambda: nc.vector.reciprocal(out=SM[:, c0:c0 + 4], in_=SM[:, c0:c0 + 4]), r=[SMb], w=[SMb])
                        mk.op(mk.dve, lambda: nc.vector.tensor_tensor(out=SM[:, 12:16], in0=SM[:, c0:c0 + 4],
                                                                      in1=G[j][:].rearrange("p (h k) -> p h k", k=3)[:, :, gi], op=ALU.mult),
                              r=[SMb, inb[j]], w=[SMb])

                    def accum(src, srcb, first):
                        for hh in range(4):
                            if first:
                                mk.op(mk.dve, lambda: nc.vector.tensor_scalar(out=A[:, hh, :], in0=src[:, hh, 0:128],
                                                                              scalar1=SM[:, 12 + hh:13 + hh], scalar2=None,
                                                                              op0=ALU.mult), r=[srcb, SMb], w=[Ab])
                            else:
                                mk.op(mk.dve, lambda: nc.vector.scalar_tensor_tensor(out=A[:, hh, :], in0=src[:, hh, 0:128],
                                                                                     scalar=SM[:, 12 + hh:13 + hh], in1=A[:, hh, :],
                                                                                     op0=ALU.mult, op1=ALU.add),
                                      r=[srcb, SMb, Ab], w=[Ab])

                    assert KP == 128
                    kts_c = [kt for kt in range(NKT) if 8 * qt + 6 >= kt * 128]
                    banks = branch(kts_c, lambda kt: KcT[:, kt * KP:(kt + 1) * KP],
                                   lambda kt: [(self.ident_bf[0][:KP, :KP], CB[j][:KP, kt].rearrange("p h s -> p (h s)"),
                                                [self.ident_bf[1], inb[j]])],
                                   VC, lambda kt: VC[:KP, kt, :], RW, True)
                    AC, ACb = accC[j], accCb[j]
                    evac(banks, RW, AC, ACb)
                    sums(AC, ACb, 0)
                    IM, IMb = imp[j], impb[j]
                    for hh in range(4):
                        reg = AC[:, hh, 129:RW]
                        if hh == 0:
                            mk.op(mk.dve, lambda: nc.vector.tensor_scalar(out=IM[:, 0, :], in0=reg, scalar1=SM[:, 0:1], scalar2=None,
                                                                          op0=ALU.mult), r=[ACb, SMb], w=[IMb])
                        else:
                            mk.op(mk.dve, lambda: nc.vector.scalar_tensor_tensor(out=IM[:, 0, :], in0=reg, scalar=SM[:, hh:hh + 1],
                                                                                 in1=IM[:, 0, :], op0=ALU.mult, op1=ALU.add),
                                  r=[ACb, SMb, IMb], w=[IMb])
                    mk.op(mk.dve, lambda: nc.vector.tensor_tensor(out=IM[:, 0, :], in0=IM[:, 0, :], in1=KA[j][:, 0, :], op=ALU.mult),
                          r=[IMb, inb[j]], w=[IMb])
                    mk.op(mk.dve, lambda: nc.vector.tensor_tensor(out=IM[:, 0, :], in0=IM[:, 0, :], in1=KA[j][:, 1, :], op=ALU.add),
                          r=[IMb, inb[j]], w=[IMb])
                    M8 = m8[j]
                    mk.op(mk.dve, lambda: nc.vector.max(out=M8[:, 0:8], in_=IM[:, 0, :]), r=[IMb], w=[IMb])
                    mk.op(mk.dve, lambda: nc.vector.match_replace(out=IM[:, 1, :], in_to_replace=M8[:, 0:8], in_values=IM[:, 0, :],
                                                                  imm_value=-3.0e6), r=[IMb], w=[IMb])
                    mk.op(mk.dve, lambda: nc.vector.max(out=M8[:, 8:16], in_=IM[:, 1, :]), r=[IMb], w=[IMb])
                    mk.op(mk.dve, lambda: nc.vector.tensor_scalar(out=IM[:, 2, :], in0=IM[:, 0, :], scalar1=M8[:, 15:16], scalar2=None,
                                                                  op0=ALU.is_ge), r=[IMb], w=[IMb])
                    mk.op(mk.dve, lambda: nc.vector.tensor_scalar(out=mng[j][:], in0=IM[:, 2, :], scalar1=1.0, scalar2=BIG,
                                                                  op0=ALU.subtract, op1=ALU.mult), r=[IMb], w=[IMb])
                    accum(AC, ACb, True)
                    banks = branch(list(range(max(0, qt - 4), qt + 1)), lambda kt: KwT[:, kt * 128:(kt + 1) * 128],
                                   lambda kt: [(self.ident_bf[0][:], TP[:, (qt - kt) if qt - kt < 4 else 9].rearrange("p h s -> p (h s)"),
                                                [self.ident_bf[1], resb])],
                                   Vw, lambda kt: Vw[:, kt, :], 129, False)
                    AW, AWb = accW[j], accWb[j]
                    evac(banks, 129, AW, AWb)
                    bt, btb = self.ps()
                    btv = bt[:].bitcast(BF16)
                    mk.op(mk.pe, lambda: nc.tensor.transpose(out=btv[:NSB, 0:128], in_=mng[j][:, :], identity=self.ident_bf[0][:]),
                          r=[IMb, self.ident_bf[1]], w=[btb])
                    mk.op(mk.dve, lambda: nc.vector.tensor_copy(out=mT4[j][:NSB], in_=btv[:NSB, 0:128].unsqueeze(1).to_broadcast([NSB, 4, 128])),
                          r=[btb], w=[mT4b[j]])
                    mTf = mT4[j][:NSB].rearrange("p h s -> p (h s)")
                    sums(AW, AWb, 2)
                    accum(AW, AWb, False)
                    banks = branch(list(range(qt + 1)), lambda kt: KsT[:, kt * 128:(kt + 1) * 128],
                                   lambda kt: [(self.ident_bf[0][:], TP[:, min(qt - kt, 8)].rearrange("p h s -> p (h s)"),
                                                [self.ident_bf[1], resb]),
                                               (E[:NSB, kt, :], mTf, [resb, mT4b[j]])],
                                   Vs, lambda kt: Vs[:, kt, :], 129, False)
                    AS, ASb = accW[j], accWb[j]
                    evac(banks, 129, AS, ASb)
                    sums(AS, ASb, 1)
                    accum(AS, ASb, False)
                    self.copy(mk.act, ob[j][:], A[:].rearrange("p h d -> p (h d)"), r=[Ab], w=[obb[j]])
                    mk.dma(mk.pool, mix_in[rows, g * 512:(g + 1) * 512], ob[j][:], r=[obb[j]])
                mk.barrier()
        self.rot = save_rot


for _n, _f in list(ProgNSA.__dict__.items()):
    if callable(_f):
        setattr(Prog, _n, _f)


class ProgGDN:
    def gdn_in(self, h, ln, w_in, conv_w):
        nc, mk = self.nc, self.mk
        S = self.S
        qT = self.scr("gdn_qT", [2048, S])
        kT = self.scr("gdn_kT", [2048, S])
        vT = self.scr("gdn_vT", [4096, S])
        zs = self.scr("gdn_zs", [S, 4096])
        ba = self.scr("gdn_ba", [S, 64], F32)

        def jobs(st, TB):
            cw4 = self.sbt(st, "cw4", [4, 8192], F32)
            CW = self.sbt(st, "CW", [128, 64, 4], F32)
            cwb = Buf()
            mk.dma(mk.sp, cw4[:], conv_w, w=[cwb])
            for c0 in range(0, 64, 32):
                bk, bkb = self.ps()
                for c in range(c0, c0 + 32):
                    mk.op(mk.pe, lambda: nc.tensor.transpose(out=bk[:, (c - c0) * 4:(c - c0) * 4 + 4], in_=cw4[0:4, c * 128:(c + 1) * 128],
                                                             identity=self.ident_f[0][0:4, 0:4]), r=[cwb, self.ident_f[1]], w=[bkb])
                self.copy(mk.dve, CW[:, c0:c0 + 32, :], bk[:, 0:128].rearrange("p (c k) -> p c k", k=4), r=[bkb], w=[cwb])
            HL = self.sbt(st, "HL", [128, 64, 3], F32)
            hlb = Buf()
            mk.op(mk.pool, lambda: nc.gpsimd.memset(HL[:], 0.0), w=[hlb])
            tmp = [self.sbt(st, "cvt", [128, TB + 3], F32) for _ in range(2)]
            tmpb = [Buf(), Buf()]
            accs = [self.sbt(st, "cva", [128, TB], F32) for _ in range(2)]
            accb = [Buf(), Buf()]
            sil = [self.sbt(st, "cvs", [128, TB], F32) for _ in range(4)]
            silb = [Buf() for _ in range(4)]
            silh = [self.sbt(st, "cvsh", [128, TB], BF16) for _ in range(2)]
            silhb = [Buf(), Buf()]
            one = self.sbt(st, "onec", [128, 2], F32)
            oneb = Buf()
            mk.op(mk.pool, lambda: nc.gpsimd.memset(one[:, 0:1], 128 ** -0.5), w=[oneb])
            mk.op(mk.pool, lambda: nc.gpsimd.memset(one[:, 1:2], 1.0), w=[oneb])
            hn = self.epi_fm_headnorm(st, None, 0, TB, one, oneb, mean=False, eps=EPS)
            cnt = [0]

            def conv(grp, tb):
                (ps, psb, col, cw), = grp
                c = col // 128
                j = cnt[0] % 2
                cnt[0] += 1
                T, Tb = tmp[j], tmpb[j]
                self.copy(mk.act, T[:, 3:TB + 3], ps, r=[psb], w=[Tb])

                def stage2():
                    mk.op(mk.dve, lambda: nc.vector.tensor_copy(out=T[:, 0:3], in_=HL[:, c, :]), r=[hlb], w=[Tb])
                    mk.op(mk.dve, lambda: nc.vector.tensor_copy(out=HL[:, c, :], in_=T[:, TB:TB + 3]), r=[Tb], w=[hlb])
                    A, Ab = accs[j], accb[j]
                    mk.op(mk.dve, lambda: nc.vector.tensor_scalar(out=A[:], in0=T[:, 3:TB + 3], scalar1=CW[:, c, 3:4], scalar2=None,
                                                                  op0=ALU.mult), r=[Tb, cwb], w=[Ab])
                    for k in (2, 1, 0):
                        mk.op(mk.dve, lambda: nc.vector.scalar_tensor_tensor(out=A[:], in0=T[:, k:k + TB], scalar=CW[:, c, k:k + 1], in1=A[:],
                                                                             op0=ALU.mult, op1=ALU.add), r=[Tb, cwb, Ab], w=[Ab])

                    def stage3():
                        if c < 32:
                            j4 = c % 4
                            mk.op(mk.act, lambda: nc.scalar.activation(out=sil[j4][:], in_=A[:], func=AF.Silu), r=[Ab], w=[silb[j4]])
                            dst = qT if c < 16 else kT
                            r0 = (c % 16) * 128
                            hn([(sil[j4][:], silb[j4], 0, 128)], tb, gi=(0 if c < 16 else 1), deep=True,
                               store=lambda t, tb_: mk.dma(mk.pool, dst[r0:r0 + 128, tb * TB:(tb + 1) * TB], t[:], r=[tb_]))
                        else:
                            mk.op(mk.act, lambda: nc.scalar.activation(out=silh[j][:], in_=A[:], func=AF.Silu), r=[Ab], w=[silhb[j]])
                            r0 = (c - 32) * 128
                            mk.dma(mk.pool, vT[r0:r0 + 128, tb * TB:(tb + 1) * TB], silh[j][:], r=[silhb[j]])
                    self.defer(stage3)
                self.defer(stage2)
            return [dict(W=w_in, c0=0, c1=8192, layout="FM", epi=conv),
                    dict(W=w_in, c0=8192, c1=12288, layout="TM", epi=self.epi_tm_store(st, zs, -8192, TB, func=AF.Silu)),
                    dict(W=w_in, c0=12288, c1=12352, layout="TM", epi=self.epi_tm_store(st, ba, -12288, TB, dt=F32, width=64))]
        self.linear("gdn", h, S, D, ln, jobs)

    def gdn_gates(self, a_log, dt_bias):
        nc, mk = self.nc, self.mk
        S = self.S
        ba = self.dram["gdn_ba"]
        gt = self.scr("gdn_gt", [S, 5, 32], F32)
        X = self.scr("gdn_X", [S, 3, 64], F32)
        if "gdn_tri" not in self.dram:
            m = np.arange(128)
            self.const_dram("gdn_tri", (m[:, None] <= m[None, :]).astype(np.float32), F32)
            l = np.zeros((128, 128), np.float32)
            l[127, :] = 1.0
            self.const_dram("gdn_l127", l, F32)
        with ExitStack() as st:
            TRI = self.sbt(st, "gTRI", [128, 128], F32)
            L127 = self.sbt(st, "gL127", [128, 128], F32)
            AL = self.sbt(st, "gAL", [128, 32], F32)
            DTB = self.sbt(st, "gDTB", [128, 32], F32)
            cb = Buf()
            mk.dma(mk.sp, TRI[:], self.dram["gdn_tri"], w=[cb])
            mk.dma(mk.sp, L127[:], self.dram["gdn_l127"], w=[cb])
            mk.dma(mk.sp, AL[:], a_log.partition_broadcast(128), w=[cb])
            mk.dma(mk.sp, DTB[:], dt_bias.partition_broadcast(128), w=[cb])
            mk.op(mk.act, lambda: nc.scalar.activation(out=AL[:], in_=AL[:], func=AF.Exp), r=[cb], w=[cb])
            inn = [self.sbt(st, "gin", [128, 64], F32) for _ in range(2)]
            innb = [Buf(), Buf()]
            W = [self.sbt(st, "gW", [128, 8, 32], F32) for _ in range(2)]
            Wb = [Buf(), Buf()]
            GT = [self.sbt(st, "gGT", [128, 5, 32], F32) for _ in range(2)]
            GTb = [Buf(), Buf()]
            XX = [self.sbt(st, "gXX", [128, 3, 64], F32) for _ in range(2)]
            XXb = [Buf(), Buf()]
            for n in range(S // 128):
                j = n % 2
                rows = slice(n * 128, (n + 1) * 128)
                I, Ib, w, wb, G, Gb, Xt, Xb = inn[j], innb[j], W[j], Wb[j], GT[j], GTb[j], XX[j], XXb[j]
                mk.dma(mk.sp, I[:], ba[rows, :], w=[Ib])
                mk.op(mk.act, lambda: nc.scalar.activation(out=w[:, 0, :], in_=I[:, 0:32], func=AF.Exp, scale=-1.0), r=[Ib], w=[wb])
                mk.op(mk.dve, lambda: nc.vector.tensor_scalar(out=w[:, 0, :], in0=w[:, 0, :], scalar1=1.0, scalar2=None, op0=ALU.add),
                      r=[wb], w=[wb])
                mk.op(mk.dve, lambda: nc.vector.reciprocal(out=G[:, 0, :], in_=w[:, 0, :]), r=[wb], w=[Gb])
                mk.op(mk.act, lambda: nc.scalar.activation(out=w[:, 1, :], in_=w[:, 0, :], func=AF.Ln), r=[wb], w=[wb])
                mk.op(mk.dve, lambda: nc.vector.tensor_tensor(out=w[:, 2, :], in0=I[:, 32:64], in1=DTB[:], op=ALU.add), r=[Ib, cb], w=[wb])
                mk.op(mk.act, lambda: nc.scalar.activation(out=w[:, 2, :], in_=w[:, 2, :], func=AF.Exp), r=[wb], w=[wb])
                mk.op(mk.dve, lambda: nc.vector.tensor_scalar(out=w[:, 2, :], in0=w[:, 2, :], scalar1=1.0, scalar2=None, op0=ALU.add),
                      r=[wb], w=[wb])
                mk.op(mk.act, lambda: nc.scalar.activation(out=w[:, 2, :], in_=w[:, 2, :], func=AF.Ln), r=[wb], w=[wb])
                mk.op(mk.dve, lambda: nc.vector.scalar_tensor_tensor(out=w[:, 3, :], in0=w[:, 2, :], scalar=-1.0, in1=AL[:],
                                                                     op0=ALU.mult, op1=ALU.mult), r=[wb, cb], w=[wb])
                bk, bkb = self.ps()
                mk.op(mk.pe, lambda: nc.tensor.matmul(bk[:, 0:32], lhsT=TRI[:], rhs=w[:, 3, :], start=True, stop=True), r=[cb, wb], w=[bkb])
                self.copy(mk.dve, w[:, 4, :], bk[:, 0:32], r=[bkb], w=[wb])
                bk2, bk2b = self.ps()
                mk.op(mk.pe, lambda: nc.tensor.matmul(bk2[:, 0:32], lhsT=L127[:], rhs=w[:, 4, :], start=True, stop=True), r=[cb, wb], w=[bk2b])
                self.copy(mk.dve, w[:, 5, :], bk2[:, 0:32], r=[bk2b], w=[wb])
                mk.op(mk.act, lambda: nc.scalar.activation(out=G[:, 1, :], in_=w[:, 4, :], func=AF.Exp), r=[wb], w=[Gb])
                mk.op(mk.dve, lambda: nc.vector.tensor_tensor(out=G[:, 2, :], in0=G[:, 0, :], in1=G[:, 1, :], op=ALU.mult), r=[Gb], w=[Gb])
                mk.op(mk.dve, lambda: nc.vector.tensor_tensor(out=w[:, 6, :], in0=w[:, 5, :], in1=w[:, 4, :], op=ALU.subtract), r=[wb], w=[wb])
                mk.op(mk.act, lambda: nc.scalar.activation(out=G[:, 3, :], in_=w[:, 6, :], func=AF.Exp), r=[wb], w=[Gb])
                mk.op(mk.act, lambda: nc.scalar.activation(out=G[:, 4, :], in_=w[:, 5, :], func=AF.Exp), r=[wb], w=[Gb])
                mk.dma(mk.pool, gt[rows], G[:], r=[Gb])
                mk.op(mk.pool, lambda: nc.gpsimd.memset(Xt[:], 1.0), w=[Xb])
                mk.op(mk.dve, lambda: nc.vector.tensor_scalar(out=Xt[:, 0, 0:32], in0=w[:, 4, :], scalar1=-1.0, scalar2=None, op0=ALU.mult),
                      r=[wb], w=[Xb])
                self.copy(mk.dve, Xt[:, 1, 32:64], w[:, 4, :], r=[wb], w=[Xb])
                mk.op(mk.dve, lambda: nc.vector.tensor_tensor(out=Xt[:, 2, 32:64], in0=w[:, 4, :], in1=w[:, 1, :], op=ALU.subtract),
                      r=[wb], w=[Xb])
                mk.dma(mk.pool, X[rows], Xt[:], r=[Xb])
            mk.barrier()

    def gdn_chunks(self, norm_gain, gdn_o):
        nc, mk = self.nc, self.mk
        S = self.S
        NCH = S // 128
        F32R = F32
        qT, kT, vT, zs, gt, X = (self.dram[k] for k in ("gdn_qT", "gdn_kT", "gdn_vT", "gdn_zs", "gdn_gt", "gdn_X"))
        if "gdn_mstrict" not in self.dram:
            m = np.arange(128)
            ms = np.where(m[None, :] > m[:, None], 0.0, -1e4).astype(np.float32)
            mi = np.where(m[None, :] >= m[:, None], 0.0, -1e4).astype(np.float32)
            self.const_dram("gdn_mstrict", np.tile(ms[:, None, :], (1, 4, 1)), F32)
            self.const_dram("gdn_mincl", np.tile(mi[:, None, :], (1, 4, 1)), F32)
            oh = np.zeros((64, 32), np.float32)
            for k in range(64):
                oh[k, k % 32] = 1.0
            self.const_dram("gdn_oh", oh, F32)
            self.const_dram("gdn_i4", np.tile(np.eye(128, dtype=np.float32)[:, None, :], (1, 4, 1)), F32)
        with ExitStack() as st:
            MS = self.sbt(st, "dMS", [128, 4, 128], F32)
            MI = self.sbt(st, "dMI", [128, 4, 128], F32)
            OH = self.sbt(st, "dOH", [64, 32], F32)
            I4 = self.sbt(st, "dI4", [128, 4, 128], F32)
            NG = self.sbt(st, "dNG", [128, 128], F32)
            cb = Buf()
            mk.dma(mk.sp, MS[:], self.dram["gdn_mstrict"], w=[cb])
            mk.dma(mk.sp, MI[:], self.dram["gdn_mincl"], w=[cb])
            mk.dma(mk.sp, OH[:], self.dram["gdn_oh"], w=[cb])
            mk.dma(mk.sp, I4[:], self.dram["gdn_i4"], w=[cb])
            mk.dma(mk.sp, NG[:], norm_gain.partition_broadcast(128), w=[cb])
            Sf = self.sbt(st, "dSf", [128, 32, 128], F32)
            Sb = self.sbt(st, "dSb", [128, 32, 128], BF16)
            Sfb = [Buf() for _ in range(8)]
            Sbb = [Buf() for _ in range(8)]
            mk.op(mk.pool, lambda: nc.gpsimd.memset(Sf[:], 0.0), w=Sfb)
            mk.op(mk.pool, lambda: nc.gpsimd.memset(Sb[:], 0.0), w=Sbb)
            GTt = [self.sbt(st, "dGT", [128, 5, 32], F32) for _ in range(2)]
            XXt = [self.sbt(st, "dXX", [128, 3, 64], F32) for _ in range(2)]
            XT = [self.sbt(st, "dXT", [64, 3, 128], F32) for _ in range(2)]
            LA = [self.sbt(st, "dLA", [64, 32, 128], F32) for _ in range(2)]
            chb = [Buf(), Buf()]
            NS = 4

            def two(name, shape, dt):
                return [self.sbt(st, name, shape, dt) for _ in range(NS)], [Buf() for _ in range(NS)]
            kTt, kTb = two("dkT", [128, 2, 128], BF16)
            qTt, qTb = two("dqT", [128, 2, 128], BF16)
            vTt, vTb = two("dvT", [128, 4, 128], BF16)
            zst, zsb = two("dzs", [128, 512], BF16)
            vbt, vbb = two("dvb", [128, 4, 128], BF16)
            kbg, kbgb = two("dkbg", [128, 4, 128], BF16)
            kdc, kdcb = two("dkdc", [128, 4, 128], BF16)
            EYt, EYb = two("dEY", [128, 4, 128], F32)
            EAt, EAb = two("dEA", [128, 4, 128], F32)
            YPt = [[self.sbt(st, "dYP", [128, 4, 2, 128], F32R) for _ in range(2)] for _ in range(NS)]
            YTt = [[self.sbt(st, "dYT", [128, 5, 128], F32R) for _ in range(2)] for _ in range(NS)]
            YPb = [[Buf(), Buf()] for _ in range(NS)]
            YTb = [[Buf(), Buf()] for _ in range(NS)]
            for sl in range(NS):
                for pp in range(2):
                    self.copy(mk.dve, YTt[sl][pp][:], Sf[:, 0:5, :], r=[Sfb[0]], w=[YTb[sl][pp]])
            atT, atTb = two("datT", [128, 4, 128], BF16)
            TT, TTb = two("dTT", [128, 4, 128], BF16)
            usb, usbb = two("du", [128, 4, 128], F32)
            wT, wTb = two("dwT", [128, 4, 128], BF16)
            vnw, vnwb = two("dvn", [128, 4, 128], BF16)
            osb, osbb = two("do", [128, 4, 128], F32)
            junk, junkb = two("djunk", [128, 4, 128], F32)
            stt_, sttb = two("dst", [128, 8], F32)
            yo, yob = two("dyo", [128, 512], BF16)

            def f4(t):
                return t[:].rearrange("p h c -> p (h c)")

            class Ctx:
                pass

            def ph_load(c):
                j, hg, rows = c.j, c.hg, c.rows
                mk.dma(mk.sp, kTt[j][:], kT[hg * 256:(hg + 1) * 256, rows].rearrange("(h p) s -> p h s", p=128), w=[kTb[j]])
                mk.dma(mk.sp, qTt[j][:], qT[hg * 256:(hg + 1) * 256, rows].rearrange("(h p) s -> p h s", p=128), w=[qTb[j]])
                mk.dma(mk.sp, vTt[j][:], vT[hg * 512:(hg + 1) * 512, rows].rearrange("(h p) s -> p h s", p=128), w=[vTb[j]])
                mk.dma(mk.sp, zst[j][:], zs[rows, hg * 512:(hg + 1) * 512], w=[zsb[j]])
                bk, bkb = self.ps()
                bv = bk[:].bitcast(BF16)
                for qh in range(2):
                    mk.op(mk.pe, lambda: nc.tensor.transpose(out=bv[:, qh * 128:(qh + 1) * 128], in_=kTt[j][:, qh, :],
                                                             identity=self.ident_bf[0][:]), r=[kTb[j], self.ident_bf[1]], w=[bkb])
                for hh in range(4):
                    mk.op(mk.pe, lambda: nc.tensor.transpose(out=bv[:, (2 + hh) * 128:(3 + hh) * 128], in_=vTt[j][:, hh, :],
                                                             identity=self.ident_bf[0][:]), r=[vTb[j], self.ident_bf[1]], w=[bkb])
                G_ = c.GT_
                H0 = c.H0
                kv4 = bv[:, 0:256].rearrange("p (h d) -> p h d", d=128).unsqueeze(2).to_broadcast([128, 2, 2, 128])
                vv4 = bv[:, 256:768].rearrange("p (h d) -> p h d", d=128)

                def sc(i):
                    return G_[:, i, H0:H0 + 4]
                mk.op(mk.dve, lambda: nc.vector.tensor_tensor(out=vbt[j][:], in0=vv4, in1=sc(0).unsqueeze(2).to_broadcast([128, 4, 128]),
                                                              op=ALU.mult), r=[bkb, c.cbb], w=[vbb[j]])
                mk.op(mk.dve, lambda: nc.vector.tensor_tensor(out=kbg[j][:].rearrange("p (a b) d -> p a b d", b=2), in0=kv4,
                                                              in1=sc(2).rearrange("p (a b) -> p a b", b=2).unsqueeze(3).to_broadcast([128, 2, 2, 128]),
                                                              op=ALU.mult), r=[bkb, c.cbb], w=[kbgb[j]])
                mk.op(mk.dve, lambda: nc.vector.tensor_tensor(out=kdc[j][:].rearrange("p (a b) d -> p a b d", b=2), in0=kv4,
                                                              in1=sc(3).rearrange("p (a b) -> p a b", b=2).unsqueeze(3).to_broadcast([128, 2, 2, 128]),
                                                              op=ALU.mult), r=[bkb, c.cbb], w=[kdcb[j]])

            def ph_gram1(c):
                j = c.j
                c.bDY, c.bDYb = self.ps()
                c.bDA, c.bDAb = self.ps()
                for hh in range(4):
                    mk.op(mk.pe, lambda: nc.tensor.matmul(c.bDY[:, hh * 128:(hh + 1) * 128], lhsT=c.LA_[:, c.H0 + hh, :], rhs=c.XT_[:, 2, :],
                                                          start=True, stop=True), r=[c.cbb], w=[c.bDYb])
                    mk.op(mk.pe, lambda: nc.tensor.matmul(c.bDA[:, hh * 128:(hh + 1) * 128], lhsT=c.LA_[:, c.H0 + hh, :], rhs=c.XT_[:, 1, :],
                                                          start=True, stop=True), r=[c.cbb], w=[c.bDAb])

            def ph_gram2(c):
                j = c.j
                EY, EA = EYt[j], EAt[j]
                mk.op(mk.dve, lambda: nc.vector.scalar_tensor_tensor(out=f4(EY), in0=c.bDY[:, :], scalar=0.0, in1=f4(MS),
                                                                     op0=ALU.min, op1=ALU.add), r=[c.bDYb, cb], w=[EYb[j]])
                mk.op(mk.dve, lambda: nc.vector.scalar_tensor_tensor(out=f4(EA), in0=c.bDA[:, :], scalar=0.0, in1=f4(MI),
                                                                     op0=ALU.min, op1=ALU.add), r=[c.bDAb, cb], w=[EAb[j]])

            def ph_gram3(c):
                j = c.j
                EY, EA = EYt[j], EAt[j]
                mk.op(mk.act, lambda: nc.scalar.activation(out=f4(EY), in_=f4(EY), func=AF.Exp), r=[EYb[j]], w=[EYb[j]])
                mk.op(mk.act, lambda: nc.scalar.activation(out=f4(EA), in_=f4(EA), func=AF.Exp), r=[EAb[j]], w=[EAb[j]])
                self.copy(mk.act, YPt[j][0][:, :, 1, :], I4[:], r=[cb], w=[YPb[j][0]])
                c.bG, c.bGb = self.ps()
                bG, bGb = c.bG, c.bGb
                for qh in range(2):
                    mk.op(mk.pe, lambda: nc.tensor.matmul(bG[:, qh * 128:(qh + 1) * 128], lhsT=kTt[j][:, qh, :], rhs=kTt[j][:, qh, :],
                                                          start=True, stop=True), r=[kTb[j]], w=[bGb])
                    mk.op(mk.pe, lambda: nc.tensor.matmul(bG[:, (2 + qh) * 128:(3 + qh) * 128], lhsT=kTt[j][:, qh, :], rhs=qTt[j][:, qh, :],
                                                          start=True, stop=True), r=[kTb[j], qTb[j]], w=[bGb])

            def ph_gram4(c):
                j = c.j
                EY, EA = EYt[j], EAt[j]
                bG, bGb = c.bG, c.bGb
                YP0 = YPt[j][0]
                for qh in range(2):
                    Gv = bG[:, qh * 128:(qh + 1) * 128].unsqueeze(1).to_broadcast([128, 2, 128])
                    mk.op(mk.dve, lambda: nc.vector.scalar_tensor_tensor(out=YP0[:, 2 * qh:2 * qh + 2, 0, :], in0=Gv, scalar=-1.0,
                                                                         in1=EY[:, 2 * qh:2 * qh + 2, :], op0=ALU.mult, op1=ALU.mult),
                          r=[bGb, EYb[j]], w=[YPb[j][0]])
                    Av = bG[:, (2 + qh) * 128:(3 + qh) * 128].unsqueeze(1).to_broadcast([128, 2, 128])
                    mk.op(mk.dve, lambda: nc.vector.tensor_tensor(out=atT[j][:, 2 * qh:2 * qh + 2, :], in0=Av, in1=EA[:, 2 * qh:2 * qh + 2, :],
                                                                  op=ALU.mult), r=[bGb, EAb[j]], w=[atTb[j]])

            def ph_gram5(c):
                j = c.j
                YP0 = YPt[j][0]
                bk, bkb = self.ps()
                for hh in range(4):
                    mk.op(mk.pe, lambda: nc.tensor.transpose(out=bk[:, hh * 128:(hh + 1) * 128], in_=YP0[:, hh, 0, :].bitcast(F32), identity=self.ident_f[0][:]),
                          r=[YPb[j][0], self.ident_f[1]], w=[bkb])
                self.copy(mk.act, YTt[j][0][:, 0:4, :].rearrange("p h c -> p (h c)"), bk[:, :], r=[bkb], w=[YTb[j][0]])
                c.cur = 0

            def ph_level(c, lvl):
                j = c.j
                cur, nxt = c.cur, 1 - c.cur
                YPc, YTc, YPn, YTn = YPt[j][cur], YTt[j][cur], YPt[j][nxt], YTt[j][nxt]
                first, last = (lvl == 0), (lvl == 6)
                b1 = [self.ps(), self.ps()]
                for hh in range(4):
                    bk, bkb = b1[hh // 2]
                    mk.op(mk.pe, lambda: nc.tensor.matmul(bk[:, (hh % 2) * 256:(hh % 2) * 256 + 256], lhsT=YTc[:, hh, :],
                                                          rhs=YPc[:, hh, :, :].rearrange("p a c -> p (a c)"),
                                                          start=True, stop=True), r=[YPb[j][cur], YTb[j][cur]], w=[bkb])
                if not last:
                    b2, b2b = self.ps()
                    for hh in range(4):
                        mk.op(mk.pe, lambda: nc.tensor.matmul(b2[:, hh * 128:(hh + 1) * 128], lhsT=YPc[:, hh, 0, :], rhs=YTc[:, hh, :],
                                                              start=True, stop=True), r=[YPb[j][cur], YTb[j][cur]], w=[b2b])
                    self.copy(mk.act, YTn[:, 0:4, :].rearrange("p h c -> p (h c)"), b2[:, :], r=[b2b], w=[YTb[j][nxt]])
                for half in range(2):
                    bk, bkb = b1[half]
                    v = bk[:, :].rearrange("p (h a c) -> p h a c", a=2, c=128)
                    hs = slice(2 * half, 2 * half + 2)
                    if not last:
                        self.copy(mk.dve, YPn[:, hs, 0, :], v[:, :, 0, :], r=[bkb], w=[YPb[j][nxt]])
                    if not last:
                        mk.op(mk.dve, lambda: nc.vector.tensor_tensor(out=YPn[:, hs, 1, :], in0=v[:, :, 1, :], in1=YPc[:, hs, 1, :].bitcast(F32), op=ALU.add),
                              r=[bkb, YPb[j][cur]], w=[YPb[j][nxt]])
                    else:
                        mk.op(mk.dve, lambda: nc.vector.tensor_tensor(out=TT[j][:, hs, :], in0=v[:, :, 1, :], in1=YPc[:, hs, 1, :].bitcast(F32), op=ALU.add),
                              r=[bkb, YPb[j][cur]], w=[TTb[j]])
                c.cur = nxt

            def ph_uw(c):
                j = c.j
                bU, bUb = self.ps()
                bW, bWb = self.ps()
                for hh in range(4):
                    mk.op(mk.pe, lambda: nc.tensor.matmul(bU[:, hh * 128:(hh + 1) * 128], lhsT=TT[j][:, hh, :], rhs=vbt[j][:, hh, :],
                                                          start=True, stop=True), r=[TTb[j], vbb[j]], w=[bUb])
                    mk.op(mk.pe, lambda: nc.tensor.matmul(bW[:, hh * 128:(hh + 1) * 128], lhsT=kbg[j][:, hh, :], rhs=TT[j][:, hh, :],
                                                          start=True, stop=True), r=[TTb[j], kbgb[j]], w=[bWb])
                self.copy(mk.act, f4(usb[j]), bU[:, :], r=[bUb], w=[usbb[j]])
                self.copy(mk.dve, f4(wT[j]), bW[:, :], r=[bWb], w=[wTb[j]])

            def ph_state1(c):
                j, hg, H0 = c.j, c.hg, c.H0
                bWS, bWSb = self.ps()
                for hh in range(4):
                    mk.op(mk.pe, lambda: nc.tensor.matmul(bWS[:, hh * 128:(hh + 1) * 128], lhsT=wT[j][:, hh, :], rhs=Sb[:, H0 + hh, :],
                                                          start=True, stop=True), r=[wTb[j], Sbb[hg]], w=[bWSb])
                mk.op(mk.dve, lambda: nc.vector.tensor_tensor(out=f4(vnw[j]), in0=f4(usb[j]), in1=bWS[:, :], op=ALU.subtract),
                      r=[usbb[j], bWSb], w=[vnwb[j]])

            def ph_s2a(c):
                j, hg, H0 = c.j, c.hg, c.H0
                c.bAV, c.bAVb = self.ps()
                c.bDS, c.bDSb = self.ps()
                for hh in range(4):
                    mk.op(mk.pe, lambda: nc.tensor.matmul(c.bDS[:, hh * 128:(hh + 1) * 128], lhsT=kdc[j][:, hh, :], rhs=vnw[j][:, hh, :],
                                                          start=True, stop=True), r=[kdcb[j], vnwb[j]], w=[c.bDSb])
                for hh in range(4):
                    mk.op(mk.pe, lambda: nc.tensor.matmul(c.bAV[:, hh * 128:(hh + 1) * 128], lhsT=atT[j][:, hh, :], rhs=vnw[j][:, hh, :],
                                                          start=True, stop=True), r=[atTb[j], vnwb[j]], w=[c.bAVb])

            def ph_s2b(c):
                j, hg, H0 = c.j, c.hg, c.H0
                G_ = c.GT_
                Sv = Sf[:, H0:H0 + 4, :]
                mk.op(mk.dve, lambda: nc.vector.tensor_tensor(out=Sv, in0=Sv, in1=G_[:, 4, H0:H0 + 4].unsqueeze(2).to_broadcast([128, 4, 128]),
                                                              op=ALU.mult), r=[c.cbb, Sfb[hg]], w=[Sfb[hg]])
                mk.op(mk.dve, lambda: nc.vector.tensor_tensor(out=Sv, in0=Sv, in1=c.bDS[:, :].rearrange("p (h c) -> p h c", c=128), op=ALU.add),
                      r=[c.bDSb, Sfb[hg]], w=[Sfb[hg]])
                O, Ob = osb[j], osbb[j]
                self.copy(mk.act, f4(O), c.bAV[:, :], r=[c.bAVb], w=[Ob])
                c.bQS, c.bQSb = self.ps()
                for hh in range(4):
                    mk.op(mk.pe, lambda: nc.tensor.matmul(c.bQS[:, hh * 128:(hh + 1) * 128], lhsT=qTt[j][:, hh // 2, :], rhs=Sb[:, H0 + hh, :],
                                                          start=True, stop=True), r=[qTb[j], Sbb[hg]], w=[c.bQSb])

            def ph_s2c(c):
                j, hg, H0 = c.j, c.hg, c.H0
                G_ = c.GT_
                O, Ob = osb[j], osbb[j]
                self.copy(mk.act, Sb[:, H0:H0 + 4, :], Sf[:, H0:H0 + 4, :], r=[Sfb[hg]], w=[Sbb[hg]])
                J, Jb = junk[j], junkb[j]
                mk.op(mk.dve, lambda: nc.vector.tensor_tensor(out=J[:], in0=c.bQS[:, :].rearrange("p (h c) -> p h c", c=128),
                                                              in1=G_[:, 1, H0:H0 + 4].unsqueeze(2).to_broadcast([128, 4, 128]), op=ALU.mult),
                      r=[c.bQSb, c.cbb], w=[Jb])
                mk.op(mk.dve, lambda: nc.vector.tensor_tensor(out=O[:], in0=O[:], in1=J[:], op=ALU.add), r=[Ob, Jb], w=[Ob])

            def ph_s2d(c):
                j = c.j
                O, Ob = osb[j], osbb[j]
                mk.op(mk.act, lambda: nc.scalar.activation(out=junk[j][:], in_=O[:], func=AF.Square, scale=128 ** -0.5), r=[Ob], w=[junkb[j]])

            def ph_s2e(c):
                j = c.j
                ST, STb = stt_[j], sttb[j]
                mk.op(mk.dve, lambda: nc.vector.tensor_reduce(out=ST[:, 0:4], in_=junk[j][:], axis=AX.X, op=ALU.add), r=[junkb[j]], w=[STb])

            def ph_s2f(c):
                j = c.j
                ST, STb = stt_[j], sttb[j]
                self.rsqrt(ST[:, 4:8], ST[:, 0:4], EPS, [STb], [STb])

            def ph_s2g(c):
                j, hg, rows = c.j, c.hg, c.rows
                O, Ob = osb[j], osbb[j]
                ST, STb = stt_[j], sttb[j]
                mk.op(mk.dve, lambda: nc.vector.tensor_tensor(out=O[:], in0=O[:], in1=ST[:, 4:8].unsqueeze(2).to_broadcast([128, 4, 128]), op=ALU.mult),
                      r=[Ob, STb], w=[Ob])
                mk.op(mk.dve, lambda: nc.vector.tensor_tensor(out=O[:], in0=O[:], in1=NG[:].unsqueeze(1).to_broadcast([128, 4, 128]), op=ALU.mult),
                      r=[Ob, cb], w=[Ob])
                mk.op(mk.pool, lambda: nc.gpsimd.tensor_tensor(out=yo[j][:], in0=f4(O), in1=zst[j][:], op=ALU.mult),
                      r=[Ob, zsb[j]], w=[yob[j]])
                mk.dma(mk.pool, gdn_o[rows, hg * 512:(hg + 1) * 512], yo[j][:], r=[yob[j]])

            phases = ([ph_load, ph_gram1, ph_gram2, ph_gram3, ph_gram4, ph_gram5] + [(lambda c, l=l: ph_level(c, l)) for l in range(7)]
                      + [ph_uw, ph_state1, ph_s2a, ph_s2b, ph_s2c, ph_s2d, ph_s2e, ph_s2f, ph_s2g])
            def chunk_prep(n):
                cj = n % 2
                rows = slice(n * 128, (n + 1) * 128)
                GT_, XX_, XT_, LA_, cbb = GTt[cj], XXt[cj], XT[cj], LA[cj], chb[cj]
                mk.dma(mk.sp, GT_[:], gt[rows], w=[cbb])
                mk.dma(mk.sp, XX_[:], X[rows], w=[cbb])
                bk, bkb = self.ps()
                for i in range(3):
                    mk.op(mk.pe, lambda: nc.tensor.transpose(out=bk[:64, i * 128:(i + 1) * 128], in_=XX_[:, i, :], identity=self.ident_f[0][:]),
                          r=[cbb, self.ident_f[1]], w=[bkb])
                self.copy(mk.dve, XT_[:], bk[:64, 0:384].rearrange("p (i c) -> p i c", c=128), r=[bkb], w=[cbb])
                mk.op(mk.dve, lambda: nc.vector.tensor_tensor(out=LA_[:], in0=XT_[:, 0, :].unsqueeze(1).to_broadcast([64, 32, 128]),
                                                              in1=OH[:].unsqueeze(2).to_broadcast([64, 32, 128]), op=ALU.mult),
                      r=[cbb, cb], w=[cbb])
                return rows, GT_, XT_, LA_, cbb

            groups = [(n, hg) for n in range(NCH) for hg in range(8)]
            pairs = [groups[i:i + 2] for i in range(0, len(groups), 2)]
            while len(phases) % 2:
                phases.append(lambda c: None)
            NP = len(phases)
            off = NP // 2
            prep = {}
            live = {}
            for step in range((len(pairs) - 1) * off + NP):
                p_hi = min(len(pairs) - 1, step // off)
                p_lo = max(0, (step - NP) // off + 1)
                for p in range(p_lo, p_hi + 1):
                    ph = step - p * off
                    if not (0 <= ph < NP):
                        continue
                    if ph == 0:
                        ctxs = []
                        for k, (n, hg) in enumerate(pairs[p]):
                            if hg == 0:
                                prep[n] = chunk_prep(n)
                                prep.pop(n - 2, None)
                            c = Ctx()
                            c.j, c.hg, c.H0 = (p % 2) * 2 + k, hg, hg * 4
                            c.rows, c.GT_, c.XT_, c.LA_, c.cbb = prep[n]
                            ctxs.append(c)
                        live[p] = ctxs
                    for c in live[p]:
                        phases[ph](c)
                    if ph == NP - 1:
                        del live[p]
            mk.barrier()


for _n, _f in list(ProgGDN.__dict__.items()):
    if callable(_f):
        setattr(Prog, _n, _f)


INPUT_SHAPES = dict(
    rel_bias=(32, 8), ln_mix=(4, D), ln_mem=(4, D), ln_ffn=(4, D), hyb_w_in=(2, D, HYB_IN), hyb_w_out=(2, 2048, D),
    nsa_q_gain=(2, 128), nsa_k_gain=(2, 3, 128), nsa_cmp_pos=(2, 2, 32, 128), nsa_cmp_w1=(2, 2, 4096, 256),
    nsa_cmp_w2=(2, 2, 256, 128), ret_gn_gain=(2, 1024), gdn_w_in=(2, D, GDN_IN), gdn_conv_w=(2, 4, 8192),
    gdn_a_log=(2, 32), gdn_dt_bias=(2, 32), gdn_norm_gain=(2, 128), gdn_w_out=(2, 4096, D), xa_wq=(4, D, 512),
    xa_wkv=(4, D, 1024), xa_q_gain=(4, 128), xa_k_gain=(4, 128), xa_mem_gain=(4, D), xa_wo=(4, 512, D),
    ffn_w_in=(4, D, 2 * FFN_H), ffn_w_out=(4, FFN_H, D))
CAST = ("hyb_w_in", "hyb_w_out", "nsa_cmp_w1", "nsa_cmp_w2", "gdn_w_in", "gdn_w_out", "xa_wq", "xa_wkv", "xa_wo",
        "ffn_w_in", "ffn_w_out")


def build_program(S=4096, M=256, depth=4, dbg_layers=False):
    P = Prog(S=S, M=M, dbg_out=())
    nc, mk = P.nc, P.mk
    x = P.inp("x", [S, D])
    mem = P.inp("mem", [M, D])
    h = nc.dram_tensor("out", [S, D], F32, kind="ExternalOutput").ap()
    A = {}
    for k, shp in INPUT_SHAPES.items():
        if k != "rel_bias":
            A[k] = P.inp(k, shp)
    toep_i, cmpx_i, OFF = nsa_index_tables(S)
    rel_toep = P.inp("rel_toep", [128, 10, 8, 128])
    rel_cmpx = P.inp("rel_cmpx", [cmpx_i.shape[0], 8, 128])
    P.idx = (toep_i, cmpx_i)
    W = {}
    P.wreg = {}
    first, later = [], []
    def layer_of(k, i):
        if k.startswith("hyb") or k.startswith("nsa"):
            return 2 * i
        if k.startswith("gdn"):
            return 2 * i + 1
        return i
    for k in CAST:
        src = A[k]
        dst = P.scr(k + "_bf", list(src.shape), BF16)
        W[k] = dst
        nl = src.shape[0]
        per = int(np.prod(src.shape[1:]))
        bufs = [Buf(f"{k}{i}") for i in range(nl)]
        P.wreg[dst.tensor.name] = (per, bufs)
        for i in range(nl):
            s2, d2 = src[i], dst[i]
            if len(s2.shape) == 3:
                s2, d2 = s2.rearrange("a b c -> (a b) c"), d2.rearrange("a b c -> (a b) c")
            R_, C_ = s2.shape
            rp = max(1, min(R_, (2 << 20) // C_))
            for r0 in range(0, R_, rp):
                r1 = min(R_, r0 + rp)
                item = (layer_of(k, i), d2[r0:r1, :], s2[r0:r1, :], bufs[i])
                (first if item[0] == 0 else later).append(item)
    for _, d_, s_, b_ in first:
        mk.dma(mk.pool, d_, s_, w=[b_])
    later.sort(key=lambda t: t[0])
    P.cast_pending = list(later)
    mk.on_barrier = lambda: P.emit_casts(7)
    for r0 in range(0, S, 512):
        mk.dma(mk.sp, h[r0:r0 + 512, :], x[r0:r0 + 512, :])
    mk.barrier()
    mix_in = P.scr("mix_in", [S, 2048])
    gdn_o = P.scr("gdn_o", [S, 4096])
    dbg = []
    for layer in range(depth):
        P.emit_casts_upto(layer)
        if layer % 2 == 0:
            e = layer // 2
            P.hyb_in(h, A["ln_mix"][layer], W["hyb_w_in"][e], A["nsa_q_gain"][e], A["nsa_k_gain"][e])
            P.nsa_cmp(A["nsa_cmp_pos"][e], W["nsa_cmp_w1"][e], W["nsa_cmp_w2"][e], A["nsa_k_gain"][e][0])
            P.nsa_attn(rel_toep, rel_cmpx, OFF, mix_in)
            P.retention(A["ret_gn_gain"][e], mix_in)

            def jobs_o(st, TB, e=e):
                return [dict(W=W["hyb_w_out"][e], c0=0, c1=D, layout="TM", epi=P.epi_residual(st, h, TB))]
            P.linear("hybo", mix_in, S, 2048, None, jobs_o, src_bf16=True)
        else:
            o = layer // 2
            P.gdn_in(h, A["ln_mix"][layer], W["gdn_w_in"][o], A["gdn_conv_w"][o])
            P.gdn_gates(A["gdn_a_log"][o], A["gdn_dt_bias"][o])
            P.gdn_chunks(A["gdn_norm_gain"][o], gdn_o)

            def jobs_g(st, TB, o=o):
                return [dict(W=W["gdn_w_out"][o], c0=0, c1=D, layout="TM", epi=P.epi_residual(st, h, TB, width=256))]
            P.linear("gdno", gdn_o, S, 4096, None, jobs_g, src_bf16=True)
        P.xa(h, mem, A["ln_mem"][layer], A["xa_mem_gain"][layer], W["xa_wq"][layer], W["xa_wkv"][layer],
             A["xa_q_gain"][layer], A["xa_k_gain"][layer], W["xa_wo"][layer])
        P.ffn(h, A["ln_ffn"][layer], W["ffn_w_in"][layer], W["ffn_w_out"][layer])
        if dbg_layers:
            d = nc.dram_tensor(f"dbg_h{layer}", [S, D], F32, kind="ExternalOutput").ap()
            for r0 in range(0, S, 512):
                mk.dma(mk.sp, d[r0:r0 + 512, :], h[r0:r0 + 512, :])
            mk.barrier()
    P.finish()
    return P


def host_inputs(P, inputs, b):
    toep_i, cmpx_i = P.idx
    rel_ext = np.concatenate([np.asarray(inputs["rel_bias"], np.float32), np.full((1, 8), -BIG, np.float32)], 0)
    im = dict(P.consts)
    im["x"] = np.ascontiguousarray(inputs["x"][b])
    im["mem"] = np.ascontiguousarray(inputs["mem"][b])
    im["rel_toep"] = np.ascontiguousarray(rel_ext[toep_i].transpose(0, 1, 3, 2))
    im["rel_cmpx"] = np.ascontiguousarray(rel_ext[cmpx_i].transpose(0, 2, 1))
    for k in INPUT_SHAPES:
        if k != "rel_bias":
            im[k] = np.ascontiguousarray(np.asarray(inputs[k], np.float32))
    return im


def kernel(**inputs):
    inputs = {k: np.asarray(v) for k, v in inputs.items()}
    B, S, _ = inputs["x"].shape
    P = build_program(S=S, M=inputs["mem"].shape[1])
    shared = host_inputs(P, inputs, 0)
    in_maps = []
    for b in range(B):
        im = dict(shared)
        im["x"] = np.ascontiguousarray(inputs["x"][b])
        im["mem"] = np.ascontiguousarray(inputs["mem"][b])
        in_maps.append(im)
    res = run_bass_kernel_spmd(P.nc, in_maps, core_ids=list(range(B)))
    return np.stack([np.asarray(r["out"], np.float32) for r in res.results], 0)
```
